# Optimizing a Trainium2 kernel written in Bass

```python
import math
import jax, jax.numpy as jnp
from jax import lax
import numpy as np

D_MODEL = 2048
BATCH = 2
SEQ = 4096
DEPTH = 2
DEC_BATCH = 16
DEC_SEQ = 16
PAST_LEN = 1024

CHUNK = 64
D_MIX = D_MODEL
D_SSM = D_MIX // 4
SSM_GROUP = 16
N_SSM_GROUPS = D_SSM // SSM_GROUP
SSM_STATE = 64
D_ATT = D_MIX // 2
ATT_HEAD_DIM = 64
N_ATT_HEADS = D_ATT // ATT_HEAD_DIM
LEFT_CHUNKS = 8
BAND_CHUNKS = LEFT_CHUNKS + 1
ATT_REACH = LEFT_CHUNKS * CHUNK
REL_CLIP = 128
ATT_SCALE = ATT_HEAD_DIM ** -0.5
NEG_INF = -1e30
D_RWKV = D_MIX - D_SSM - D_ATT
RWKV_HEAD_DIM = 64
N_RWKV_HEADS = D_RWKV // RWKV_HEAD_DIM
RWKV_LORA = 64
GN_EPS = 64e-5
LN_EPS = 1e-5
SPLIT_SIZES = (D_SSM, D_SSM, D_ATT, D_ATT, D_ATT, D_ATT, D_RWKV, D_RWKV, D_RWKV, D_RWKV, D_RWKV)
SPLIT_POINTS = tuple(int(s) for s in np.cumsum(SPLIT_SIZES)[:-1])
D_IN = sum(SPLIT_SIZES)
ALPHA = (2.0 * DEPTH) ** 0.25
BETA = (8.0 * DEPTH) ** -0.25

kernel_name = 'hybrid_streaming_encoder_step'


def layer_norm(x, g, b):
    xf = x.astype(jnp.float32)
    mu = jnp.mean(xf, axis=-1, keepdims=True)
    var = jnp.mean(jnp.square(xf - mu), axis=-1, keepdims=True)
    y = (xf - mu) * lax.rsqrt(var + LN_EPS) * g.astype(jnp.float32) + b.astype(jnp.float32)
    return y.astype(x.dtype)


def complex_affine_combine(e1, e2):
    a1r, a1i, b1r, b1i = e1
    a2r, a2i, b2r, b2i = e2
    return (a2r * a1r - a2i * a1i, a2r * a1i + a2i * a1r,
            a2r * b1r - a2i * b1i + b2r, a2r * b1i + a2i * b1r + b2i)


def s5_mixer(u, h0_re, h0_im, lam_re, lam_im, log_dt, b_re, b_im, c_re, c_im, d_skip, w_glu, b_glu):
    n_b, n_l, _ = u.shape
    f32 = lambda t: t.astype(jnp.float32)
    uf = f32(u)
    ug = uf.reshape(n_b, n_l, N_SSM_GROUPS, SSM_GROUP)
    lr, li = f32(lam_re), f32(lam_im)
    dt = jnp.exp(f32(log_dt))[:, None]
    e = jnp.exp(lr * dt)
    ab_re, ab_im = e * jnp.cos(li * dt), e * jnp.sin(li * dt)
    den = lr * lr + li * li
    nr, ni = ab_re - 1.0, ab_im
    q_re = (nr * lr + ni * li) / den
    q_im = (ni * lr - nr * li) / den
    br, bi = f32(b_re), f32(b_im)
    bb_re = q_re[..., None] * br - q_im[..., None] * bi
    bb_im = q_re[..., None] * bi + q_im[..., None] * br
    bu_re = jnp.einsum('gpc,blgc->blgp', bb_re, ug)
    bu_im = jnp.einsum('gpc,blgc->blgp', bb_im, ug)
    if h0_re is not None:
        h0r, h0i = f32(h0_re), f32(h0_im)
        bu_re = bu_re.at[:, 0].add(ab_re * h0r - ab_im * h0i)
        bu_im = bu_im.at[:, 0].add(ab_re * h0i + ab_im * h0r)
    a_re = jnp.broadcast_to(ab_re, bu_re.shape)
    a_im = jnp.broadcast_to(ab_im, bu_im.shape)
    _, _, h_re, h_im = lax.associative_scan(complex_affine_combine, (a_re, a_im, bu_re, bu_im), axis=1)
    y = jnp.einsum('gcp,blgp->blgc', f32(c_re), h_re) - jnp.einsum('gcp,blgp->blgc', f32(c_im), h_im)
    y = y.reshape(n_b, n_l, D_SSM) + f32(d_skip) * uf
    z = jax.nn.gelu(y)
    out = z * jax.nn.sigmoid(z @ f32(w_glu) + f32(b_glu))
    return out.astype(u.dtype), h_re[:, -1], h_im[:, -1]


def rel_bias_lookup(table, rel):
    return table[:, jnp.clip(rel, -REL_CLIP, REL_CLIP) + REL_CLIP].astype(jnp.float32)


def band_attention_prompt(q, k, v, rel_table):
    n_b, n_l, n_h, n_d = q.shape
    n_c = n_l // CHUNK
    qc = q.reshape(n_b, n_c, CHUNK, n_h, n_d)
    pad = ((0, 0), (ATT_REACH, 0), (0, 0), (0, 0))
    kp = jnp.pad(k, pad).reshape(n_b, n_c + LEFT_CHUNKS, CHUNK, n_h, n_d)
    vp = jnp.pad(v, pad).reshape(n_b, n_c + LEFT_CHUNKS, CHUNK, n_h, n_d)
    kb = jnp.concatenate([kp[:, i:i + n_c] for i in range(BAND_CHUNKS)], axis=2)
    vb = jnp.concatenate([vp[:, i:i + n_c] for i in range(BAND_CHUNKS)], axis=2)
    qi = jnp.arange(CHUNK)
    kj = jnp.arange(BAND_CHUNKS * CHUNK)
    bias = rel_bias_lookup(rel_table, ATT_REACH + qi[:, None] - kj[None, :])
    valid = (jnp.arange(n_c)[:, None] * CHUNK - ATT_REACH + kj[None, :]) >= 0
    s = jnp.einsum('bcqhd,bckhd->bchqk', qc, kb).astype(jnp.float32) * ATT_SCALE + bias
    s = jnp.where(valid[None, :, None, None, :], s, NEG_INF)
    p = jax.nn.softmax(s, axis=-1).astype(v.dtype)
    o = jnp.einsum('bchqk,bckhd->bcqhd', p, vb)
    return o.reshape(n_b, n_l, n_h * n_d)


def band_attention_step(q, k_new, v_new, k_cache, v_cache, rel_table):
    n_b, n_s, n_h, n_d = q.shape
    n_w = k_cache.shape[1]
    kk = jnp.concatenate([k_cache.astype(k_new.dtype), k_new], axis=1)
    vv = jnp.concatenate([v_cache.astype(v_new.dtype), v_new], axis=1)
    rel = (n_w + jnp.arange(n_s))[:, None] - jnp.arange(n_w + n_s)[None, :]
    bias = rel_bias_lookup(rel_table, rel)
    s = jnp.einsum('bqhd,bkhd->bhqk', q, kk).astype(jnp.float32) * ATT_SCALE + bias[None]
    p = jax.nn.softmax(s, axis=-1).astype(vv.dtype)
    o = jnp.einsum('bhqk,bkhd->bqhd', p, vv)
    return o.reshape(n_b, n_s, n_h * n_d)


def rwkv7_mixer(r_p, k_p, v_p, u_p, wkv0, shift0, mu, w0, w1, w2, a0, a1, a2, k_k, k_a, r_k, lnx_g, lnx_b):
    n_b, n_l, _ = r_p.shape
    cur = jnp.stack([r_p, k_p, v_p, u_p], axis=2)
    if shift0 is None:
        first = jnp.zeros_like(cur[:, :1])
    else:
        first = shift0.reshape(n_b, 1, 4, D_RWKV).astype(cur.dtype)
    delta = jnp.concatenate([first, cur[:, :-1]], axis=1) - cur
    r = r_p + delta[:, :, 0] * mu[0]
    k = k_p + delta[:, :, 1] * mu[1]
    v = v_p + delta[:, :, 2] * mu[2]
    xw = u_p + delta[:, :, 3] * mu[3]
    xa = u_p + delta[:, :, 3] * mu[4]
    w = -jax.nn.softplus(-(w0 + jnp.tanh(xw @ w1) @ w2)) - 0.5
    decay = jnp.exp(-jnp.exp(w.astype(jnp.float32)))
    a = jax.nn.sigmoid(a0 + (xa @ a1) @ a2)
    heads = lambda t: t.reshape(n_b, n_l, N_RWKV_HEADS, RWKV_HEAD_DIM).astype(jnp.float32)
    kk = heads(k * k_k)
    kk = kk * lax.rsqrt(jnp.maximum(jnp.sum(kk * kk, axis=-1, keepdims=True), 1e-24))
    k = k * (1.0 + (a - 1.0) * k_a)
    rh, kh, vh, wh, ah = heads(r), heads(k), heads(v), heads(decay), heads(a)
    bh = kk * ah
    if wkv0 is None:
        s0 = jnp.zeros((n_b, N_RWKV_HEADS, RWKV_HEAD_DIM, RWKV_HEAD_DIM), jnp.float32)
    else:
        s0 = wkv0.astype(jnp.float32)

    def step(S, inp):
        r_t, w_t, k_t, v_t, kk_t, b_t = inp
        sa = jnp.einsum('bhij,bhj->bhi', S, -kk_t)
        S = S * w_t[:, :, None, :] + sa[..., None] * b_t[:, :, None, :] + v_t[..., None] * k_t[:, :, None, :]
        return S, jnp.einsum('bhij,bhj->bhi', S, r_t)

    tm = lambda t: jnp.swapaxes(t, 0, 1)
    s_last, ys = lax.scan(step, s0, (tm(rh), tm(wh), tm(kh), tm(vh), tm(kk), tm(bh)))
    y = tm(ys)
    mean = jnp.mean(y, axis=-1, keepdims=True)
    var = jnp.mean(jnp.square(y - mean), axis=-1, keepdims=True)
    y = (y - mean) * lax.rsqrt(var + GN_EPS) * lnx_g.astype(jnp.float32).reshape(N_RWKV_HEADS, RWKV_HEAD_DIM) \
        + lnx_b.astype(jnp.float32).reshape(N_RWKV_HEADS, RWKV_HEAD_DIM)
    y = y + jnp.sum(rh * kh * r_k.astype(jnp.float32), axis=-1, keepdims=True) * vh
    return y.reshape(n_b, n_l, D_RWKV).astype(r_p.dtype), s_last, cur[:, -1].reshape(n_b, 4 * D_RWKV)


def hybrid_layer(x, st, lw):
    n_b, n_l, _ = x.shape
    proj = x @ lw['w_in']
    u_s, g_s, q, k, v, g_a, r_c, k_c, v_c, u_c, g_c = jnp.split(proj, SPLIT_POINTS, axis=-1)
    if st is None:
        k_cache = v_cache = h0_re = h0_im = wkv0 = shift0 = None
    else:
        k_cache, v_cache, h0_re, h0_im, wkv0, shift0 = st
    y_s, h_re, h_im = s5_mixer(u_s, h0_re, h0_im, lw['ssm_lam_re'], lw['ssm_lam_im'], lw['ssm_log_dt'],
                               lw['ssm_b_re'], lw['ssm_b_im'], lw['ssm_c_re'], lw['ssm_c_im'],
                               lw['ssm_d'], lw['ssm_w_glu'], lw['ssm_b_glu'])
    hd = lambda t: t.reshape(n_b, n_l, N_ATT_HEADS, ATT_HEAD_DIM)
    qh, kh, vh = hd(q), hd(k), hd(v)
    if st is None:
        y_a = band_attention_prompt(qh, kh, vh, lw['att_rel_bias'])
        n_keep = min(ATT_REACH, n_l)
        k_rows, v_rows = kh[:, n_l - n_keep:], vh[:, n_l - n_keep:]
    else:
        y_a = band_attention_step(qh, kh, vh, k_cache, v_cache, lw['att_rel_bias'])
        k_rows, v_rows = kh, vh
    y_c, wkv, shift = rwkv7_mixer(r_c, k_c, v_c, u_c, wkv0, shift0, lw['rwkv_mu'], lw['rwkv_w0'],
                                  lw['rwkv_w1'], lw['rwkv_w2'], lw['rwkv_a0'], lw['rwkv_a1'], lw['rwkv_a2'],
                                  lw['rwkv_k_k'], lw['rwkv_k_a'], lw['rwkv_r_k'], lw['rwkv_lnx_g'], lw['rwkv_lnx_b'])
    mixed = jnp.concatenate([y_s * jax.nn.silu(g_s), y_a * jax.nn.silu(g_a), y_c * jax.nn.silu(g_c)], axis=-1)
    out = mixed @ lw['w_out']
    y = layer_norm(ALPHA * x + out, lw['ln_g'], lw['ln_b'])
    return y, (k_rows, v_rows, h_re, h_im, wkv, shift)


def setup_inputs(seed: int = 0) -> dict:
    key = jax.random.key(seed)
    ks = iter(jax.random.split(key, 48))
    nrm = lambda shape, scale: scale * jax.random.normal(next(ks), shape, jnp.float32)
    att_rows = min(ATT_REACH, PAST_LEN)
    G, P, L_ = N_SSM_GROUPS, SSM_STATE, DEPTH
    inp = {}
    inp['x_prompt'] = nrm((BATCH, SEQ, D_MODEL), 1.0)
    inp['x_sample'] = nrm((DEC_BATCH, DEC_SEQ, D_MODEL), 1.0)
    inp['cache_att_k'] = nrm((L_, DEC_BATCH, att_rows, N_ATT_HEADS, ATT_HEAD_DIM), 1.0)
    inp['cache_att_v'] = nrm((L_, DEC_BATCH, att_rows, N_ATT_HEADS, ATT_HEAD_DIM), 1.0)
    inp['state_ssm_re'] = nrm((L_, DEC_BATCH, G, P), 0.5)
    inp['state_ssm_im'] = nrm((L_, DEC_BATCH, G, P), 0.5)
    inp['state_rwkv'] = nrm((L_, DEC_BATCH, N_RWKV_HEADS, RWKV_HEAD_DIM, RWKV_HEAD_DIM), 0.5)
    inp['state_rwkv_shift'] = nrm((L_, DEC_BATCH, 4 * D_RWKV), 1.0)
    inp['w_in'] = nrm((L_, D_MODEL, D_IN), D_MODEL ** -0.5)
    inp['ssm_lam_re'] = -0.5 * jnp.exp(nrm((L_, G, P), 0.02))
    inp['ssm_lam_im'] = jnp.pi * jnp.arange(P, dtype=jnp.float32) + nrm((L_, G, P), 0.01)
    inp['ssm_log_dt'] = jax.random.uniform(next(ks), (L_, G), jnp.float32, math.log(1e-3), math.log(1e-1))
    inp['ssm_b_re'] = nrm((L_, G, P, SSM_GROUP), (2.0 * SSM_GROUP) ** -0.5)
    inp['ssm_b_im'] = nrm((L_, G, P, SSM_GROUP), (2.0 * SSM_GROUP) ** -0.5)
    inp['ssm_c_re'] = nrm((L_, G, SSM_GROUP, P), (2.0 * P) ** -0.5)
    inp['ssm_c_im'] = nrm((L_, G, SSM_GROUP, P), (2.0 * P) ** -0.5)
    inp['ssm_d'] = nrm((L_, D_SSM), 1.0)
    inp['ssm_w_glu'] = nrm((L_, D_SSM, D_SSM), D_SSM ** -0.5)
    inp['ssm_b_glu'] = nrm((L_, D_SSM), 0.02)
    inp['att_rel_bias'] = nrm((L_, N_ATT_HEADS, 2 * REL_CLIP + 1), 0.5)
    inp['rwkv_mu'] = jax.random.uniform(next(ks), (L_, 5, D_RWKV), jnp.float32)
    inp['rwkv_w0'] = jnp.linspace(-6.5, -1.5, D_RWKV, dtype=jnp.float32) + nrm((L_, D_RWKV), 0.1)
    inp['rwkv_w1'] = nrm((L_, D_RWKV, RWKV_LORA), D_RWKV ** -0.5)
    inp['rwkv_w2'] = nrm((L_, RWKV_LORA, D_RWKV), 0.1 * RWKV_LORA ** -0.5)
    inp['rwkv_a0'] = nrm((L_, D_RWKV), 0.1)
    inp['rwkv_a1'] = nrm((L_, D_RWKV, RWKV_LORA), D_RWKV ** -0.5)
    inp['rwkv_a2'] = nrm((L_, RWKV_LORA, D_RWKV), 0.1 * RWKV_LORA ** -0.5)
    inp['rwkv_k_k'] = 0.85 + nrm((L_, D_RWKV), 0.05)
    inp['rwkv_k_a'] = 1.0 + nrm((L_, D_RWKV), 0.05)
    inp['rwkv_r_k'] = nrm((L_, N_RWKV_HEADS, RWKV_HEAD_DIM), 0.1)
    inp['rwkv_lnx_g'] = 1.0 + nrm((L_, D_RWKV), 0.02)
    inp['rwkv_lnx_b'] = nrm((L_, D_RWKV), 0.02)
    inp['w_out'] = nrm((L_, D_MIX, D_MODEL), BETA * D_MIX ** -0.5)
    inp['ln_g'] = 1.0 + nrm((L_, D_MODEL), 0.02)
    inp['ln_b'] = nrm((L_, D_MODEL), 0.02)
    return inp


def reference(x_prompt, x_sample, cache_att_k, cache_att_v, state_ssm_re, state_ssm_im, state_rwkv,
              state_rwkv_shift, w_in, ssm_lam_re, ssm_lam_im, ssm_log_dt, ssm_b_re, ssm_b_im, ssm_c_re,
              ssm_c_im, ssm_d, ssm_w_glu, ssm_b_glu, att_rel_bias, rwkv_mu, rwkv_w0, rwkv_w1, rwkv_w2,
              rwkv_a0, rwkv_a1, rwkv_a2, rwkv_k_k, rwkv_k_a, rwkv_r_k, rwkv_lnx_g, rwkv_lnx_b, w_out,
              ln_g, ln_b):
    y_p = x_prompt
    y_s = x_sample
    p_st = []
    s_st = []
    for l in range(DEPTH):
        lw = {'w_in': w_in[l], 'ssm_lam_re': ssm_lam_re[l], 'ssm_lam_im': ssm_lam_im[l],
              'ssm_log_dt': ssm_log_dt[l], 'ssm_b_re': ssm_b_re[l], 'ssm_b_im': ssm_b_im[l],
              'ssm_c_re': ssm_c_re[l], 'ssm_c_im': ssm_c_im[l], 'ssm_d': ssm_d[l],
              'ssm_w_glu': ssm_w_glu[l], 'ssm_b_glu': ssm_b_glu[l], 'att_rel_bias': att_rel_bias[l],
              'rwkv_mu': rwkv_mu[l], 'rwkv_w0': rwkv_w0[l], 'rwkv_w1': rwkv_w1[l], 'rwkv_w2': rwkv_w2[l],
              'rwkv_a0': rwkv_a0[l], 'rwkv_a1': rwkv_a1[l], 'rwkv_a2': rwkv_a2[l],
              'rwkv_k_k': rwkv_k_k[l], 'rwkv_k_a': rwkv_k_a[l], 'rwkv_r_k': rwkv_r_k[l],
              'rwkv_lnx_g': rwkv_lnx_g[l], 'rwkv_lnx_b': rwkv_lnx_b[l], 'w_out': w_out[l],
              'ln_g': ln_g[l], 'ln_b': ln_b[l]}
        y_p, st_p = hybrid_layer(y_p, None, lw)
        y_s, st_s = hybrid_layer(y_s, (cache_att_k[l], cache_att_v[l], state_ssm_re[l], state_ssm_im[l],
                                       state_rwkv[l], state_rwkv_shift[l]), lw)
        p_st.append(st_p)
        s_st.append(st_s)

    def stacked(states, i):
        return jnp.stack([st[i] for st in states], axis=0)

    return (y_p, y_s,
            stacked(p_st, 0), stacked(p_st, 1), stacked(p_st, 2), stacked(p_st, 3), stacked(p_st, 4), stacked(p_st, 5),
            stacked(s_st, 0), stacked(s_st, 1), stacked(s_st, 2), stacked(s_st, 3), stacked(s_st, 4), stacked(s_st, 5))
```

```python
import numpy as np
from concourse.bass_utils import run_bass_kernel_spmd
import concourse.bass as bass
import concourse.mybir as mybir

F32 = mybir.dt.float32
BF16 = mybir.dt.bfloat16
ALU = mybir.AluOpType
AF = mybir.ActivationFunctionType
AX = mybir.AxisListType

ENGS = ("pe", "act", "dve", "pool", "sp")


def _region(ap):
    t = ap.tensor
    pat = ap.ap
    off = int(ap.offset)
    if isinstance(t, bass.DRamTensorHandle):
        ext = 1
        for st, cn in pat:
            ext += (cn - 1) * abs(st)
        return (t.name, 0, 1, off, off + ext)
    row = pat[0][0]
    if row == 0:
        row = 1 << 40
    p0 = off // row
    f0 = off % row
    ext = 1
    for st, cn in pat[1:]:
        ext += (cn - 1) * abs(st)
    return (t.name, p0, p0 + pat[0][1], f0, f0 + ext)


class Op:
    __slots__ = ("eng", "fn", "dma", "deps", "pos", "signal", "sigval", "vc", "dk",
                 "waits", "sem", "semval", "idx", "raw")


class Prog:
    def __init__(self, nc, n_dma_sems=16, same_engine_sync=True):
        self.nc = nc
        self.ops = []
        self.acc = {}
        self.n_dma_sems = n_dma_sems
        self.same_engine_sync = same_engine_sync

    def op(self, eng, fn, reads=(), writes=(), dma=False):
        o = Op()
        o.eng = eng
        o.fn = fn
        o.dma = dma
        o.idx = len(self.ops)
        o.signal = False
        o.raw = set()
        deps = set()
        for ap in reads:
            self._access(o, ap, False, deps)
        for ap in writes:
            self._access(o, ap, True, deps)
        deps.discard(o.idx)
        o.deps = deps
        self.ops.append(o)
        return o

    def _access(self, o, ap, is_w, deps):
        name, p0, p1, f0, f1 = _region(ap)
        lst = self.acc.setdefault(name, [])
        keep = []
        psum = name.startswith("pp_")
        if psum:
            esz = mybir.dt.size(ap.dtype)
            b0, b1 = (f0 * esz) // 2048, ((f1 * esz) - 1) // 2048
            f0, f1 = (b0 * 2048) // esz, ((b1 + 1) * 2048) // esz
            p0, p1 = 0, 128
        for e in lst:
            ov = not (e[3] <= p0 or p1 <= e[2] or e[5] <= f0 or f1 <= e[4])
            if ov and (is_w or e[1]):
                deps.add(e[0])
                if (not is_w) and e[1]:
                    o.raw.add(e[0])
            if psum and (not is_w) and (not e[1]) and e[6] != o.eng and e[0] != o.idx:
                if not (e[8] < b0 or b1 < e[7]):
                    deps.add(e[0])
            if e[0] == o.idx:
                keep.append(e)
                continue
            contained = (p0 <= e[2] and e[3] <= p1 and f0 <= e[4] and e[5] <= f1)
            if contained and is_w:
                continue
            if contained and (not is_w) and (not e[1]) and e[6] == o.eng and not o.dma \
                    and not self.ops[e[0]].dma:
                continue
            keep.append(e)
        keep.append([o.idx, is_w, p0, p1, f0, f1, o.eng] + ([b0, b1] if psum else [0, 0]))
        self.acc[name] = keep

    def emit(self):
        nc = self.nc
        ops = self.ops
        known_vc = {e: {x: 0 for x in ENGS} for e in ENGS}
        known_dk = {e: {} for e in ENGS}
        count = {e: 0 for e in ENGS}
        dma_n = {e: 0 for e in ENGS}
        dma_last = {}
        by_pos = {e: [] for e in ENGS}
        for o in ops:
            E = o.eng
            kv = known_vc[E]
            kd = known_dk[E]
            waits = {}
            for di in sorted(o.deps):
                d = ops[di]
                if d.dma:
                    key = d.sem
                    if kd.get(key, 0) >= d.semval:
                        continue
                    waits[("d",) + key] = max(waits.get(("d",) + key, 0), d.semval)
                    d.signal = True
                    kd[key] = d.semval
                else:
                    Ed = d.eng
                    if kv[Ed] >= d.pos:
                        continue
                    if Ed == E and (E == "pe" or not self.same_engine_sync):
                        continue
                    waits[("c", Ed)] = max(waits.get(("c", Ed), 0), d.pos)
                    kv[Ed] = d.pos
                for x in ENGS:
                    if d.vc[x] > kv[x]:
                        kv[x] = d.vc[x]
                for k2, v2 in d.dk.items():
                    if kd.get(k2, 0) < v2:
                        kd[k2] = v2
            if o.dma:
                n = dma_n[E]
                dma_n[E] += 1
                key = (E, n % self.n_dma_sems)
                prev = dma_last.get(key)
                if prev is not None and kd.get(key, 0) < prev.semval:
                    waits[("d",) + key] = max(waits.get(("d",) + key, 0), prev.semval)
                    kd[key] = prev.semval
                    prev.signal = True
                o.sem = key
                o.semval = 16 * (n // self.n_dma_sems + 1)
                dma_last[key] = o
                o.pos = 0
                o.vc = dict(kv)
                o.dk = dict(kd)
            else:
                count[E] += 1
                o.pos = count[E]
                o.vc = dict(kv)
                o.vc[E] = o.pos
                o.dk = dict(kd)
                by_pos[E].append(o)
            o.waits = waits
        for o in ops:
            for k, v in o.waits.items():
                if k[0] == "c":
                    by_pos[k[1]][v - 1].signal = True
        for E in ENGS:
            s = 0
            for o in by_pos[E]:
                if o.signal:
                    s += 1
                o.sigval = s
        self.stats = {e: count[e] for e in ENGS}
        self.stats["dma"] = dict(dma_n)
        self.stats["waits"] = sum(len(o.waits) for o in ops)

        import contextlib
        with contextlib.ExitStack() as st:
            csem = {e: st.enter_context(nc.semaphore("c_" + e)) for e in ENGS}
            dsem = {}
            for e in ENGS:
                if dma_n[e]:
                    for i in range(min(self.n_dma_sems, dma_n[e])):
                        dsem[(e, i)] = st.enter_context(nc.semaphore("d_%s_%d" % (e, i)))
            block = st.enter_context(nc.Block())

            def run(E, eng):
                for o in ops:
                    if o.eng != E:
                        continue
                    for k, v in o.waits.items():
                        if k[0] == "c":
                            eng.wait_ge(csem[k[1]], by_pos[k[1]][v - 1].sigval)
                        else:
                            eng.wait_ge(dsem[(k[1], k[2])], v)
                    if o.fn is None:
                        continue
                    ins = o.fn(eng)
                    if o.dma:
                        ins.then_inc(dsem[o.sem], 16)
                    elif o.signal:
                        ins.then_inc(csem[E], 1)

            @block.tensor
            def _(eng):
                run("pe", eng)

            @block.scalar
            def _(eng):
                run("act", eng)

            @block.vector
            def _(eng):
                run("dve", eng)

            @block.gpsimd
            def _(eng):
                run("pool", eng)

            @block.sync
            def _(eng):
                run("sp", eng)

    def dma(self, q, out, in_, **kw):
        return self.op(q, lambda e: e.dma_start(out=out, in_=in_, **kw), [in_], [out], dma=True)

    def mm(self, out, lhsT, rhs, start=True, stop=True, **kw):
        return self.op("pe", lambda e: e.matmul(out, lhsT=lhsT, rhs=rhs, start=start, stop=stop, **kw),
                       [lhsT, rhs], [out])

    def tr(self, out, in_, ident):
        return self.op("pe", lambda e: e.transpose(out, in_, ident), [in_, ident], [out])

    def act(self, out, in_, func, bias=None, scale=1.0, accum_out=None, eng="act"):
        rd = [in_]
        wr = [out]
        kw = {}
        if bias is not None:
            kw["bias"] = bias
            if not isinstance(bias, (int, float)):
                rd.append(bias)
        if not isinstance(scale, (int, float)):
            rd.append(scale)
        if accum_out is not None:
            kw["accum_out"] = accum_out
            wr.append(accum_out)
        return self.op(eng, lambda e: e.activation(out=out, in_=in_, func=func, scale=scale, **kw), rd, wr)

    def tt(self, eng, out, in0, in1, op):
        return self.op(eng, lambda e: e.tensor_tensor(out=out, in0=in0, in1=in1, op=op), [in0, in1], [out])

    def ts(self, eng, out, in0, s1, op0, s2=None, op1=None, accum_out=None):
        rd = [in0]
        wr = [out]
        if not isinstance(s1, (int, float)):
            rd.append(s1)
        if s2 is not None and not isinstance(s2, (int, float)):
            rd.append(s2)
        kw = {}
        if op1 is not None:
            kw["op1"] = op1
        if accum_out is not None:
            kw["accum_out"] = accum_out
            wr.append(accum_out)
        return self.op(eng, lambda e: e.tensor_scalar(out=out, in0=in0, scalar1=s1, scalar2=s2, op0=op0, **kw),
                       rd, wr)

    def stt(self, out, in0, scalar, in1, op0, op1, eng="dve"):
        rd = [in0, in1]
        if not isinstance(scalar, (int, float)):
            rd.append(scalar)
        return self.op(eng, lambda e: e.scalar_tensor_tensor(out=out, in0=in0, scalar=scalar, in1=in1,
                                                             op0=op0, op1=op1), rd, [out])

    def scan(self, out, data0, data1, initial, op0=None, op1=None):
        rd = [data0, data1]
        if not isinstance(initial, (int, float)):
            rd.append(initial)
        op0 = op0 or ALU.mult
        op1 = op1 or ALU.add
        return self.op("dve", lambda e: e.tensor_tensor_scan(out=out, data0=data0, data1=data1,
                                                             initial=initial, op0=op0, op1=op1), rd, [out])

    def copy(self, eng, out, in_):
        if eng == "act":
            return self.act(out, in_, AF.Copy)
        return self.op(eng, lambda e: e.tensor_copy(out=out, in_=in_), [in_], [out])

    def memset(self, eng, out, val):
        return self.op(eng, lambda e: e.memset(out, val), [], [out])

    def recip(self, out, in_):
        return self.op("dve", lambda e: e.reciprocal(out=out, in_=in_), [in_], [out])

    def reduce(self, out, in_, op, axis=None, eng="dve"):
        axis = axis or AX.X
        return self.op(eng, lambda e: e.tensor_reduce(out=out, in_=in_, axis=axis, op=op), [in_], [out])

    def fence(self, q, reads):
        return self.op(q, None, [], list(reads))

import math

D = 2048
DIN = 7680
NL = 2
SEQ = 4096
TT = 256
NS = 2
SQ = 16
NCORE = 8
ALPHA = (2.0 * NL) ** 0.25
KAPPA = math.exp(-0.5)
NEG = -30000.0
CW = 256
NCH_IN = DIN // CW
NCH_OUT = D // CW
NCH = NCH_IN + NCH_OUT
GN_EPS = 64e-5
LN_EPS = 1e-5
NPV = 56
DBG_NOSH = False

_CST = {}
_off = 0
for _n, _w in [("identf", 128), ("mus128", 128), ("mls128", 128), ("mui128", 64),
               ("mus32", 32), ("mls32", 32), ("mui32", 16), ("cmask64", TT), ("cmask16", NS * SQ),
               ("zm", 2), ("ob", 128), ("bneg", 256), ("maskB", 64), ("bmask", 8)]:
    _CST[_n] = (_off, _w)
    _off += _w
NCST = _off


def _vec4(v):
    return np.ascontiguousarray(np.asarray(v, np.float32).reshape(4, 128).T)


def _consts():
    c = np.zeros((128, NCST), np.float32)

    def put(name, arr):
        o, w = _CST[name]
        c[:arr.shape[0], o:o + w] = arr

    put("identf", np.eye(128, dtype=np.float32))
    for C, sfx in ((64, "128"), (16, "32")):
        Z = 2 * C
        p = np.arange(Z)[:, None]
        q = np.arange(Z)[None, :]
        same = (p // C) == (q // C)
        put("mus" + sfx, (same & ((p % C) < (q % C))).astype(np.float32))
        put("mls" + sfx, (same & ((p % C) > (q % C))).astype(np.float32))
        t = np.arange(C)[None, :]
        put("mui" + sfx, ((p % C) <= t).astype(np.float32))
    cm = np.ones((128, TT), np.float32)
    cm[:, ::64] = 0.0
    put("cmask64", cm)
    cm = np.ones((128, NS * SQ), np.float32)
    cm[:, ::16] = 0.0
    put("cmask16", cm)
    pp = np.arange(128)
    put("zm", np.stack([(pp < 64), (pp >= 64)], 1).astype(np.float32))
    put("ob", ((pp[:, None] // 64) == (pp[None, :] // 64)).astype(np.float32))
    bn = np.zeros((128, 256), np.float32)
    bn[:64, 192:] = NEG
    put("bneg", bn)
    mb = np.zeros((128, 64), np.float32)
    mb[64:, :] = NEG
    put("maskB", mb)
    gl = pp // 16
    bm = np.zeros((128, 4, 2), np.float32)
    for pm in range(4):
        for g2 in range(2):
            bm[:, pm, g2] = (gl == 2 * pm + g2)
    put("bmask", bm.reshape(128, 8))
    return c


def _shared_layouts(inp):
    f = lambda k: np.asarray(inp[k], np.float32)
    out = {}
    pv = np.zeros((NL, 128, NPV), np.float32)
    for l in range(NL):
        for i in range(5):
            pv[l, :, i * 4:(i + 1) * 4] = _vec4(f("rwkv_mu")[l, i])
        for j, k in enumerate(["rwkv_w0", "rwkv_a0", "rwkv_k_k", "rwkv_k_a", "rwkv_r_k", "rwkv_lnx_g",
                               "rwkv_lnx_b", "ssm_d", "ssm_b_glu"]):
            pv[l, :, 20 + 4 * j:24 + 4 * j] = _vec4(f(k)[l].reshape(-1))
    out["pv"] = pv
    def pair(a):
        return np.ascontiguousarray(a.reshape(16, 2, 64).transpose(1, 2, 0).reshape(128, 16))
    sp = np.zeros((NL, 128, 3, 16), np.float32)
    sc = np.zeros((NL, 128, 5, 4, 64), np.float32)
    cx = np.zeros((NL, 2, 128, 16, 128), np.float32)
    for l in range(NL):
        lr, li, ld = f("ssm_lam_re")[l], f("ssm_lam_im")[l], f("ssm_log_dt")[l]
        sp[l, :, 0] = pair(lr)
        sp[l, :, 1] = pair(li)
        sp[l, :, 2] = pair(np.repeat(ld[:, None], 64, 1))
        def ch(a):
            return np.repeat(a.reshape(4, 8, 1, 64), 16, 2).transpose(1, 2, 0, 3).reshape(128, 4, 64)
        sc[l, :, 0] = ch(lr)
        sc[l, :, 1] = ch(li)
        sc[l, :, 2] = ch(np.repeat(ld[:, None], 64, 1))
        for j, k in ((3, "ssm_b_re"), (4, "ssm_b_im")):
            b = f(k)[l]
            sc[l, :, j] = b.reshape(4, 8, 64, 16).transpose(1, 3, 0, 2).reshape(128, 4, 64)
        for j, k in ((0, "ssm_c_re"), (1, "ssm_c_im")):
            cc = f(k)[l]
            for pi in range(16):
                for g2 in range(2):
                    g = 2 * pi + g2
                    glo = g % 8
                    cx[l, j, g2 * 64:(g2 + 1) * 64, pi, glo * 16:(glo + 1) * 16] = cc[g].T
    out["ssm_pair"] = sp
    out["ssm_ch"] = sc.reshape(NL, 128, 5, 256)
    out["cext"] = cx.reshape(NL, 2, 128, 2048)
    tab = f("att_rel_bias")
    i = np.arange(128)[:, None]
    jj = np.arange(256)[None, :]
    idxA = np.minimum(256 + i - jj, 256)
    idxB = np.minimum(320 + (i - 64) - jj, 256)
    idx = np.where(i < 64, idxA, idxB)
    idx = np.clip(idx, 0, 256)
    braw = tab[:, :, idx]
    out["braw"] = np.ascontiguousarray(braw.transpose(0, 2, 1, 3))
    out["bconst"] = np.ascontiguousarray(np.repeat(tab[:, None, :, 256], 128, 1))
    lg = np.zeros((NL, 128, 2, D), np.float32)
    lg[:, :, 0, :] = f("ln_g")[:, None, :]
    lg[:, :, 1, :] = f("ln_b")[:, None, :]
    out["lngb"] = lg
    out["cst"] = _consts()
    for k in ("w_in", "w_out", "ssm_w_glu", "rwkv_w1", "rwkv_w2", "rwkv_a1", "rwkv_a2"):
        out[k] = np.ascontiguousarray(f(k))
    return out


def _core_inputs(inp, core, shared):
    f = lambda k: np.asarray(inp[k], np.float32)
    m = dict(shared)
    if core < 2:
        m["xp"] = np.ascontiguousarray(f("x_prompt")[core])
    else:
        m["xp"] = np.zeros((SEQ, D), np.float32)
    s0 = NS * core
    m["xs"] = np.ascontiguousarray(f("x_sample")[s0:s0 + NS].reshape(NS * SQ, D))
    m["ck"] = np.ascontiguousarray(f("cache_att_k")[:, s0:s0 + NS].reshape(NL, NS, 512, 1024))
    m["cv"] = np.ascontiguousarray(f("cache_att_v")[:, s0:s0 + NS].reshape(NL, NS, 512, 1024))
    hs = np.zeros((NL, NS, 128, 16, 2), np.float32)
    for j, k in ((0, "state_ssm_re"), (1, "state_ssm_im")):
        a = f(k)[:, s0:s0 + NS]
        hs[..., j] = a.reshape(NL, NS, 16, 2, 64).transpose(0, 1, 3, 4, 2).reshape(NL, NS, 128, 16)
    m["hss"] = hs.reshape(NL, NS, 128, 32)
    sr = f("state_rwkv")[:, s0:s0 + NS]
    z = np.zeros((NL, NS, 2, 64, 4, 2, 64), np.float32)
    for ft in range(4):
        for h2 in range(2):
            z[:, :, h2, :, ft, h2, :] = sr[:, :, 2 * ft + h2].transpose(0, 1, 3, 2)
    m["srw"] = z.reshape(NL, NS, 128, 512)
    sh = f("state_rwkv_shift")[:, s0:s0 + NS]
    m["ssh"] = np.ascontiguousarray(sh.reshape(NL, NS, 16, 128).transpose(0, 1, 3, 2))
    return m

class Arena:
    def __init__(self, t, size):
        self.t = t
        self.size = size
        self.top = 0
        self.hi = 0

    def alloc(self, *shape, parts=128):
        n = 1
        for s in shape:
            n *= s
        off = self.top
        self.top += n
        self.hi = max(self.hi, self.top)
        assert self.top <= self.size, ("arena overflow", self.top, self.size)
        ap = self.t[0:parts, off:off + n]
        if len(shape) == 2:
            ap = ap.rearrange("p (a b) -> p a b", a=shape[0])
        elif len(shape) == 3:
            ap = ap.rearrange("p (a b c) -> p a b c", a=shape[0], b=shape[1])
        elif len(shape) == 4:
            ap = ap.rearrange("p (a b c d) -> p a b c d", a=shape[0], b=shape[1], c=shape[2])
        return ap

    def reset(self, top=0):
        self.top = top


def cview(cst, name, parts=128, w=None):
    o, ww = _CST[name]
    return cst[0:parts, o:o + (w or ww)]


class _Stop(Exception):
    pass


def build_program(seq=SEQ, do_prompt=True, do_sample=True, nlayers=NL, stage=99, ntiles=None):
    import contextlib
    nc = bass.Bass("TRN2", target_bir_lowering=False)
    NT = seq // TT if ntiles is None else ntiles

    def din(name, shape, dt=F32):
        return nc.dram_tensor(name, list(shape), dt, kind="ExternalInput").ap()

    def dout(name, shape, dt=F32):
        return nc.dram_tensor(name, list(shape), dt, kind="ExternalOutput").ap()

    I = {}
    I["xp"] = din("xp", [SEQ, D])
    I["xs"] = din("xs", [NS * SQ, D])
    I["ck"] = din("ck", [NL, NS, 512, 1024])
    I["cv"] = din("cv", [NL, NS, 512, 1024])
    I["hss"] = din("hss", [NL, NS, 128, 32])
    I["srw"] = din("srw", [NL, NS, 128, 512])
    I["ssh"] = din("ssh", [NL, NS, 128, 16])
    I["w_in"] = din("w_in", [NL, D, DIN])
    I["w_out"] = din("w_out", [NL, D, D])
    I["pv"] = din("pv", [NL, 128, NPV])
    I["ssm_pair"] = din("ssm_pair", [NL, 128, 3, 16])
    I["ssm_ch"] = din("ssm_ch", [NL, 128, 5, 256])
    I["cext"] = din("cext", [NL, 2, 128, 2048])
    I["ssm_w_glu"] = din("ssm_w_glu", [NL, 512, 512])
    I["braw"] = din("braw", [NL, 128, 16, 256])
    I["bconst"] = din("bconst", [NL, 128, 16])
    I["rwkv_w1"] = din("rwkv_w1", [NL, 512, 64])
    I["rwkv_w2"] = din("rwkv_w2", [NL, 64, 512])
    I["rwkv_a1"] = din("rwkv_a1", [NL, 512, 64])
    I["rwkv_a2"] = din("rwkv_a2", [NL, 64, 512])
    I["lngb"] = din("lngb", [NL, 128, 2, D])
    I["cst"] = din("cst", [128, NCST])

    O = {}
    O["y_p"] = dout("y_p", [SEQ, D])
    O["y_s"] = dout("y_s", [NS * SQ, D])
    O["p_k"] = dout("p_k", [NL, 512, 1024])
    O["p_v"] = dout("p_v", [NL, 512, 1024])
    O["p_ssm"] = dout("p_ssm", [NL, 128, 32])
    O["p_rwkv"] = dout("p_rwkv", [NL, 128, 512])
    O["p_shift"] = dout("p_shift", [NL, 1, 2048])
    O["s_k"] = dout("s_k", [NL, NS * SQ, 1024])
    O["s_v"] = dout("s_v", [NL, NS * SQ, 1024])
    O["s_ssm"] = dout("s_ssm", [NL, NS, 128, 32])
    O["s_rwkv"] = dout("s_rwkv", [NL, NS, 128, 512])
    O["s_shift"] = dout("s_shift", [NL, NS, 1, 2048])
    wsc = nc.dram_tensor("wsc", [NL, NCH, 128, 16 * CW], BF16, kind="Internal").ap()
    y0p = nc.dram_tensor("y0p", [SEQ, D], F32, kind="Internal").ap()
    y0s = nc.dram_tensor("y0s", [NS * SQ, D], F32, kind="Internal").ap()

    ST = contextlib.ExitStack()

    def sb(name, shape, dt=F32):
        return ST.enter_context(nc.sbuf_tensor("sb_" + name, list(shape), dt))

    def pst(name, shape, dt=F32):
        return ST.enter_context(nc.psum_tensor("pp_" + name, list(shape), dt))

    with ST:
        P = Prog(nc)
        cst = sb("cst", [128, NCST])
        identb = sb("identb", [128, 128], BF16)
        obb = sb("obb", [128, 128], BF16)
        musb = {64: sb("mus64", [128, 128], BF16), 16: sb("mus16", [32, 32], BF16)}
        mlsb = {64: sb("mls64", [128, 128], BF16), 16: sb("mls16", [32, 32], BF16)}
        muib = {64: sb("mui64", [128, 64], BF16), 16: sb("mui16", [32, 16], BF16)}
        maskBb = sb("maskBb", [128, 64], BF16)
        xT = sb("xT", [128, 16, TT], BF16)
        wbuf = [sb("wbuf0", [128, 16, CW], BF16), sb("wbuf1", [128, 16, CW], BF16)]
        mixT = sb("mixT", [128, 16, TT], BF16)
        pv = sb("pv", [128, NPV])
        pvx = sb("pvx", [128, 8])
        tabc = sb("tabc", [128, 16, 128])
        tabs = sb("tabs", [128, 16, 128])
        rho = sb("rho", [128, 16])
        bext = [sb("bext_re", [128, 16, 128], BF16), sb("bext_im", [128, 16, 128], BF16)]
        cext = [sb("cext_re", [128, 16, 128], BF16), sb("cext_nim", [128, 16, 128], BF16)]
        wglu = sb("wglu", [128, 4, 512], BF16)
        hcar = sb("hcar", [128, 16, 2])
        KT = sb("KT", [128, 8, 768], BF16)
        VT = sb("VT", [128, 6, 1024], BF16)
        btab = sb("btab", [128, 16, 256], BF16)
        lw1 = sb("lw1", [128, 4, 64], BF16)
        la1 = sb("la1", [128, 4, 64], BF16)
        lw2 = sb("lw2", [64, 512], BF16)
        la2 = sb("la2", [64, 512], BF16)
        Sst = sb("Sst", [128, 4, 128])
        shc = sb("shc", [128, 16])
        hcar1 = sb("hcar1", [128, 16, 2])
        Sst1 = sb("Sst1", [128, 4, 128])
        shc1 = sb("shc1", [128, 16])
        A32N = 8448
        A16N = 30400
        ar32 = sb("ar32", [128, A32N])
        ar16 = sb("ar16", [128, A16N], BF16)
        A32 = Arena(ar32, A32N)
        A16 = Arena(ar16, A16N)
        psA = pst("psA", [128, 512])
        psI = [pst("psI0", [128, 512]), pst("psI1", [128, 512])]
        psS = [pst("psS0", [128, 1024]), pst("psS1", [128, 1024])]
        psB = pst("psB", [128, 1024], BF16)

        identf = cview(cst, "identf")

        def chk(n):
            if stage == n:
                raise _Stop()

        P.dma("sp", cst[:], I["cst"][:, :])
        P.copy("dve", identb[:], identf)
        P.copy("dve", obb[:], cview(cst, "ob"))
        P.copy("dve", musb[64][:], cview(cst, "mus128"))
        P.copy("dve", mlsb[64][:], cview(cst, "mls128"))
        P.copy("dve", muib[64][:], cview(cst, "mui128"))
        P.copy("dve", musb[16][:], cview(cst, "mus32", 32))
        P.copy("dve", mlsb[16][:], cview(cst, "mls32", 32))
        P.copy("dve", muib[16][:], cview(cst, "mui32", 32))
        P.copy("dve", maskBb[:], cview(cst, "maskB"))
        zm = cview(cst, "zm")

        cvt_stage = [A16.alloc(16, CW), A16.alloc(16, CW)]
        for l in range(nlayers):
            for c in range(NCH):
                if c < NCH_IN:
                    src = I["w_in"][l, :, c * CW:(c + 1) * CW]
                else:
                    src = I["w_out"][l, :, (c - NCH_IN) * CW:(c - NCH_IN + 1) * CW]
                stg = cvt_stage[(l * NCH + c) % 2]
                P.dma("pool", stg, src.rearrange("(kt p) c -> p kt c", p=128))
                P.dma("sp", wsc[l, c].rearrange("p (kt c) -> p kt c", kt=16), stg)

        wslot = [0]

        def load_w(l, c):
            wb = wbuf[wslot[0]]
            wslot[0] ^= 1
            P.dma("sp", wb[:], wsc[l, c].rearrange("p (kt c) -> p kt c", kt=16))
            return wb

        evq = [0]

        def ev_eng():
            evq[0] ^= 1
            return "act" if evq[0] else "dve"

        pin = [0]

        def fm_proj(wb, T, evac):
            for c2 in range(2):
                ps = psI[pin[0]]
                pin[0] ^= 1
                for kt in range(16):
                    P.mm(ps[:, 0:T], wb[:, kt, c2 * 128:(c2 + 1) * 128], xT[:, kt, 0:T],
                         start=(kt == 0), stop=(kt == 15))
                evac(ps[:, 0:T], c2)

        def tm_proj(wb, T, evac):
            for t0 in range(0, T, 128):
                n = min(128, T - t0)
                ps = psI[pin[0]]
                pin[0] ^= 1
                for kt in range(16):
                    P.mm(ps[0:n, 0:CW], xT[:, kt, t0:t0 + n], wb[:, kt, :], start=(kt == 0), stop=(kt == 15))
                evac(ps[0:n, 0:CW], t0, n)

        def layer_setup(l):
            A32.reset()
            A16.reset()
            P.dma("sp", pv[:], I["pv"][l])
            P.ts("dve", pvx[:, 0:4], pv[:, 32:36], -1.0, ALU.mult, 1.0, ALU.add)
            P.dma("pool", wglu[:], I["ssm_w_glu"][l].rearrange("(kt p) c -> p kt c", p=128))
            P.dma("pool", lw1[:], I["rwkv_w1"][l].rearrange("(kt p) c -> p kt c", p=128))
            P.dma("pool", la1[:], I["rwkv_a1"][l].rearrange("(kt p) c -> p kt c", p=128))
            P.dma("pool", lw2[:], I["rwkv_w2"][l])
            P.dma("pool", la2[:], I["rwkv_a2"][l])
            P.dma("pool", cext[0][:], I["cext"][l, 0].rearrange("p (a b) -> p a b", a=16))
            P.dma("pool", cext[1][:], I["cext"][l, 1].rearrange("p (a b) -> p a b", a=16))
            P.ts("pool", cext[1][:], cext[1][:], -1.0, ALU.mult)
            braw = A32.alloc(16, 256)
            bco = A32.alloc(16)
            P.dma("sp", braw, I["braw"][l])
            P.dma("sp", bco, I["bconst"][l])
            P.tt("dve", braw, braw, bco.unsqueeze(2).to_broadcast([128, 16, 256]), ALU.subtract)
            P.tt("dve", btab[:], braw, cview(cst, "bneg").unsqueeze(1).to_broadcast([128, 16, 256]), ALU.add)
            A32.reset()

            def abq(lr, li, ldt, shape, want_q):
                al = lambda: A32.alloc(*shape)
                dt = al()
                P.act(dt, ldt, AF.Exp)
                e = al()
                P.tt("dve", e, lr, dt, ALU.mult)
                P.act(e, e, AF.Exp)
                th = al()
                P.tt("dve", th, li, dt, ALU.mult)
                kf = al()
                P.ts("dve", kf, th, 1.0 / (2 * math.pi), ALU.mult)
                P.ts("dve", kf, kf, 12582912.0, ALU.add)
                P.ts("dve", kf, kf, 12582912.0, ALU.subtract)
                r = al()
                P.stt(r, kf, -6.28125, th, ALU.mult, ALU.add)
                P.stt(r, kf, -(2 * math.pi - 6.28125), r, ALU.mult, ALU.add)
                P.ts("dve", r, r, math.pi, ALU.min, -math.pi, ALU.max)
                sn = al()
                P.act(sn, r, AF.Sin)
                ab = al()
                P.act(ab, r, AF.Abs)
                P.ts("dve", ab, ab, -1.0, ALU.mult, math.pi / 2, ALU.add)
                cs = al()
                P.act(cs, ab, AF.Sin)
                res = {"rho": e, "cos": cs, "sin": sn}
                if want_q:
                    abr = al()
                    abi = al()
                    P.tt("dve", abr, e, cs, ALU.mult)
                    P.tt("dve", abi, e, sn, ALU.mult)
                    P.ts("dve", abr, abr, -1.0, ALU.add)
                    den = al()
                    t2 = al()
                    P.tt("dve", den, lr, lr, ALU.mult)
                    P.tt("dve", t2, li, li, ALU.mult)
                    P.tt("dve", den, den, t2, ALU.add)
                    P.recip(den, den)
                    qr = al()
                    qi = al()
                    P.tt("dve", qr, abr, lr, ALU.mult)
                    P.tt("dve", t2, abi, li, ALU.mult)
                    P.tt("dve", qr, qr, t2, ALU.add)
                    P.tt("dve", qr, qr, den, ALU.mult)
                    P.tt("dve", qi, abi, lr, ALU.mult)
                    P.tt("dve", t2, abr, li, ALU.mult)
                    P.tt("dve", qi, qi, t2, ALU.subtract)
                    P.tt("dve", qi, qi, den, ALU.mult)
                    res["qr"] = qr
                    res["qi"] = qi
                return res

            spr = A32.alloc(3, 16)
            P.dma("sp", spr, I["ssm_pair"][l])
            r1 = abq(spr[:, 0, :], spr[:, 1, :], spr[:, 2, :], (16,), False)
            P.copy("dve", rho[:], r1["rho"])
            P.copy("dve", tabc[:, :, 0], r1["cos"])
            P.copy("dve", tabs[:, :, 0], r1["sin"])
            n = 1
            while n < 128:
                tc0 = tabc[:, :, 0:n]
                ts0 = tabs[:, :, 0:n]
                Cn = tabc[:, :, n - 1:n].to_broadcast([128, 16, n])
                Sn = tabs[:, :, n - 1:n].to_broadcast([128, 16, n])
                t1 = A32.alloc(16, n)
                t2 = A32.alloc(16, n)
                P.tt("dve", t1, tc0, Cn, ALU.mult)
                P.tt("dve", t2, ts0, Sn, ALU.mult)
                P.tt("dve", tabc[:, :, n:2 * n], t1, t2, ALU.subtract)
                P.tt("dve", t1, ts0, Cn, ALU.mult)
                P.tt("dve", t2, tc0, Sn, ALU.mult)
                P.tt("dve", tabs[:, :, n:2 * n], t1, t2, ALU.add)
                n *= 2
            A32.reset()
            sch = A32.alloc(5, 256)
            P.dma("sp", sch, I["ssm_ch"][l])
            r2 = abq(sch[:, 0, :], sch[:, 1, :], sch[:, 2, :], (256,), True)
            bre = A32.alloc(4, 64)
            bim = A32.alloc(4, 64)
            t3 = A32.alloc(256)
            brf = bre.rearrange("p a b -> p (a b)")
            bif = bim.rearrange("p a b -> p (a b)")
            P.tt("dve", brf, r2["qr"], sch[:, 3, :], ALU.mult)
            P.tt("dve", t3, r2["qi"], sch[:, 4, :], ALU.mult)
            P.tt("dve", brf, brf, t3, ALU.subtract)
            P.tt("dve", bif, r2["qr"], sch[:, 4, :], ALU.mult)
            P.tt("dve", t3, r2["qi"], sch[:, 3, :], ALU.mult)
            P.tt("dve", bif, bif, t3, ALU.add)
            bmo = _CST["bmask"][0]
            for pi in range(16):
                for g2 in range(2):
                    msk = cst[:, bmo + (pi % 4) * 2 + g2: bmo + (pi % 4) * 2 + g2 + 1]
                    P.ts("dve", bext[0][:, pi, g2 * 64:(g2 + 1) * 64], bre[:, pi // 4, :], msk, ALU.mult)
                    P.ts("pool", bext[1][:, pi, g2 * 64:(g2 + 1) * 64], bim[:, pi // 4, :], msk, ALU.mult)
            A32.reset()

        def load_xT(src, T):
            for t0 in range(0, T, 128):
                n = min(128, T - t0)
                top = A32.top
                xs = A32.alloc(D)
                P.dma("sp", xs[0:n, :], src[t0:t0 + n, :])
                for g in range(4):
                    for j in range(4):
                        kt = g * 4 + j
                        P.tr(psA[:, j * 128:j * 128 + n], xs[0:n, kt * 128:(kt + 1) * 128], identf[0:n, 0:n])
                    P.copy(ev_eng(), xT[:, g * 4:(g + 1) * 4, t0:t0 + n],
                           psA[:].rearrange("p (a b) -> p a b", a=4)[:, :, 0:n])
                A32.reset(top)

        def ssm_phase(l, T, segs, gs_out):
            ussm = A16.alloc(4, T)
            gs = A16.alloc(4, T)

            def ev_u(ci):
                def f(ps, c2):
                    P.copy(ev_eng(), ussm[:, ci * 2 + c2, :], ps)
                return f

            def ev_g(ci):
                def f(ps, c2):
                    P.act(gs[:, ci * 2 + c2, :], ps, AF.Silu)
                return f

            for ci in range(2):
                fm_proj(load_w(l, ci), T, ev_u(ci))
            for ci in range(2):
                fm_proj(load_w(l, 2 + ci), T, ev_g(ci))
            z32 = A32.alloc(4, T)
            zbf = A16.alloc(4, T)
            for ct in range(4):
                pY = psS[0][:, 512:512 + T]
                items = []
                for (c0, ncol, car) in segs:
                    for s0 in range(c0, c0 + ncol, 128):
                        ns = min(128, c0 + ncol - s0)
                        for pm in range(4):
                            items.append(dict(s0=s0, ns=ns, pm=pm, pi=ct * 4 + pm, car=car))
                base32, base16 = A32.top, A16.top

                def stA(n, it):
                    pZt = (psS[1][:, 0:256], psS[1][:, 512:768], psA[:, 0:256])[n % 3]
                    pZ = pZt.rearrange("p (a b) -> p a b", a=2)[:, :, 0:it["ns"]]
                    it["pZ"] = pZ
                    u_ = ussm[:, ct, it["s0"]:it["s0"] + it["ns"]]
                    P.mm(pZ[:, 0, :], bext[0][:, it["pi"], :], u_)
                    P.mm(pZ[:, 1, :], bext[1][:, it["pi"], :], u_)

                def stB1(n, it):
                    ns, pi, car, pZ = it["ns"], it["pi"], it["car"], it["pZ"]
                    A32.reset(base32 + (n % 3) * 1024)
                    A16.reset(base16 + (n % 3) * 256)
                    tcb = tabc[:, pi, 0:ns].unsqueeze(1).to_broadcast([128, 2, ns])
                    tsb = tabs[:, pi, 0:ns].unsqueeze(1).to_broadcast([128, 2, ns])
                    M1 = A32.alloc(2, ns)
                    M2 = A32.alloc(2, ns)
                    P.tt("dve", M1, pZ, tcb, ALU.mult)
                    P.tt("dve", M2, pZ, tsb, ALU.mult)
                    Zp = A32.alloc(2, ns)
                    P.tt("pool", Zp[:, 0, :], M1[:, 0, :], M2[:, 1, :], ALU.add)
                    P.tt("pool", Zp[:, 1, :], M1[:, 1, :], M2[:, 0, :], ALU.subtract)
                    Hp = A32.alloc(2, ns)
                    Dm = A16.alloc(2, ns)
                    it.update(M1=M1, M2=M2, Zp=Zp, Hp=Hp, Dm=Dm, tcb=tcb, tsb=tsb)

                def stB2(n, it):
                    ns, pi, car = it["ns"], it["pi"], it["car"]
                    M1, M2, Zp, Hp, Dm, tcb, tsb = (it[k_] for k_ in ("M1", "M2", "Zp", "Hp", "Dm", "tcb", "tsb"))
                    rb = rho[:, pi:pi + 1].to_broadcast([128, ns])
                    P.scan(Hp[:, 0, :], rb, Zp[:, 0, :], car[:, pi, 0:1])
                    P.scan(Hp[:, 1, :], rb, Zp[:, 1, :], car[:, pi, 1:2])
                    P.tt("dve", M1, Hp, tcb, ALU.mult)
                    P.tt("dve", M2, Hp, tsb, ALU.mult)
                    P.tt("pool", Dm[:, 0, :], M1[:, 0, :], M2[:, 1, :], ALU.subtract)
                    P.tt("pool", Dm[:, 1, :], M1[:, 1, :], M2[:, 0, :], ALU.add)
                    P.tt("dve", car[:, pi, 0:1], M1[:, 0, ns - 1:ns], M2[:, 1, ns - 1:ns], ALU.subtract)
                    P.tt("dve", car[:, pi, 1:2], M1[:, 1, ns - 1:ns], M2[:, 0, ns - 1:ns], ALU.add)
                    it["Dm"] = Dm

                def stC(n, it):
                    s0, ns, pm, pi, Dm = it["s0"], it["ns"], it["pm"], it["pi"], it["Dm"]
                    P.mm(pY[:, s0:s0 + ns], cext[0][:, pi, :], Dm[:, 0, :], start=(pm == 0), stop=False)
                    P.mm(pY[:, s0:s0 + ns], cext[1][:, pi, :], Dm[:, 1, :], start=False, stop=(pm == 3))

                nit = len(items)
                for n in range(min(2, nit)):
                    stA(n, items[n])
                stB1(0, items[0])
                for n in range(nit):
                    if n + 1 < nit:
                        stB1(n + 1, items[n + 1])
                    stB2(n, items[n])
                    if n + 2 < nit:
                        stA(n + 2, items[n + 2])
                    stC(n, items[n])
                A32.reset(base32)
                A16.reset(base16)
                top32 = A32.top
                ysb = A32.alloc(T)
                P.stt(ysb, ussm[:, ct, :], pv[:, 48 + ct:49 + ct], pY, ALU.mult, ALU.add)
                t1 = A32.alloc(T)
                P.tt("pool", t1, ysb, ysb, ALU.mult)
                P.ts("pool", t1, t1, 0.044715, ALU.mult, 1.0, ALU.add)
                P.tt("pool", t1, t1, ysb, ALU.mult)
                P.act(t1, t1, AF.Sigmoid, scale=1.5957691216057308)
                P.tt("dve", z32[:, ct, :], ysb, t1, ALU.mult)
                P.copy("pool", zbf[:, ct, :], z32[:, ct, :])
                A32.reset(top32)
            for co in range(4):
                ps = psI[pin[0]]
                pin[0] ^= 1
                for ci in range(4):
                    P.mm(ps[:, 0:T], wglu[:, ci, co * 128:(co + 1) * 128], zbf[:, ci, :], start=(ci == 0), stop=(ci == 3))
                top32 = A32.top
                sg = A32.alloc(T)
                P.act(sg, ps[:, 0:T], AF.Sigmoid, bias=pv[:, 52 + co:53 + co])
                P.tt("dve", sg, sg, z32[:, co, :], ALU.mult)
                P.tt("dve", mixT[:, co, 0:T], sg, gs[:, co, :], ALU.mult)
                A32.reset(top32)

        def att_phase(l, T, units, kcol0, vrow0, kv_out):
            qT = A16.alloc(8, T)
            ga = A16.alloc(8, T)
            kvstage = {}

            def ev_q(ci):
                def f(ps, c2):
                    P.ts("dve", qT[:, ci * 2 + c2, :], ps, 0.125, ALU.mult)
                return f

            def ev_k(ci):
                def f(ps, c2):
                    P.copy(ev_eng(), KT[:, ci * 2 + c2, kcol0:kcol0 + T], ps)
                return f

            def ev_g(ci):
                def f(ps, c2):
                    P.act(ga[:, ci * 2 + c2, :], ps, AF.Silu)
                return f

            def ev_tok(ci, dst_bf, dram_fn):
                def f(ps, t0, n):
                    if dst_bf is not None:
                        r0 = vrow0 + t0
                        P.copy("act", VT[(r0 % 128):(r0 % 128) + n, r0 // 128, ci * CW:(ci + 1) * CW], ps)
                    if dram_fn is not None:
                        key = (id(dram_fn), t0)
                        if key not in kvstage:
                            kvstage[key] = A32.alloc(1024)
                        stg = kvstage[key]
                        P.copy("dve", stg[0:n, ci * CW:(ci + 1) * CW], ps)
                        if ci == 3:
                            P.dma("sp", dram_fn(t0, n), stg[0:n, :])
                return f

            for ci in range(4):
                fm_proj(load_w(l, 4 + ci), T, ev_q(ci))
            for ci in range(4):
                wb = load_w(l, 8 + ci)
                fm_proj(wb, T, ev_k(ci))
                if kv_out[0] is not None:
                    tm_proj(wb, T, ev_tok(ci, None, kv_out[0]))
            for ci in range(4):
                tm_proj(load_w(l, 12 + ci), T, ev_tok(ci, True if kv_out[2] else None, kv_out[1]))
            for ci in range(4):
                fm_proj(load_w(l, 16 + ci), T, ev_g(ci))

            sbuf_i = [0]
            att_rot = [0]

            def run_units(units):
                items = [(u, h) for u in units for h in range(16)]
                base32, base16 = A32.top, A16.top
                stt_ = {}

                def stA(i):
                    u, h = items[i]
                    nq, ncol = u["nq"], u["ncol"]
                    ft, bp = h // 2, 64 * (h % 2)
                    ps = psS[i % 2]
                    for (r0, nr, tq0, kc0, c0, c1) in u["halves"]:
                        for (a_, b_) in ((c0, min(c1, 512)), (max(c0, 512), c1)):
                            if b_ <= a_:
                                continue
                            P.mm(ps[r0:r0 + nr, a_:b_], qT[bp:bp + 64, ft, tq0:tq0 + nr],
                                 KT[bp:bp + 64, ft, kc0 + a_ - c0:kc0 + b_ - c0], start=True, stop=False,
                                 skip_group_check=True)
                    for (a_, b_) in ((384, 512), (512, min(640, ncol))):
                        if b_ <= a_:
                            continue
                        P.mm(ps[0:nq, a_:b_], identb[0:nq, 0:nq], btab[0:nq, h, a_ - 384:b_ - 384], start=False, stop=True,
                             skip_group_check=True)
                    if u["maskB"]:
                        P.mm(ps[0:nq, 0:64], identb[0:nq, 0:nq], maskBb[0:nq, :], start=False, stop=True,
                             skip_group_check=True)
                    if u["inv"] > 0:
                        P.ts("dve", ps[0:nq, 0:u["inv"]], ps[0:nq, 0:u["inv"]], NEG, ALU.add)
                    stt_[i] = dict(ps=ps)

                def stB1(i):
                    u, h = items[i]
                    nq, ncol = u["nq"], u["ncol"]
                    nb = (ncol + 127) // 128
                    ps = stt_[i]["ps"]
                    A32.reset(base32 + (i % 3) * 4)
                    A16.reset(base16 + (i % 3) * 1280)
                    mx = A32.alloc(1)
                    P.op("dve", (lambda e, o_=mx[0:nq, :], i_=ps[0:nq, 0:ncol]: e.tensor_reduce(
                        out=o_, in_=i_, axis=AX.X, op=ALU.max, negate=True)), [ps[0:nq, 0:ncol]], [mx[0:nq, :]])
                    pb = A16.alloc(nb * 128)
                    rs = A32.alloc(1)
                    P.act(pb[0:nq, 0:ncol], ps[0:nq, 0:ncol], AF.Exp, bias=mx[0:nq, :], accum_out=rs[0:nq, :])
                    pts = A16.alloc(nb, 128)
                    stt_[i].update(pb=pb, pts=pts, nb=nb, rs=rs)

                def stB2(i):
                    u, h = items[i]
                    nq, ncol = u["nq"], u["ncol"]
                    pb, rs = stt_[i]["pb"], stt_[i]["rs"]
                    P.recip(rs[0:nq, :], rs[0:nq, :])
                    P.ts("dve", pb[0:nq, 0:ncol], pb[0:nq, 0:ncol], rs[0:nq, :], ALU.mult)

                def stC(i):
                    u, h = items[i]
                    nq, ncol = u["nq"], u["ncol"]
                    pb, nb = stt_[i]["pb"], stt_[i]["nb"]
                    pT = psB[:, 0:nb * 128].rearrange("p (a b) -> p a b", a=nb)
                    for bi in range(nb):
                        kb = min(128, ncol - bi * 128)
                        P.tr(pT[0:kb, bi, 0:nq], pb[0:nq, bi * 128:bi * 128 + kb], identb[0:nq, 0:nq])
                    pts = stt_[i]["pts"]
                    nfull = ncol // 128
                    if nfull:
                        P.copy("act", pts[:, 0:nfull, 0:nq], pT[:, 0:nfull, 0:nq])
                    if ncol % 128:
                        kb = ncol % 128
                        P.copy("act", pts[0:kb, nfull, 0:nq], pT[0:kb, nfull, 0:nq])

                def stE(i):
                    u, h = items[i]
                    nq, ncol = u["nq"], u["ncol"]
                    ft, bp = h // 2, 64 * (h % 2)
                    pts, nb = stt_[i]["pts"], stt_[i]["nb"]
                    po = psA[bp:bp + 64, (i % 4) * 128:(i % 4) * 128 + nq]
                    vb0 = u["vblk0"]
                    for bi in range(nb):
                        kb = min(128, ncol - bi * 128)
                        P.mm(po, VT[0:kb, vb0 + bi, h * 64:(h + 1) * 64], pts[0:kb, bi, 0:nq],
                             start=(bi == 0), stop=(bi == nb - 1))
                    tq = u["tq0"]
                    P.tt("dve", mixT[bp:bp + 64, 4 + ft, tq:tq + nq], po, ga[bp:bp + 64, ft, tq:tq + nq], ALU.mult)
                    del stt_[i]

                n = len(items)
                for i in range(min(2, n)):
                    stA(i)
                stB1(0)
                for i in range(n):
                    if i + 1 < n:
                        stB1(i + 1)
                    stB2(i)
                    if i + 2 < n:
                        stA(i + 2)
                    stC(i)
                    if i >= 1:
                        stE(i - 1)
                stE(n - 1)
                A32.reset(base32)
                A16.reset(base16)

            if units is not None:
                run_units(units)
            return run_units

        def rwkv_phase(l, T, C, segs, shift_out=None):
            Z = 2 * C
            NCk = T // C
            gc = A16.alloc(4, T)
            xr = A16.alloc(4, T)
            xk = A16.alloc(4, T)
            xv = A16.alloc(4, T)
            markW = A16.top
            xw = A16.alloc(4, T)
            xa = A16.alloc(4, T)
            markR = A16.top
            raw = A16.alloc(4, 4, T)
            def ev_g(ci):
                def f(ps, c2):
                    P.act(gc[:, ci * 2 + c2, :], ps, AF.Silu)
                return f

            top32_0 = A32.top
            newsh = {}
            for kind in range(4):
                for ci in range(2):
                    wb = load_w(l, 20 + kind * 2 + ci)

                    def f(ps, c2, kind=kind, ci=ci):
                        P.copy(ev_eng(), raw[:, kind, ci * 2 + c2, :], ps)
                    fm_proj(wb, T, f)
                    if shift_out is not None:
                        for si, (c0, ncol, st, sc) in enumerate(segs):
                            psr = psI[pin[0]]
                            pin[0] ^= 1
                            lc = c0 + ncol - 1
                            for kt in range(16):
                                P.mm(psr[0:1, 0:CW], xT[:, kt, lc:lc + 1], wb[:, kt, :], start=(kt == 0), stop=(kt == 15))
                            top = A32.top
                            stg = A32.alloc(CW)
                            P.copy("act", stg[0:1, :], psr[0:1, 0:CW])
                            o0 = kind * 512 + ci * CW
                            P.dma("sp", shift_out[si][:, o0:o0 + CW], stg[0:1, :])
                            A32.reset(top)
            for ci in range(2):
                fm_proj(load_w(l, 28 + ci), T, ev_g(ci))
            chk(601)
            dl = A32.alloc(4, 4, T)
            for (c0, ncol, st, sc) in segs:
                P.tt("dve", dl[:, :, :, c0:c0 + 1], sc[:].rearrange("p (a b) -> p a b", a=4).unsqueeze(3),
                     raw[:, :, :, c0:c0 + 1], ALU.subtract)
                if ncol > 1:
                    P.tt("dve", dl[:, :, :, c0 + 1:c0 + ncol], raw[:, :, :, c0:c0 + ncol - 1],
                         raw[:, :, :, c0 + 1:c0 + ncol], ALU.subtract)
            chk(602)
            for si, (c0, ncol, st, sc) in enumerate(segs):
                lc = c0 + ncol - 1
                P.copy("pool", sc[:].rearrange("p (a b) -> p a b", a=4).unsqueeze(3), raw[:, :, :, lc:lc + 1])
            chk(603)
            mu = lambda i: pv[:, i * 4:(i + 1) * 4].unsqueeze(2).to_broadcast([128, 4, T])
            tmp = A32.alloc(4, T)
            for dst, kind, mi in ((xr, 0, 0), (xk, 1, 1), (xv, 2, 2), (xw, 3, 3), (xa, 3, 4)):
                P.tt("dve", tmp, dl[:, kind, :, :], mu(mi), ALU.mult)
                P.tt("dve", dst, tmp, raw[:, kind, :, :], ALU.add)
            A32.reset(top32_0)
            A16.reset(markR)
            chk(61)
            sig = A32.alloc(4, T)
            aa = A32.alloc(4, T)
            for (src, w1, w2, dst, b0, use_tanh) in ((xw, lw1, lw2, sig, 20, True), (xa, la1, la2, aa, 24, False)):
                ps = psI[pin[0]]
                pin[0] ^= 1
                for kt in range(4):
                    P.mm(ps[0:64, 0:T], w1[:, kt, :], src[:, kt, :], start=(kt == 0), stop=(kt == 3))
                top16 = A16.top
                hh = A16.alloc(T)
                if use_tanh:
                    P.act(hh[0:64, :], ps[0:64, 0:T], AF.Tanh)
                else:
                    P.copy("dve", hh[0:64, :], ps[0:64, 0:T])
                for ft in range(4):
                    ps2 = psI[pin[0]]
                    pin[0] ^= 1
                    P.mm(ps2[:, 0:T], w2[:, ft * 128:(ft + 1) * 128], hh[0:64, :])
                    P.act(dst[:, ft, :], ps2[:, 0:T], AF.Sigmoid, bias=pv[:, b0 + ft:b0 + ft + 1])
                A16.reset(top16)
            A16.reset(markW)
            chk(62)
            kkn = A16.alloc(4, T)
            kp = A16.alloc(4, T)
            bb = A16.alloc(4, T)
            top32 = A32.top
            kk32 = A32.alloc(4, T)
            sq16 = A16.alloc(4, T)
            for ft in range(4):
                P.ts("dve", kk32[:, ft, :], xk[:, ft, :], pv[:, 28 + ft:29 + ft], ALU.mult)
            P.tt("pool", sq16, kk32, kk32, ALU.mult)
            rn = A32.alloc(4, T)
            for half in range(0, 4, 2):
                psq = psS[0]
                for ft in range(half, half + 2):
                    P.mm(psq[:, (ft - half) * 512:(ft - half) * 512 + T], obb[:], sq16[:, ft, :])
                    P.ts("dve", rn[:, ft, :], psq[:, (ft - half) * 512:(ft - half) * 512 + T], 1e-24, ALU.max)
            P.act(rn, rn, AF.Sqrt)
            P.recip(rn, rn)
            P.tt("dve", kkn, kk32, rn, ALU.mult)
            t1 = A32.alloc(4, T)
            for ft in range(4):
                P.ts("dve", t1[:, ft, :], aa[:, ft, :], pv[:, 32 + ft:33 + ft], ALU.mult, pvx[:, ft:ft + 1], ALU.add)
            P.tt("dve", kp, xk, t1, ALU.mult)
            P.tt("pool", bb, kkn, aa, ALU.mult)
            A32.reset(top32)
            chk(63)
            ld = A32.alloc(4, T)
            cum = A32.alloc(4, T)
            P.ts("dve", ld, sig, -KAPPA, ALU.mult)
            cmk = cview(cst, "cmask64" if C == 64 else "cmask16", w=T)
            for ft in range(4):
                P.scan(cum[:, ft, :], cmk, ld[:, ft, :], 0.0)
            cumc = cum.rearrange("p f (k c) -> p f k c", c=C)
            cend = cumc[:, :, :, C - 1:C]
            E = A32.alloc(4, T)
            rt = A16.alloc(4, T)
            at = A16.alloc(4, T)
            bt = A16.alloc(4, T)
            kt_ = A16.alloc(4, T)
            bh = A16.alloc(4, T)
            kh = A16.alloc(4, T)
            P.act(E, cum, AF.Exp)
            P.tt("dve", rt, xr, E, ALU.mult)
            P.tt("dve", E, cum, ld, ALU.subtract)
            P.act(E, E, AF.Exp)
            P.stt(at, kkn, -1.0, E, ALU.mult, ALU.mult)
            P.act(E, cum, AF.Exp, scale=-1.0)
            P.tt("dve", bt, bb, E, ALU.mult)
            P.tt("pool", kt_, kp, E, ALU.mult)
            Ec = E.rearrange("p f (k c) -> p f k c", c=C)
            P.tt("dve", Ec, cend.to_broadcast([128, 4, NCk, C]), cumc, ALU.subtract)
            P.act(E, E, AF.Exp)
            P.tt("dve", bh, bb, E, ALU.mult)
            P.tt("pool", kh, kp, E, ALU.mult)
            WC = A32.alloc(4, NCk)
            P.act(WC.unsqueeze(3), cend, AF.Exp)
            rkr = A16.alloc(4, T)
            for ft in range(4):
                P.stt(rkr[:, ft, :], xr[:, ft, :], pv[:, 36 + ft:37 + ft], kp[:, ft, :], ALU.mult, ALU.mult)
            Yb = A32.alloc(4, T)
            chk(64)
            mus, mls, mui = musb[C], mlsb[C], muib[C]
            nsteps = {64: 5, 16: 3}[C]
            zmb = zm.unsqueeze(1).unsqueeze(3)
            musB = mus[:].unsqueeze(1).to_broadcast([Z, 4, Z])
            mlsB = mls[:].unsqueeze(1).to_broadcast([Z, 4, Z])
            loop32, loop16 = A32.top, A16.top
            slots = []
            for _i in range(2):
                slots.append(dict(az=A16.alloc(4, 2, C), Tz=A16.alloc(4, Z)[0:Z], Uka=A16.alloc(4, Z)[0:Z],
                                  Ubk=A16.alloc(2, 4, C)[0:Z], vkT=A16.alloc(2, 4, 128), bT=A16.alloc(4, 128)))
            tmpz = [A16.alloc(4, 2, C) for _i in range(5)]
            Az = [A16.alloc(4, Z)[0:Z], A16.alloc(4, Z)[0:Z]]
            ATz = [A16.alloc(4, Z)[0:Z], A16.alloc(4, Z)[0:Z]]
            STb = A16.alloc(4, 128)
            XT = A16.alloc(4, 128)[0:Z]
            SAT = A16.alloc(4, 128)[0:Z]
            pA = psS[0][:, 0:512].rearrange("p (f z) -> p f z", f=4)[0:Z, :, 0:Z]
            pAT = psS[0][:, 512:1024].rearrange("p (f z) -> p f z", f=4)[0:Z, :, 0:Z]
            pU = psS[1][:, 0:512].rearrange("p (f z) -> p f z", f=4)[0:Z, :, 0:Z]
            pU2 = psS[1][:, 512:1024].rearrange("p (a f c) -> p a f c", a=2, f=4)[0:Z, :, :, 0:C]
            pTt = pU
            pB = psB[:].rearrange("p (a f i) -> p a f i", a=2, f=4)

            def zexp_into(dst, src, t0):
                P.tt("dve", dst, src[:, :, t0:t0 + C].unsqueeze(2).to_broadcast([128, 4, 2, C]),
                     zmb.to_broadcast([128, 4, 2, C]), ALU.mult)
                return dst.rearrange("p f a c -> p f (a c)")

            def pre_gen(ck, sl):
                t0 = ck * C
                az = zexp_into(sl["az"], at, t0)
                bz, kz, bhz, khz, vz = [zexp_into(d_, s_, t0) for d_, s_ in zip(tmpz, (bt, kt_, bh, kh, xv))]
                vkT, bT = sl["vkT"], sl["bT"]
                for ft in range(4):
                    P.tr(pB[0:Z, 0, ft, :], vz[:, ft, :], identb[:])
                    P.tr(pB[0:Z, 1, ft, :], khz[:, ft, :], identb[:])
                P.copy("act", vkT[0:Z], pB[0:Z])
                for ft in range(4):
                    P.tr(pB[0:Z, 0, ft, :], bhz[:, ft, :], identb[:])
                P.copy("act", bT[0:Z], pB[0:Z, 0])
                yield
                for ft in range(4):
                    P.mm(pA[:, ft, :], bz[:, ft, :], az[:, ft, :])
                    P.mm(pAT[:, ft, :], az[:, ft, :], bz[:, ft, :])
                    P.mm(pU[:, ft, :], kz[:, ft, :], az[:, ft, :])
                    P.mm(pU2[:, 0, ft, :], bz[:, ft, :], rt[:, ft, t0:t0 + C])
                    P.mm(pU2[:, 1, ft, :], kz[:, ft, :], rt[:, ft, t0:t0 + C])
                Tz, Uka, Ubk = sl["Tz"], sl["Uka"], sl["Ubk"]
                P.tt("dve", Az[0], pA, musB, ALU.mult)
                P.tt("dve", ATz[0], pAT, mlsB, ALU.mult)
                P.tt("dve", Uka, pU, musB, ALU.mult)
                P.tt("dve", Ubk, pU2, mui[:].unsqueeze(1).unsqueeze(1).to_broadcast([Z, 2, 4, C]), ALU.mult)
                P.tt("pool", Tz, Az[0], identb[0:Z, 0:Z].unsqueeze(1).to_broadcast([Z, 4, Z]), ALU.add)
                yield
                cur = 0
                for stp in range(nsteps):
                    last = (stp == nsteps - 1)
                    nxt = cur ^ 1
                    for ft in range(4):
                        if not last:
                            P.mm(pA[:, ft, :], ATz[cur][:, ft, :], Az[cur][:, ft, :])
                        P.mm(pAT[:, ft, :], Az[cur][:, ft, :], ATz[cur][:, ft, :])
                    if not last:
                        P.copy("act", Az[nxt], pA)
                    P.copy("dve", ATz[nxt], pAT)
                    for ft in range(4):
                        P.mm(pTt[:, ft, :], ATz[nxt][:, ft, :], Tz[:, ft, :])
                    P.tt("dve", Tz, Tz, pTt, ALU.add)
                    cur = nxt
                    yield

            def chain_gen(ck, sl, Sfull):
                t0 = ck * C
                az = sl["az"].rearrange("p f a c -> p f (a c)")
                Tz, Uka, Ubk = sl["Tz"], sl["Uka"], sl["Ubk"]
                vT = sl["vkT"][0:Z, 0]
                kT = sl["vkT"][0:Z, 1]
                bT = sl["bT"]
                P.copy("act", STb, Sfull)
                pX = psI[1][:, 0:512].rearrange("p (f i) -> p f i", f=4)[0:Z]
                for ft in range(4):
                    P.mm(pX[:, ft, :], az[:, ft, :], STb[:, ft, :], start=True, stop=False)
                    P.mm(pX[:, ft, :], Uka[:, ft, :], vT[:, ft, :], start=False, stop=True)
                P.copy("act", XT, pX)
                yield
                for ft in range(4):
                    P.mm(pX[:, ft, :], Tz[:, ft, :], XT[:, ft, :])
                P.copy("act", SAT, pX)
                yield
                pY = psI[0][:, 0:4 * C].rearrange("p (f c) -> p f c", f=4)
                for ft in range(4):
                    P.mm(pY[:, ft, :], STb[:, ft, :], rt[:, ft, t0:t0 + C], start=True, stop=False)
                    P.mm(pY[:, ft, :], SAT[:, ft, :], Ubk[:, 0, ft, :], start=False, stop=False)
                    P.mm(pY[:, ft, :], vT[:, ft, :], Ubk[:, 1, ft, :], start=False, stop=True)
                P.copy("dve", Yb[:, :, t0:t0 + C], pY)
                yield
                pS = psA[:].rearrange("p (f i) -> p f i", f=4)
                for ft in range(4):
                    P.mm(pS[:, ft, :], bT[0:Z, ft, :], SAT[:, ft, :], start=True, stop=False)
                    P.mm(pS[:, ft, :], kT[:, ft, :], vT[:, ft, :], start=False, stop=True)
                P.tt("dve", Sfull, Sfull, WC[:, :, ck:ck + 1].to_broadcast([128, 4, 128]), ALU.mult)
                P.tt("dve", Sfull, Sfull, pS, ALU.add)
                yield

            chunks = [(ck, Sfull) for (c0, ncol, Sfull, sc) in segs for ck in range(c0 // C, (c0 + ncol) // C)]
            for _ in pre_gen(chunks[0][0], slots[0]):
                pass
            for j, (ck, Sfull) in enumerate(chunks):
                g1 = pre_gen(chunks[j + 1][0], slots[(j + 1) % 2]) if j + 1 < len(chunks) else iter(())
                g2 = chain_gen(ck, slots[j % 2], Sfull)
                d1 = d2 = False
                while not (d1 and d2):
                    if not d1:
                        d1 = next(g1, "end") == "end"
                    if not d2:
                        d2 = next(g2, "end") == "end"
            A32.reset(loop32)
            A16.reset(loop16)
            chk(66)
            ybf = A16.alloc(4, T)
            P.copy("act", ybf, Yb)
            pM = psS[0][:].rearrange("p (f t) -> p f t", f=4)[:, :, 0:T] if T == 256 else \
                psS[0][:, 0:4 * T].rearrange("p (f t) -> p f t", f=4)
            pV = psS[1][:].rearrange("p (f t) -> p f t", f=4)[:, :, 0:T] if T == 256 else \
                psS[1][:, 0:4 * T].rearrange("p (f t) -> p f t", f=4)
            for ft in range(4):
                P.mm(pM[:, ft, :], obb[:], ybf[:, ft, :])
            ym = A32.alloc(4, T)
            P.stt(ym, pM, -1.0 / 64, Yb, ALU.mult, ALU.add)
            P.act(ybf, ym, AF.Square)
            for ft in range(4):
                P.mm(pV[:, ft, :], obb[:], ybf[:, ft, :])
            sd = A32.alloc(4, T)
            P.ts("dve", sd, pV, 1.0 / 64, ALU.mult, GN_EPS, ALU.add)
            P.act(sd, sd, AF.Sqrt)
            P.recip(sd, sd)
            P.tt("dve", ym, ym, sd, ALU.mult)
            for ft in range(4):
                P.ts("dve", ym[:, ft, :], ym[:, ft, :], pv[:, 40 + ft:41 + ft], ALU.mult, pv[:, 44 + ft:45 + ft], ALU.add)
            for ft in range(4):
                P.mm(pM[:, ft, :], obb[:], rkr[:, ft, :])
            P.tt("dve", sd, pM, xv, ALU.mult)
            P.tt("dve", ym, ym, sd, ALU.add)
            P.tt("dve", mixT[:, 12:16, 0:T], ym, gc, ALU.mult)

        def out_phase(l, T, xsrc, ydst):
            top32 = A32.top
            nsub = (T + 127) // 128
            lg = A32.alloc(2, D)
            P.dma("sp", lg, I["lngb"][l])
            zs = []
            for si in range(nsub):
                n = min(128, T - si * 128)
                z = A32.alloc(D)
                P.dma("sp", z[0:n, :], xsrc[si * 128:si * 128 + n, :])
                zs.append((z, n))
            for c in range(NCH_OUT):
                wb = load_w(l, NCH_IN + c)
                for si, (z, n) in enumerate(zs):
                    ps = psI[pin[0]]
                    pin[0] ^= 1
                    for kt in range(16):
                        P.mm(ps[0:n, 0:CW], mixT[:, kt, si * 128:si * 128 + n], wb[:, kt, :],
                             start=(kt == 0), stop=(kt == 15))
                    P.stt(z[0:n, c * CW:(c + 1) * CW], z[0:n, c * CW:(c + 1) * CW], ALPHA, ps[0:n, 0:CW],
                          ALU.mult, ALU.add)
            for si, (z, n) in enumerate(zs):
                st = A32.alloc(4, 6)
                mv = A32.alloc(2)
                zc = z.rearrange("p (a b) -> p a b", a=4)
                for a in range(4):
                    P.op("dve", (lambda e, o=st[0:n, a, :], i=zc[0:n, a, :]: e.bn_stats(out=o, in_=i)),
                         [zc[0:n, a, :]], [st[0:n, a, :]])
                stf = st.rearrange("p a b -> p (a b)")
                P.op("dve", (lambda e, o=mv[0:n, :], i=stf[0:n, :]: e.bn_aggr(out=o, in_=i)), [stf[0:n, :]], [mv[0:n, :]])
                rs = A32.alloc(1)
                P.ts("dve", rs[0:n, :], mv[0:n, 1:2], LN_EPS, ALU.add)
                P.act(rs[0:n, :], rs[0:n, :], AF.Sqrt)
                P.recip(rs[0:n, :], rs[0:n, :])
                P.ts("dve", z[0:n, :], z[0:n, :], mv[0:n, 0:1], ALU.subtract, rs[0:n, :], ALU.mult)
                P.tt("pool", z[0:n, :], z[0:n, :], lg[0:n, 0, :], ALU.mult)
                P.tt("dve", z[0:n, :], z[0:n, :], lg[0:n, 1, :], ALU.add)
                P.dma("sp", ydst[si * 128:si * 128 + n, :], z[0:n, :])
            A32.reset(top32)

        try:
            chk(1)
            for l in range(nlayers):
                layer_setup(l)
                chk(2)
                last = (l == nlayers - 1)
                if do_prompt:
                    xsrc = I["xp"] if l == 0 else y0p
                    ydst = O["y_p"] if last else y0p
                    P.memset("pool", KT[:], 0.0)
                    P.memset("pool", VT[:], 0.0)
                    P.memset("dve", hcar[:], 0.0)
                    P.memset("dve", Sst[:], 0.0)
                    P.memset("dve", shc[:], 0.0)
                    for ti in range(NT):
                        A32.reset()
                        A16.reset()
                        ts_ = ti * TT
                        load_xT(xsrc[ts_:ts_ + TT, :], TT)
                        chk(101)
                        ssm_phase(l, TT, [(0, TT, hcar)], None)
                        chk(102)
                        A32.reset()
                        A16.reset()
                        units = []
                        for m in range(TT // 128):
                            a0 = 128 * m
                            units.append(dict(nq=128, ncol=640, maskB=True, vblk0=m, tq0=128 * m,
                                              inv=max(0, min(640, 512 - ts_ - a0)),
                                              halves=[(0, 64, 128 * m, a0, 0, 576),
                                                      (64, 64, 128 * m + 64, a0 + 64, 64, 640)]))
                        kv_out = None
                        if ts_ >= seq - 512:
                            r0 = ts_ - (seq - 512)
                            kv_out = (lambda t0, n, r0=r0: O["p_k"][l, r0 + t0:r0 + t0 + n, :],
                                      lambda t0, n, r0=r0: O["p_v"][l, r0 + t0:r0 + t0 + n, :], True)
                            import os as _os
                            if _os.environ.get("DBGKV") == "k":
                                kv_out = (kv_out[0], None, True)
                            if _os.environ.get("DBGKV") == "v":
                                kv_out = (None, kv_out[1], True)
                        else:
                            kv_out = (None, None, True)
                        att_phase(l, TT, units, 512, 512, kv_out)
                        chk(103)
                        P.copy("pool", KT[:, :, 0:256], KT[:, :, 256:512])
                        P.copy("pool", KT[:, :, 256:512], KT[:, :, 512:768])
                        P.copy("pool", VT[:, 0:2, :], VT[:, 2:4, :])
                        P.copy("pool", VT[:, 2:4, :], VT[:, 4:6, :])
                        A32.reset()
                        A16.reset()
                        chk(104)
                        rwkv_phase(l, TT, 64, [(0, TT, Sst[:], shc)], [O["p_shift"][l]] if ti == NT - 1 else None)
                        chk(105)
                        A32.reset()
                        A16.reset()
                        out_phase(l, TT, xsrc[ts_:ts_ + TT, :], ydst[ts_:ts_ + TT, :])
                    P.dma("sp", O["p_ssm"][l], hcar[:].rearrange("p a b -> p (a b)"))
                    P.dma("sp", O["p_rwkv"][l], Sst[:].rearrange("p a b -> p (a b)"))
                if do_sample:
                    T = NS * SQ
                    xsrc = I["xs"] if l == 0 else y0s
                    ydst = O["y_s"] if last else y0s
                    A32.reset()
                    A16.reset()
                    hc = [hcar, hcar1]
                    Ss = [Sst, Sst1]
                    sh = [shc, shc1]
                    for s in range(NS):
                        P.dma("sp", hc[s][:].rearrange("p a b -> p (a b)"), I["hss"][l, s])
                        P.dma("sp", Ss[s][:].rearrange("p a b -> p (a b)"), I["srw"][l, s])
                        P.dma("sp", sh[s][:], I["ssh"][l, s])
                    load_xT(xsrc, T)
                    chk(3)
                    ssm_phase(l, T, [(s * SQ, SQ, hc[s]) for s in range(NS)], None)
                    chk(4)
                    A32.reset()
                    A16.reset()
                    kv_out = (lambda t0, n: O["s_k"][l, t0:t0 + n, :], lambda t0, n: O["s_v"][l, t0:t0 + n, :], False)
                    att_in = att_phase(l, T, None, 640, 0, kv_out)
                    chk(5)
                    for s in range(NS):
                        for blk in range(4):
                            top = A32.top
                            cs_ = A32.alloc(1024)
                            P.dma("sp", cs_, I["ck"][l, s, blk * 128:(blk + 1) * 128, :])
                            for g in range(2):
                                for j in range(4):
                                    kt = g * 4 + j
                                    P.tr(psA[:, j * 128:(j + 1) * 128], cs_[:, kt * 128:(kt + 1) * 128], identf)
                                P.copy(ev_eng(), KT[:, g * 4:(g + 1) * 4, blk * 128:(blk + 1) * 128],
                                       psA[:].rearrange("p (a b) -> p a b", a=4))
                            A32.reset(top)
                        P.dma("pool", VT[:, 0:4, :], I["cv"][l, s].rearrange("(b p) c -> p b c", p=128))
                        P.copy("pool", KT[:, :, 512:512 + SQ], KT[:, :, 640 + s * SQ:640 + (s + 1) * SQ])
                        P.dma("pool", VT[0:SQ, 4, :], O["s_v"][l, s * SQ:(s + 1) * SQ, :])
                        att_in([dict(nq=SQ, ncol=512 + SQ, maskB=False, vblk0=0, tq0=s * SQ, inv=0,
                                     halves=[(0, SQ, s * SQ, 0, 0, 512 + SQ)])])
                    A32.reset()
                    A16.reset()
                    chk(6)
                    rwkv_phase(l, T, 16, [(s * SQ, SQ, Ss[s][:], sh[s]) for s in range(NS)], [O["s_shift"][l, s] for s in range(NS)])
                    chk(7)
                    A32.reset()
                    A16.reset()
                    out_phase(l, T, xsrc, ydst)
                    for s in range(NS):
                        P.dma("sp", O["s_ssm"][l, s], hc[s][:].rearrange("p a b -> p (a b)"))
                        P.dma("sp", O["s_rwkv"][l, s], Ss[s][:].rearrange("p a b -> p (a b)"))
        except _Stop:
            if do_prompt and stage >= 100:
                P.dma("sp", O["p_ssm"][l], hcar[:].rearrange("p a b -> p (a b)"))
                P.dma("sp", O["p_rwkv"][l], Sst[:].rearrange("p a b -> p (a b)"))
            if do_sample and 3 <= stage < 100:
                for s in range(NS):
                    P.dma("sp", O["s_ssm"][l, s], [hcar, hcar1][s][:].rearrange("p a b -> p (a b)"))
                    P.dma("sp", O["s_rwkv"][l, s], [Sst, Sst1][s][:].rearrange("p a b -> p (a b)"))
        P.fence("sp", list(O.values()))
        P.emit()
        stats = dict(P.stats)
        stats["A32"] = A32.hi
        stats["A16"] = A16.hi
    return nc, stats


def _assemble(results):
    f32 = np.float32
    y_p = np.stack([results[c]["y_p"] for c in range(2)]).astype(f32)
    y_s = np.concatenate([results[c]["y_s"].reshape(NS, SQ, D) for c in range(NCORE)]).astype(f32)
    p_k = np.stack([results[c]["p_k"] for c in range(2)], 1).reshape(NL, 2, 512, 16, 64)
    p_v = np.stack([results[c]["p_v"] for c in range(2)], 1).reshape(NL, 2, 512, 16, 64)

    def ssm_unpack(a):
        a = a.reshape(a.shape[:-2] + (2, 64, 16, 2))
        a = np.moveaxis(a, -2, -4)
        a = a.reshape(a.shape[:-4] + (32, 64, 2))
        return np.ascontiguousarray(a[..., 0]), np.ascontiguousarray(a[..., 1])

    def rwkv_unpack(a):
        a = a.reshape(a.shape[:-2] + (2, 64, 4, 2, 64))
        outs = []
        for ft in range(4):
            for h2 in range(2):
                blk = a[..., h2, :, ft, h2, :]
                outs.append(np.swapaxes(blk, -1, -2))
        return np.ascontiguousarray(np.stack(outs, -3))

    def shift_unpack(a):
        return np.ascontiguousarray(a.reshape(a.shape[:-2] + (2048,)))

    pss = np.stack([results[c]["p_ssm"] for c in range(2)], 1)
    p_re, p_im = ssm_unpack(pss)
    p_rw = rwkv_unpack(np.stack([results[c]["p_rwkv"] for c in range(2)], 1))
    p_sh = shift_unpack(np.stack([results[c]["p_shift"] for c in range(2)], 1))
    s_k = np.concatenate([results[c]["s_k"].reshape(NL, NS, SQ, 16, 64) for c in range(NCORE)], 1)
    s_v = np.concatenate([results[c]["s_v"].reshape(NL, NS, SQ, 16, 64) for c in range(NCORE)], 1)
    sss = np.concatenate([results[c]["s_ssm"] for c in range(NCORE)], 1)
    s_re, s_im = ssm_unpack(sss)
    s_rw = rwkv_unpack(np.concatenate([results[c]["s_rwkv"] for c in range(NCORE)], 1))
    s_sh = shift_unpack(np.concatenate([results[c]["s_shift"] for c in range(NCORE)], 1))
    outs = (y_p, y_s, p_k, p_v, p_re, p_im, p_rw, p_sh, s_k, s_v, s_re, s_im, s_rw, s_sh)
    return tuple(np.ascontiguousarray(o, dtype=f32) for o in outs)


def kernel(**inputs):
    shared = _shared_layouts(inputs)
    in_maps = [_core_inputs(inputs, c, shared) for c in range(NCORE)]
    nc, _ = build_program()
    res = run_bass_kernel_spmd(nc, in_maps, core_ids=list(range(NCORE)))
    return _assemble(res.results)
```

```python
import numpy as np
from concourse.bass_utils import run_bass_kernel_spmd
import concourse.bass as bass
import concourse.mybir as mybir

F32 = mybir.dt.float32
BF16 = mybir.dt.bfloat16
ALU = mybir.AluOpType
AF = mybir.ActivationFunctionType
AX = mybir.AxisListType

ENGS = ("pe", "act", "dve", "pool", "sp")


def _region(ap):
    t = ap.tensor
    pat = ap.ap
    off = int(ap.offset)
    if isinstance(t, bass.DRamTensorHandle):
        ext = 1
        for st, cn in pat:
            ext += (cn - 1) * abs(st)
        return (t.name, 0, 1, off, off + ext)
    row = pat[0][0]
    if row == 0:
        row = 1 << 40
    p0 = off // row
    f0 = off % row
    ext = 1
    for st, cn in pat[1:]:
        ext += (cn - 1) * abs(st)
    return (t.name, p0, p0 + pat[0][1], f0, f0 + ext)


class Op:
    __slots__ = ("eng", "fn", "dma", "deps", "pos", "signal", "sigval", "vc", "dk",
                 "waits", "sem", "semval", "idx", "raw")


class Prog:
    def __init__(self, nc, n_dma_sems=16, same_engine_sync=True):
        self.nc = nc
        self.ops = []
        self.acc = {}
        self.n_dma_sems = n_dma_sems
        self.same_engine_sync = same_engine_sync

    def op(self, eng, fn, reads=(), writes=(), dma=False):
        o = Op()
        o.eng = eng
        o.fn = fn
        o.dma = dma
        o.idx = len(self.ops)
        o.signal = False
        o.raw = set()
        deps = set()
        for ap in reads:
            self._access(o, ap, False, deps)
        for ap in writes:
            self._access(o, ap, True, deps)
        deps.discard(o.idx)
        o.deps = deps
        self.ops.append(o)
        return o

    def _access(self, o, ap, is_w, deps):
        name, p0, p1, f0, f1 = _region(ap)
        lst = self.acc.setdefault(name, [])
        keep = []
        psum = name.startswith("pp_")
        if psum:
            esz = mybir.dt.size(ap.dtype)
            b0, b1 = (f0 * esz) // 2048, ((f1 * esz) - 1) // 2048
            f0, f1 = (b0 * 2048) // esz, ((b1 + 1) * 2048) // esz
            p0, p1 = 0, 128
        for e in lst:
            ov = not (e[3] <= p0 or p1 <= e[2] or e[5] <= f0 or f1 <= e[4])
            if ov and (is_w or e[1]):
                deps.add(e[0])
                if (not is_w) and e[1]:
                    o.raw.add(e[0])
            if psum and (not is_w) and (not e[1]) and e[6] != o.eng and e[0] != o.idx:
                if not (e[8] < b0 or b1 < e[7]):
                    deps.add(e[0])
            if e[0] == o.idx:
                keep.append(e)
                continue
            contained = (p0 <= e[2] and e[3] <= p1 and f0 <= e[4] and e[5] <= f1)
            if contained and is_w:
                continue
            if contained and (not is_w) and (not e[1]) and e[6] == o.eng and not o.dma \
                    and not self.ops[e[0]].dma:
                continue
            keep.append(e)
        keep.append([o.idx, is_w, p0, p1, f0, f1, o.eng] + ([b0, b1] if psum else [0, 0]))
        self.acc[name] = keep

    def emit(self):
        nc = self.nc
        ops = self.ops
        known_vc = {e: {x: 0 for x in ENGS} for e in ENGS}
        known_dk = {e: {} for e in ENGS}
        count = {e: 0 for e in ENGS}
        dma_n = {e: 0 for e in ENGS}
        dma_last = {}
        by_pos = {e: [] for e in ENGS}
        for o in ops:
            E = o.eng
            kv = known_vc[E]
            kd = known_dk[E]
            waits = {}
            for di in sorted(o.deps):
                d = ops[di]
                if d.dma:
                    key = d.sem
                    if kd.get(key, 0) >= d.semval:
                        continue
                    waits[("d",) + key] = max(waits.get(("d",) + key, 0), d.semval)
                    d.signal = True
                    kd[key] = d.semval
                else:
                    Ed = d.eng
                    if kv[Ed] >= d.pos:
                        continue
                    if Ed == E and (E == "pe" or not self.same_engine_sync):
                        continue
                    waits[("c", Ed)] = max(waits.get(("c", Ed), 0), d.pos)
                    kv[Ed] = d.pos
                for x in ENGS:
                    if d.vc[x] > kv[x]:
                        kv[x] = d.vc[x]
                for k2, v2 in d.dk.items():
                    if kd.get(k2, 0) < v2:
                        kd[k2] = v2
            if o.dma:
                n = dma_n[E]
                dma_n[E] += 1
                key = (E, n % self.n_dma_sems)
                prev = dma_last.get(key)
                if prev is not None and kd.get(key, 0) < prev.semval:
                    waits[("d",) + key] = max(waits.get(("d",) + key, 0), prev.semval)
                    kd[key] = prev.semval
                    prev.signal = True
                o.sem = key
                o.semval = 16 * (n // self.n_dma_sems + 1)
                dma_last[key] = o
                o.pos = 0
                o.vc = dict(kv)
                o.dk = dict(kd)
            else:
                count[E] += 1
                o.pos = count[E]
                o.vc = dict(kv)
                o.vc[E] = o.pos
                o.dk = dict(kd)
                by_pos[E].append(o)
            o.waits = waits
        for o in ops:
            for k, v in o.waits.items():
                if k[0] == "c":
                    by_pos[k[1]][v - 1].signal = True
        for E in ENGS:
            s = 0
            for o in by_pos[E]:
                if o.signal:
                    s += 1
                o.sigval = s
        self.stats = {e: count[e] for e in ENGS}
        self.stats["dma"] = dict(dma_n)
        self.stats["waits"] = sum(len(o.waits) for o in ops)

        import contextlib
        with contextlib.ExitStack() as st:
            csem = {e: st.enter_context(nc.semaphore("c_" + e)) for e in ENGS}
            dsem = {}
            for e in ENGS:
                if dma_n[e]:
                    for i in range(min(self.n_dma_sems, dma_n[e])):
                        dsem[(e, i)] = st.enter_context(nc.semaphore("d_%s_%d" % (e, i)))
            block = st.enter_context(nc.Block())

            def run(E, eng):
                for o in ops:
                    if o.eng != E:
                        continue
                    for k, v in o.waits.items():
                        if k[0] == "c":
                            eng.wait_ge(csem[k[1]], by_pos[k[1]][v - 1].sigval)
                        else:
                            eng.wait_ge(dsem[(k[1], k[2])], v)
                    if o.fn is None:
                        continue
                    ins = o.fn(eng)
                    if o.dma:
                        ins.then_inc(dsem[o.sem], 16)
                    elif o.signal:
                        ins.then_inc(csem[E], 1)

            @block.tensor
            def _(eng):
                run("pe", eng)

            @block.scalar
            def _(eng):
                run("act", eng)

            @block.vector
            def _(eng):
                run("dve", eng)

            @block.gpsimd
            def _(eng):
                run("pool", eng)

            @block.sync
            def _(eng):
                run("sp", eng)

    def dma(self, q, out, in_, **kw):
        return self.op(q, lambda e: e.dma_start(out=out, in_=in_, **kw), [in_], [out], dma=True)

    def mm(self, out, lhsT, rhs, start=True, stop=True, **kw):
        return self.op("pe", lambda e: e.matmul(out, lhsT=lhsT, rhs=rhs, start=start, stop=stop, **kw),
                       [lhsT, rhs], [out])

    def tr(self, out, in_, ident):
        return self.op("pe", lambda e: e.transpose(out, in_, ident), [in_, ident], [out])

    def act(self, out, in_, func, bias=None, scale=1.0, accum_out=None, eng="act"):
        rd = [in_]
        wr = [out]
        kw = {}
        if bias is not None:
            kw["bias"] = bias
            if not isinstance(bias, (int, float)):
                rd.append(bias)
        if not isinstance(scale, (int, float)):
            rd.append(scale)
        if accum_out is not None:
            kw["accum_out"] = accum_out
            wr.append(accum_out)
        return self.op(eng, lambda e: e.activation(out=out, in_=in_, func=func, scale=scale, **kw), rd, wr)

    def tt(self, eng, out, in0, in1, op):
        return self.op(eng, lambda e: e.tensor_tensor(out=out, in0=in0, in1=in1, op=op), [in0, in1], [out])

    def ts(self, eng, out, in0, s1, op0, s2=None, op1=None, accum_out=None):
        rd = [in0]
        wr = [out]
        if not isinstance(s1, (int, float)):
            rd.append(s1)
        if s2 is not None and not isinstance(s2, (int, float)):
            rd.append(s2)
        kw = {}
        if op1 is not None:
            kw["op1"] = op1
        if accum_out is not None:
            kw["accum_out"] = accum_out
            wr.append(accum_out)
        return self.op(eng, lambda e: e.tensor_scalar(out=out, in0=in0, scalar1=s1, scalar2=s2, op0=op0, **kw),
                       rd, wr)

    def stt(self, out, in0, scalar, in1, op0, op1, eng="dve"):
        rd = [in0, in1]
        if not isinstance(scalar, (int, float)):
            rd.append(scalar)
        return self.op(eng, lambda e: e.scalar_tensor_tensor(out=out, in0=in0, scalar=scalar, in1=in1,
                                                             op0=op0, op1=op1), rd, [out])

    def scan(self, out, data0, data1, initial, op0=None, op1=None):
        rd = [data0, data1]
        if not isinstance(initial, (int, float)):
            rd.append(initial)
        op0 = op0 or ALU.mult
        op1 = op1 or ALU.add
        return self.op("dve", lambda e: e.tensor_tensor_scan(out=out, data0=data0, data1=data1,
                                                             initial=initial, op0=op0, op1=op1), rd, [out])

    def copy(self, eng, out, in_):
        if eng == "act":
            return self.act(out, in_, AF.Copy)
        return self.op(eng, lambda e: e.tensor_copy(out=out, in_=in_), [in_], [out])

    def memset(self, eng, out, val):
        return self.op(eng, lambda e: e.memset(out, val), [], [out])

    def recip(self, out, in_):
        return self.op("dve", lambda e: e.reciprocal(out=out, in_=in_), [in_], [out])

    def reduce(self, out, in_, op, axis=None, eng="dve"):
        axis = axis or AX.X
        return self.op(eng, lambda e: e.tensor_reduce(out=out, in_=in_, axis=axis, op=op), [in_], [out])

    def fence(self, q, reads):
        return self.op(q, None, [], list(reads))

import math

D = 2048
DIN = 7680
NL = 2
SEQ = 4096
TT = 256
NS = 2
SQ = 16
NCORE = 8
ALPHA = (2.0 * NL) ** 0.25
KAPPA = math.exp(-0.5)
NEG = -30000.0
CW = 256
NCH_IN = DIN // CW
NCH_OUT = D // CW
NCH = NCH_IN + NCH_OUT
GN_EPS = 64e-5
LN_EPS = 1e-5
NPV = 56
DBG_NOSH = False

_CST = {}
_off = 0
for _n, _w in [("identf", 128), ("mus128", 128), ("mls128", 128), ("mui128", 64),
               ("mus32", 32), ("mls32", 32), ("mui32", 16), ("cmask64", TT), ("cmask16", NS * SQ),
               ("zm", 2), ("ob", 128), ("bneg", 256), ("maskB", 64), ("bmask", 8)]:
    _CST[_n] = (_off, _w)
    _off += _w
NCST = _off


def _vec4(v):
    return np.ascontiguousarray(np.asarray(v, np.float32).reshape(4, 128).T)


def _consts():
    c = np.zeros((128, NCST), np.float32)

    def put(name, arr):
        o, w = _CST[name]
        c[:arr.shape[0], o:o + w] = arr

    put("identf", np.eye(128, dtype=np.float32))
    for C, sfx in ((64, "128"), (16, "32")):
        Z = 2 * C
        p = np.arange(Z)[:, None]
        q = np.arange(Z)[None, :]
        same = (p // C) == (q // C)
        put("mus" + sfx, (same & ((p % C) < (q % C))).astype(np.float32))
        put("mls" + sfx, (same & ((p % C) > (q % C))).astype(np.float32))
        t = np.arange(C)[None, :]
        put("mui" + sfx, ((p % C) <= t).astype(np.float32))
    cm = np.ones((128, TT), np.float32)
    cm[:, ::64] = 0.0
    put("cmask64", cm)
    cm = np.ones((128, NS * SQ), np.float32)
    cm[:, ::16] = 0.0
    put("cmask16", cm)
    pp = np.arange(128)
    put("zm", np.stack([(pp < 64), (pp >= 64)], 1).astype(np.float32))
    put("ob", ((pp[:, None] // 64) == (pp[None, :] // 64)).astype(np.float32))
    bn = np.zeros((128, 256), np.float32)
    bn[:64, 192:] = NEG
    put("bneg", bn)
    mb = np.zeros((128, 64), np.float32)
    mb[64:, :] = NEG
    put("maskB", mb)
    gl = pp // 16
    bm = np.zeros((128, 4, 2), np.float32)
    for pm in range(4):
        for g2 in range(2):
            bm[:, pm, g2] = (gl == 2 * pm + g2)
    put("bmask", bm.reshape(128, 8))
    return c


def _shared_layouts(inp):
    f = lambda k: np.asarray(inp[k], np.float32)
    out = {}
    pv = np.zeros((NL, 128, NPV), np.float32)
    for l in range(NL):
        for i in range(5):
            pv[l, :, i * 4:(i + 1) * 4] = _vec4(f("rwkv_mu")[l, i])
        for j, k in enumerate(["rwkv_w0", "rwkv_a0", "rwkv_k_k", "rwkv_k_a", "rwkv_r_k", "rwkv_lnx_g",
                               "rwkv_lnx_b", "ssm_d", "ssm_b_glu"]):
            pv[l, :, 20 + 4 * j:24 + 4 * j] = _vec4(f(k)[l].reshape(-1))
    out["pv"] = pv
    def pair(a):
        return np.ascontiguousarray(a.reshape(16, 2, 64).transpose(1, 2, 0).reshape(128, 16))
    sp = np.zeros((NL, 128, 3, 16), np.float32)
    sc = np.zeros((NL, 128, 5, 4, 64), np.float32)
    cx = np.zeros((NL, 2, 128, 16, 128), np.float32)
    for l in range(NL):
        lr, li, ld = f("ssm_lam_re")[l], f("ssm_lam_im")[l], f("ssm_log_dt")[l]
        sp[l, :, 0] = pair(lr)
        sp[l, :, 1] = pair(li)
        sp[l, :, 2] = pair(np.repeat(ld[:, None], 64, 1))
        def ch(a):
            return np.repeat(a.reshape(4, 8, 1, 64), 16, 2).transpose(1, 2, 0, 3).reshape(128, 4, 64)
        sc[l, :, 0] = ch(lr)
        sc[l, :, 1] = ch(li)
        sc[l, :, 2] = ch(np.repeat(ld[:, None], 64, 1))
        for j, k in ((3, "ssm_b_re"), (4, "ssm_b_im")):
            b = f(k)[l]
            sc[l, :, j] = b.reshape(4, 8, 64, 16).transpose(1, 3, 0, 2).reshape(128, 4, 64)
        for j, k in ((0, "ssm_c_re"), (1, "ssm_c_im")):
            cc = f(k)[l]
            for pi in range(16):
                for g2 in range(2):
                    g = 2 * pi + g2
                    glo = g % 8
                    cx[l, j, g2 * 64:(g2 + 1) * 64, pi, glo * 16:(glo + 1) * 16] = cc[g].T
    out["ssm_pair"] = sp
    out["ssm_ch"] = sc.reshape(NL, 128, 5, 256)
    out["cext"] = cx.reshape(NL, 2, 128, 2048)
    tab = f("att_rel_bias")
    i = np.arange(128)[:, None]
    jj = np.arange(256)[None, :]
    idxA = np.minimum(256 + i - jj, 256)
    idxB = np.minimum(320 + (i - 64) - jj, 256)
    idx = np.where(i < 64, idxA, idxB)
    idx = np.clip(idx, 0, 256)
    braw = tab[:, :, idx]
    out["braw"] = np.ascontiguousarray(braw.transpose(0, 2, 1, 3))
    out["bconst"] = np.ascontiguousarray(np.repeat(tab[:, None, :, 256], 128, 1))
    lg = np.zeros((NL, 128, 2, D), np.float32)
    lg[:, :, 0, :] = f("ln_g")[:, None, :]
    lg[:, :, 1, :] = f("ln_b")[:, None, :]
    out["lngb"] = lg
    out["cst"] = _consts()
    for k in ("w_in", "w_out", "ssm_w_glu", "rwkv_w1", "rwkv_w2", "rwkv_a1", "rwkv_a2"):
        out[k] = np.ascontiguousarray(f(k))
    return out


def _core_inputs(inp, core, shared):
    f = lambda k: np.asarray(inp[k], np.float32)
    m = dict(shared)
    if core < 2:
        m["xp"] = np.ascontiguousarray(f("x_prompt")[core])
    else:
        m["xp"] = np.zeros((SEQ, D), np.float32)
    s0 = NS * core
    m["xs"] = np.ascontiguousarray(f("x_sample")[s0:s0 + NS].reshape(NS * SQ, D))
    m["ck"] = np.ascontiguousarray(f("cache_att_k")[:, s0:s0 + NS].reshape(NL, NS, 512, 1024))
    m["cv"] = np.ascontiguousarray(f("cache_att_v")[:, s0:s0 + NS].reshape(NL, NS, 512, 1024))
    hs = np.zeros((NL, NS, 128, 16, 2), np.float32)
    for j, k in ((0, "state_ssm_re"), (1, "state_ssm_im")):
        a = f(k)[:, s0:s0 + NS]
        hs[..., j] = a.reshape(NL, NS, 16, 2, 64).transpose(0, 1, 3, 4, 2).reshape(NL, NS, 128, 16)
    m["hss"] = hs.reshape(NL, NS, 128, 32)
    sr = f("state_rwkv")[:, s0:s0 + NS]
    z = np.zeros((NL, NS, 2, 64, 4, 2, 64), np.float32)
    for ft in range(4):
        for h2 in range(2):
            z[:, :, h2, :, ft, h2, :] = sr[:, :, 2 * ft + h2].transpose(0, 1, 3, 2)
    m["srw"] = z.reshape(NL, NS, 128, 512)
    sh = f("state_rwkv_shift")[:, s0:s0 + NS]
    m["ssh"] = np.ascontiguousarray(sh.reshape(NL, NS, 16, 128).transpose(0, 1, 3, 2))
    return m

class Arena:
    def __init__(self, t, size):
        self.t = t
        self.size = size
        self.top = 0
        self.hi = 0

    def alloc(self, *shape, parts=128):
        n = 1
        for s in shape:
            n *= s
        off = self.top
        self.top += n
        self.hi = max(self.hi, self.top)
        assert self.top <= self.size, ("arena overflow", self.top, self.size)
        ap = self.t[0:parts, off:off + n]
        if len(shape) == 2:
            ap = ap.rearrange("p (a b) -> p a b", a=shape[0])
        elif len(shape) == 3:
            ap = ap.rearrange("p (a b c) -> p a b c", a=shape[0], b=shape[1])
        elif len(shape) == 4:
            ap = ap.rearrange("p (a b c d) -> p a b c d", a=shape[0], b=shape[1], c=shape[2])
        return ap

    def reset(self, top=0):
        self.top = top


def cview(cst, name, parts=128, w=None):
    o, ww = _CST[name]
    return cst[0:parts, o:o + (w or ww)]


class _Stop(Exception):
    pass


def build_program(seq=SEQ, do_prompt=True, do_sample=True, nlayers=NL, stage=99, ntiles=None):
    import contextlib
    nc = bass.Bass("TRN2", target_bir_lowering=False)
    NT = seq // TT if ntiles is None else ntiles

    def din(name, shape, dt=F32):
        return nc.dram_tensor(name, list(shape), dt, kind="ExternalInput").ap()

    def dout(name, shape, dt=F32):
        return nc.dram_tensor(name, list(shape), dt, kind="ExternalOutput").ap()

    I = {}
    I["xp"] = din("xp", [SEQ, D])
    I["xs"] = din("xs", [NS * SQ, D])
    I["ck"] = din("ck", [NL, NS, 512, 1024])
    I["cv"] = din("cv", [NL, NS, 512, 1024])
    I["hss"] = din("hss", [NL, NS, 128, 32])
    I["srw"] = din("srw", [NL, NS, 128, 512])
    I["ssh"] = din("ssh", [NL, NS, 128, 16])
    I["w_in"] = din("w_in", [NL, D, DIN])
    I["w_out"] = din("w_out", [NL, D, D])
    I["pv"] = din("pv", [NL, 128, NPV])
    I["ssm_pair"] = din("ssm_pair", [NL, 128, 3, 16])
    I["ssm_ch"] = din("ssm_ch", [NL, 128, 5, 256])
    I["cext"] = din("cext", [NL, 2, 128, 2048])
    I["ssm_w_glu"] = din("ssm_w_glu", [NL, 512, 512])
    I["braw"] = din("braw", [NL, 128, 16, 256])
    I["bconst"] = din("bconst", [NL, 128, 16])
    I["rwkv_w1"] = din("rwkv_w1", [NL, 512, 64])
    I["rwkv_w2"] = din("rwkv_w2", [NL, 64, 512])
    I["rwkv_a1"] = din("rwkv_a1", [NL, 512, 64])
    I["rwkv_a2"] = din("rwkv_a2", [NL, 64, 512])
    I["lngb"] = din("lngb", [NL, 128, 2, D])
    I["cst"] = din("cst", [128, NCST])

    O = {}
    O["y_p"] = dout("y_p", [SEQ, D])
    O["y_s"] = dout("y_s", [NS * SQ, D])
    O["p_k"] = dout("p_k", [NL, 512, 1024])
    O["p_v"] = dout("p_v", [NL, 512, 1024])
    O["p_ssm"] = dout("p_ssm", [NL, 128, 32])
    O["p_rwkv"] = dout("p_rwkv", [NL, 128, 512])
    O["p_shift"] = dout("p_shift", [NL, 1, 2048])
    O["s_k"] = dout("s_k", [NL, NS * SQ, 1024])
    O["s_v"] = dout("s_v", [NL, NS * SQ, 1024])
    O["s_ssm"] = dout("s_ssm", [NL, NS, 128, 32])
    O["s_rwkv"] = dout("s_rwkv", [NL, NS, 128, 512])
    O["s_shift"] = dout("s_shift", [NL, NS, 1, 2048])
    wsc = nc.dram_tensor("wsc", [NL, NCH, 128, 16 * CW], BF16, kind="Internal").ap()
    y0p = nc.dram_tensor("y0p", [SEQ, D], F32, kind="Internal").ap()
    y0s = nc.dram_tensor("y0s", [NS * SQ, D], F32, kind="Internal").ap()

    ST = contextlib.ExitStack()

    def sb(name, shape, dt=F32):
        return ST.enter_context(nc.sbuf_tensor("sb_" + name, list(shape), dt))

    def pst(name, shape, dt=F32):
        return ST.enter_context(nc.psum_tensor("pp_" + name, list(shape), dt))

    with ST:
        P = Prog(nc)
        cst = sb("cst", [128, NCST])
        identb = sb("identb", [128, 128], BF16)
        obb = sb("obb", [128, 128], BF16)
        musb = {64: sb("mus64", [128, 128], BF16), 16: sb("mus16", [32, 32], BF16)}
        mlsb = {64: sb("mls64", [128, 128], BF16), 16: sb("mls16", [32, 32], BF16)}
        muib = {64: sb("mui64", [128, 64], BF16), 16: sb("mui16", [32, 16], BF16)}
        maskBb = sb("maskBb", [128, 64], BF16)
        xT = sb("xT", [128, 16, TT], BF16)
        wbuf = [sb("wbuf0", [128, 16, CW], BF16), sb("wbuf1", [128, 16, CW], BF16)]
        mixT = sb("mixT", [128, 16, TT], BF16)
        pv = sb("pv", [128, NPV])
        pvx = sb("pvx", [128, 8])
        tabc = sb("tabc", [128, 16, 128])
        tabs = sb("tabs", [128, 16, 128])
        rho = sb("rho", [128, 16])
        bext = [sb("bext_re", [128, 16, 128], BF16), sb("bext_im", [128, 16, 128], BF16)]
        cext = [sb("cext_re", [128, 16, 128], BF16), sb("cext_nim", [128, 16, 128], BF16)]
        wglu = sb("wglu", [128, 4, 512], BF16)
        hcar = sb("hcar", [128, 16, 2])
        KT = sb("KT", [128, 8, 768], BF16)
        VT = sb("VT", [128, 6, 1024], BF16)
        btab = sb("btab", [128, 16, 256], BF16)
        lw1 = sb("lw1", [128, 4, 64], BF16)
        la1 = sb("la1", [128, 4, 64], BF16)
        lw2 = sb("lw2", [64, 512], BF16)
        la2 = sb("la2", [64, 512], BF16)
        Sst = sb("Sst", [128, 4, 128])
        shc = sb("shc", [128, 16])
        hcar1 = sb("hcar1", [128, 16, 2])
        Sst1 = sb("Sst1", [128, 4, 128])
        shc1 = sb("shc1", [128, 16])
        A32N = 8448
        A16N = 30400
        ar32 = sb("ar32", [128, A32N])
        ar16 = sb("ar16", [128, A16N], BF16)
        A32 = Arena(ar32, A32N)
        A16 = Arena(ar16, A16N)
        psA = pst("psA", [128, 512])
        psI = [pst("psI0", [128, 512]), pst("psI1", [128, 512])]
        psS = [pst("psS0", [128, 1024]), pst("psS1", [128, 1024])]
        psB = pst("psB", [128, 1024], BF16)

        identf = cview(cst, "identf")

        def chk(n):
            if stage == n:
                raise _Stop()

        P.dma("sp", cst[:], I["cst"][:, :])
        P.copy("dve", identb[:], identf)
        P.copy("dve", obb[:], cview(cst, "ob"))
        P.copy("dve", musb[64][:], cview(cst, "mus128"))
        P.copy("dve", mlsb[64][:], cview(cst, "mls128"))
        P.copy("dve", muib[64][:], cview(cst, "mui128"))
        P.copy("dve", musb[16][:], cview(cst, "mus32", 32))
        P.copy("dve", mlsb[16][:], cview(cst, "mls32", 32))
        P.copy("dve", muib[16][:], cview(cst, "mui32", 32))
        P.copy("dve", maskBb[:], cview(cst, "maskB"))
        zm = cview(cst, "zm")

        cvt_stage = [A16.alloc(16, CW), A16.alloc(16, CW)]
        for l in range(nlayers):
            for c in range(NCH):
                if c < NCH_IN:
                    src = I["w_in"][l, :, c * CW:(c + 1) * CW]
                else:
                    src = I["w_out"][l, :, (c - NCH_IN) * CW:(c - NCH_IN + 1) * CW]
                stg = cvt_stage[(l * NCH + c) % 2]
                P.dma("pool", stg, src.rearrange("(kt p) c -> p kt c", p=128))
                P.dma("sp", wsc[l, c].rearrange("p (kt c) -> p kt c", kt=16), stg)

        wslot = [0]

        def load_w(l, c):
            wb = wbuf[wslot[0]]
            wslot[0] ^= 1
            P.dma("sp", wb[:], wsc[l, c].rearrange("p (kt c) -> p kt c", kt=16))
            return wb

        evq = [0]

        def ev_eng():
            evq[0] = (evq[0] + 1) % 3
            return "dve" if evq[0] == 0 else "act"

        pin = [0]

        def fm_proj(wb, T, evac):
            for c2 in range(2):
                ps = psI[pin[0]]
                pin[0] ^= 1
                for kt in range(16):
                    P.mm(ps[:, 0:T], wb[:, kt, c2 * 128:(c2 + 1) * 128], xT[:, kt, 0:T],
                         start=(kt == 0), stop=(kt == 15))
                evac(ps[:, 0:T], c2)

        def tm_proj(wb, T, evac):
            for t0 in range(0, T, 128):
                n = min(128, T - t0)
                ps = psI[pin[0]]
                pin[0] ^= 1
                for kt in range(16):
                    P.mm(ps[0:n, 0:CW], xT[:, kt, t0:t0 + n], wb[:, kt, :], start=(kt == 0), stop=(kt == 15))
                evac(ps[0:n, 0:CW], t0, n)

        def layer_setup(l):
            A32.reset()
            A16.reset()
            P.dma("sp", pv[:], I["pv"][l])
            P.ts("dve", pvx[:, 0:4], pv[:, 32:36], -1.0, ALU.mult, 1.0, ALU.add)
            P.dma("pool", wglu[:], I["ssm_w_glu"][l].rearrange("(kt p) c -> p kt c", p=128))
            P.dma("pool", lw1[:], I["rwkv_w1"][l].rearrange("(kt p) c -> p kt c", p=128))
            P.dma("pool", la1[:], I["rwkv_a1"][l].rearrange("(kt p) c -> p kt c", p=128))
            P.dma("pool", lw2[:], I["rwkv_w2"][l])
            P.dma("pool", la2[:], I["rwkv_a2"][l])
            P.dma("pool", cext[0][:], I["cext"][l, 0].rearrange("p (a b) -> p a b", a=16))
            P.dma("pool", cext[1][:], I["cext"][l, 1].rearrange("p (a b) -> p a b", a=16))
            P.ts("pool", cext[1][:], cext[1][:], -1.0, ALU.mult)
            braw = A32.alloc(16, 256)
            bco = A32.alloc(16)
            P.dma("sp", braw, I["braw"][l])
            P.dma("sp", bco, I["bconst"][l])
            P.tt("dve", braw, braw, bco.unsqueeze(2).to_broadcast([128, 16, 256]), ALU.subtract)
            P.tt("dve", btab[:], braw, cview(cst, "bneg").unsqueeze(1).to_broadcast([128, 16, 256]), ALU.add)
            A32.reset()

            def abq(lr, li, ldt, shape, want_q):
                al = lambda: A32.alloc(*shape)
                dt = al()
                P.act(dt, ldt, AF.Exp)
                e = al()
                P.tt("dve", e, lr, dt, ALU.mult)
                P.act(e, e, AF.Exp)
                th = al()
                P.tt("dve", th, li, dt, ALU.mult)
                kf = al()
                P.ts("dve", kf, th, 1.0 / (2 * math.pi), ALU.mult)
                P.ts("dve", kf, kf, 12582912.0, ALU.add)
                P.ts("dve", kf, kf, 12582912.0, ALU.subtract)
                r = al()
                P.stt(r, kf, -6.28125, th, ALU.mult, ALU.add)
                P.stt(r, kf, -(2 * math.pi - 6.28125), r, ALU.mult, ALU.add)
                P.ts("dve", r, r, math.pi, ALU.min, -math.pi, ALU.max)
                sn = al()
                P.act(sn, r, AF.Sin)
                ab = al()
                P.act(ab, r, AF.Abs)
                P.ts("dve", ab, ab, -1.0, ALU.mult, math.pi / 2, ALU.add)
                cs = al()
                P.act(cs, ab, AF.Sin)
                res = {"rho": e, "cos": cs, "sin": sn}
                if want_q:
                    abr = al()
                    abi = al()
                    P.tt("dve", abr, e, cs, ALU.mult)
                    P.tt("dve", abi, e, sn, ALU.mult)
                    P.ts("dve", abr, abr, -1.0, ALU.add)
                    den = al()
                    t2 = al()
                    P.tt("dve", den, lr, lr, ALU.mult)
                    P.tt("dve", t2, li, li, ALU.mult)
                    P.tt("dve", den, den, t2, ALU.add)
                    P.recip(den, den)
                    qr = al()
                    qi = al()
                    P.tt("dve", qr, abr, lr, ALU.mult)
                    P.tt("dve", t2, abi, li, ALU.mult)
                    P.tt("dve", qr, qr, t2, ALU.add)
                    P.tt("dve", qr, qr, den, ALU.mult)
                    P.tt("dve", qi, abi, lr, ALU.mult)
                    P.tt("dve", t2, abr, li, ALU.mult)
                    P.tt("dve", qi, qi, t2, ALU.subtract)
                    P.tt("dve", qi, qi, den, ALU.mult)
                    res["qr"] = qr
                    res["qi"] = qi
                return res

            spr = A32.alloc(3, 16)
            P.dma("sp", spr, I["ssm_pair"][l])
            r1 = abq(spr[:, 0, :], spr[:, 1, :], spr[:, 2, :], (16,), False)
            P.copy("dve", rho[:], r1["rho"])
            P.copy("dve", tabc[:, :, 0], r1["cos"])
            P.copy("dve", tabs[:, :, 0], r1["sin"])
            n = 1
            while n < 128:
                tc0 = tabc[:, :, 0:n]
                ts0 = tabs[:, :, 0:n]
                Cn = tabc[:, :, n - 1:n].to_broadcast([128, 16, n])
                Sn = tabs[:, :, n - 1:n].to_broadcast([128, 16, n])
                t1 = A32.alloc(16, n)
                t2 = A32.alloc(16, n)
                P.tt("dve", t1, tc0, Cn, ALU.mult)
                P.tt("dve", t2, ts0, Sn, ALU.mult)
                P.tt("dve", tabc[:, :, n:2 * n], t1, t2, ALU.subtract)
                P.tt("dve", t1, ts0, Cn, ALU.mult)
                P.tt("dve", t2, tc0, Sn, ALU.mult)
                P.tt("dve", tabs[:, :, n:2 * n], t1, t2, ALU.add)
                n *= 2
            A32.reset()
            sch = A32.alloc(5, 256)
            P.dma("sp", sch, I["ssm_ch"][l])
            r2 = abq(sch[:, 0, :], sch[:, 1, :], sch[:, 2, :], (256,), True)
            bre = A32.alloc(4, 64)
            bim = A32.alloc(4, 64)
            t3 = A32.alloc(256)
            brf = bre.rearrange("p a b -> p (a b)")
            bif = bim.rearrange("p a b -> p (a b)")
            P.tt("dve", brf, r2["qr"], sch[:, 3, :], ALU.mult)
            P.tt("dve", t3, r2["qi"], sch[:, 4, :], ALU.mult)
            P.tt("dve", brf, brf, t3, ALU.subtract)
            P.tt("dve", bif, r2["qr"], sch[:, 4, :], ALU.mult)
            P.tt("dve", t3, r2["qi"], sch[:, 3, :], ALU.mult)
            P.tt("dve", bif, bif, t3, ALU.add)
            bmo = _CST["bmask"][0]
            for pi in range(16):
                for g2 in range(2):
                    msk = cst[:, bmo + (pi % 4) * 2 + g2: bmo + (pi % 4) * 2 + g2 + 1]
                    P.ts("dve", bext[0][:, pi, g2 * 64:(g2 + 1) * 64], bre[:, pi // 4, :], msk, ALU.mult)
                    P.ts("pool", bext[1][:, pi, g2 * 64:(g2 + 1) * 64], bim[:, pi // 4, :], msk, ALU.mult)
            A32.reset()

        def load_xT(src, T):
            for t0 in range(0, T, 128):
                n = min(128, T - t0)
                top = A32.top
                xs = A32.alloc(D)
                P.dma("sp", xs[0:n, :], src[t0:t0 + n, :])
                for g in range(4):
                    for j in range(4):
                        kt = g * 4 + j
                        P.tr(psA[:, j * 128:j * 128 + n], xs[0:n, kt * 128:(kt + 1) * 128], identf[0:n, 0:n])
                    P.copy(ev_eng(), xT[:, g * 4:(g + 1) * 4, t0:t0 + n],
                           psA[:].rearrange("p (a b) -> p a b", a=4)[:, :, 0:n])
                A32.reset(top)

        def ssm_phase(l, T, segs, gs_out):
            ussm = A16.alloc(4, T)
            gs = A16.alloc(4, T)

            def ev_u(ci):
                def f(ps, c2):
                    P.copy(ev_eng(), ussm[:, ci * 2 + c2, :], ps)
                return f

            def ev_g(ci):
                def f(ps, c2):
                    P.act(gs[:, ci * 2 + c2, :], ps, AF.Silu)
                return f

            for ci in range(2):
                fm_proj(load_w(l, ci), T, ev_u(ci))
            for ci in range(2):
                fm_proj(load_w(l, 2 + ci), T, ev_g(ci))
            z32 = A32.alloc(4, T)
            zbf = A16.alloc(4, T)
            for ct in range(4):
                pY = psS[0][:, 512:512 + T]
                items = []
                for (c0, ncol, car) in segs:
                    for s0 in range(c0, c0 + ncol, 128):
                        ns = min(128, c0 + ncol - s0)
                        for pm in range(4):
                            items.append(dict(s0=s0, ns=ns, pm=pm, pi=ct * 4 + pm, car=car))
                base32, base16 = A32.top, A16.top

                def stA(n, it):
                    pZt = (psS[1][:, 0:256], psS[1][:, 512:768], psA[:, 0:256])[n % 3]
                    pZ = pZt.rearrange("p (a b) -> p a b", a=2)[:, :, 0:it["ns"]]
                    it["pZ"] = pZ
                    u_ = ussm[:, ct, it["s0"]:it["s0"] + it["ns"]]
                    P.mm(pZ[:, 0, :], bext[0][:, it["pi"], :], u_)
                    P.mm(pZ[:, 1, :], bext[1][:, it["pi"], :], u_)

                def stB1(n, it):
                    ns, pi, car, pZ = it["ns"], it["pi"], it["car"], it["pZ"]
                    A32.reset(base32 + (n % 3) * 1024)
                    A16.reset(base16 + (n % 3) * 256)
                    tcb = tabc[:, pi, 0:ns].unsqueeze(1).to_broadcast([128, 2, ns])
                    tsb = tabs[:, pi, 0:ns].unsqueeze(1).to_broadcast([128, 2, ns])
                    M1 = A32.alloc(2, ns)
                    M2 = A32.alloc(2, ns)
                    P.tt("dve", M1, pZ, tcb, ALU.mult)
                    P.tt("dve", M2, pZ, tsb, ALU.mult)
                    Zp = A32.alloc(2, ns)
                    P.tt("pool", Zp[:, 0, :], M1[:, 0, :], M2[:, 1, :], ALU.add)
                    P.tt("pool", Zp[:, 1, :], M1[:, 1, :], M2[:, 0, :], ALU.subtract)
                    Hp = A32.alloc(2, ns)
                    Dm = A16.alloc(2, ns)
                    it.update(M1=M1, M2=M2, Zp=Zp, Hp=Hp, Dm=Dm, tcb=tcb, tsb=tsb)

                def stB2(n, it):
                    ns, pi, car = it["ns"], it["pi"], it["car"]
                    M1, M2, Zp, Hp, Dm, tcb, tsb = (it[k_] for k_ in ("M1", "M2", "Zp", "Hp", "Dm", "tcb", "tsb"))
                    rb = rho[:, pi:pi + 1].to_broadcast([128, ns])
                    P.scan(Hp[:, 0, :], rb, Zp[:, 0, :], car[:, pi, 0:1])
                    P.scan(Hp[:, 1, :], rb, Zp[:, 1, :], car[:, pi, 1:2])
                    P.tt("dve", M1, Hp, tcb, ALU.mult)
                    P.tt("dve", M2, Hp, tsb, ALU.mult)
                    P.tt("pool", Dm[:, 0, :], M1[:, 0, :], M2[:, 1, :], ALU.subtract)
                    P.tt("pool", Dm[:, 1, :], M1[:, 1, :], M2[:, 0, :], ALU.add)
                    P.tt("pool", car[:, pi, 0:1], M1[:, 0, ns - 1:ns], M2[:, 1, ns - 1:ns], ALU.subtract)
                    P.tt("pool", car[:, pi, 1:2], M1[:, 1, ns - 1:ns], M2[:, 0, ns - 1:ns], ALU.add)
                    it["Dm"] = Dm

                def stC(n, it):
                    s0, ns, pm, pi, Dm = it["s0"], it["ns"], it["pm"], it["pi"], it["Dm"]
                    P.mm(pY[:, s0:s0 + ns], cext[0][:, pi, :], Dm[:, 0, :], start=(pm == 0), stop=False)
                    P.mm(pY[:, s0:s0 + ns], cext[1][:, pi, :], Dm[:, 1, :], start=False, stop=(pm == 3))

                nit = len(items)
                for n in range(min(2, nit)):
                    stA(n, items[n])
                stB1(0, items[0])
                for n in range(nit):
                    if n + 1 < nit:
                        stB1(n + 1, items[n + 1])
                    stB2(n, items[n])
                    if n + 2 < nit:
                        stA(n + 2, items[n + 2])
                    stC(n, items[n])
                A32.reset(base32)
                A16.reset(base16)
                top32 = A32.top
                ysb = A32.alloc(T)
                P.stt(ysb, ussm[:, ct, :], pv[:, 48 + ct:49 + ct], pY, ALU.mult, ALU.add)
                t1 = A32.alloc(T)
                P.tt("pool", t1, ysb, ysb, ALU.mult)
                P.ts("pool", t1, t1, 0.044715, ALU.mult, 1.0, ALU.add)
                P.tt("pool", t1, t1, ysb, ALU.mult)
                P.act(t1, t1, AF.Sigmoid, scale=1.5957691216057308)
                P.tt("dve", z32[:, ct, :], ysb, t1, ALU.mult)
                P.copy("pool", zbf[:, ct, :], z32[:, ct, :])
                A32.reset(top32)
            for co in range(4):
                ps = psI[pin[0]]
                pin[0] ^= 1
                for ci in range(4):
                    P.mm(ps[:, 0:T], wglu[:, ci, co * 128:(co + 1) * 128], zbf[:, ci, :], start=(ci == 0), stop=(ci == 3))
                top32 = A32.top
                sg = A32.alloc(T)
                P.act(sg, ps[:, 0:T], AF.Sigmoid, bias=pv[:, 52 + co:53 + co])
                P.tt("dve", sg, sg, z32[:, co, :], ALU.mult)
                P.tt("dve", mixT[:, co, 0:T], sg, gs[:, co, :], ALU.mult)
                A32.reset(top32)

        def att_phase(l, T, units, kcol0, vrow0, kv_out):
            qT = A16.alloc(8, T)
            ga = A16.alloc(8, T)
            kvstage = {}

            def ev_q(ci):
                def f(ps, c2):
                    P.act(qT[:, ci * 2 + c2, :], ps, AF.Copy, scale=0.125)
                return f

            def ev_k(ci):
                def f(ps, c2):
                    P.copy(ev_eng(), KT[:, ci * 2 + c2, kcol0:kcol0 + T], ps)
                return f

            def ev_g(ci):
                def f(ps, c2):
                    P.act(ga[:, ci * 2 + c2, :], ps, AF.Silu)
                return f

            def ev_tok(ci, dst_bf, dram_fn):
                def f(ps, t0, n):
                    if dst_bf is not None:
                        r0 = vrow0 + t0
                        P.copy("act", VT[(r0 % 128):(r0 % 128) + n, r0 // 128, ci * CW:(ci + 1) * CW], ps)
                    if dram_fn is not None:
                        key = (id(dram_fn), t0)
                        if key not in kvstage:
                            kvstage[key] = A32.alloc(1024)
                        stg = kvstage[key]
                        P.copy("dve", stg[0:n, ci * CW:(ci + 1) * CW], ps)
                        if ci == 3:
                            P.dma("sp", dram_fn(t0, n), stg[0:n, :])
                return f

            for ci in range(4):
                fm_proj(load_w(l, 4 + ci), T, ev_q(ci))
            for ci in range(4):
                wb = load_w(l, 8 + ci)
                fm_proj(wb, T, ev_k(ci))
                if kv_out[0] is not None:
                    tm_proj(wb, T, ev_tok(ci, None, kv_out[0]))
            for ci in range(4):
                tm_proj(load_w(l, 12 + ci), T, ev_tok(ci, True if kv_out[2] else None, kv_out[1]))
            for ci in range(4):
                fm_proj(load_w(l, 16 + ci), T, ev_g(ci))

            sbuf_i = [0]
            att_rot = [0]

            def run_units(units):
                items = [(u, h) for u in units for h in range(16)]
                base32, base16 = A32.top, A16.top
                stt_ = {}

                def stA(i):
                    u, h = items[i]
                    nq, ncol = u["nq"], u["ncol"]
                    ft, bp = h // 2, 64 * (h % 2)
                    ps = psS[i % 2]
                    for (r0, nr, tq0, kc0, c0, c1) in u["halves"]:
                        for (a_, b_) in ((c0, min(c1, 512)), (max(c0, 512), c1)):
                            if b_ <= a_:
                                continue
                            P.mm(ps[r0:r0 + nr, a_:b_], qT[bp:bp + 64, ft, tq0:tq0 + nr],
                                 KT[bp:bp + 64, ft, kc0 + a_ - c0:kc0 + b_ - c0], start=True, stop=False,
                                 skip_group_check=True)
                    for (a_, b_) in ((384, 512), (512, min(640, ncol))):
                        if b_ <= a_:
                            continue
                        P.mm(ps[0:nq, a_:b_], identb[0:nq, 0:nq], btab[0:nq, h, a_ - 384:b_ - 384], start=False, stop=True,
                             skip_group_check=True)
                    if u["maskB"]:
                        P.mm(ps[0:nq, 0:64], identb[0:nq, 0:nq], maskBb[0:nq, :], start=False, stop=True,
                             skip_group_check=True)
                    if u["inv"] > 0:
                        P.ts("dve", ps[0:nq, 0:u["inv"]], ps[0:nq, 0:u["inv"]], NEG, ALU.add)
                    stt_[i] = dict(ps=ps)

                def stB1(i):
                    u, h = items[i]
                    nq, ncol = u["nq"], u["ncol"]
                    nb = (ncol + 127) // 128
                    ps = stt_[i]["ps"]
                    A32.reset(base32 + (i % 3) * 4)
                    A16.reset(base16 + (i % 3) * 1280)
                    mx = A32.alloc(1)
                    P.op("dve", (lambda e, o_=mx[0:nq, :], i_=ps[0:nq, 0:ncol]: e.tensor_reduce(
                        out=o_, in_=i_, axis=AX.X, op=ALU.max, negate=True)), [ps[0:nq, 0:ncol]], [mx[0:nq, :]])
                    pb = A16.alloc(nb * 128)
                    rs = A32.alloc(1)
                    P.act(pb[0:nq, 0:ncol], ps[0:nq, 0:ncol], AF.Exp, bias=mx[0:nq, :], accum_out=rs[0:nq, :])
                    pts = A16.alloc(nb, 128)
                    stt_[i].update(pb=pb, pts=pts, nb=nb, rs=rs)

                def stB2(i):
                    u, h = items[i]
                    nq, ncol = u["nq"], u["ncol"]
                    pb, rs = stt_[i]["pb"], stt_[i]["rs"]
                    P.recip(rs[0:nq, :], rs[0:nq, :])
                    P.ts("dve", pb[0:nq, 0:ncol], pb[0:nq, 0:ncol], rs[0:nq, :], ALU.mult)

                def stC(i):
                    u, h = items[i]
                    nq, ncol = u["nq"], u["ncol"]
                    pb, nb = stt_[i]["pb"], stt_[i]["nb"]
                    pT = psB[:, 0:nb * 128].rearrange("p (a b) -> p a b", a=nb)
                    for bi in range(nb):
                        kb = min(128, ncol - bi * 128)
                        P.tr(pT[0:kb, bi, 0:nq], pb[0:nq, bi * 128:bi * 128 + kb], identb[0:nq, 0:nq])
                    pts = stt_[i]["pts"]
                    nfull = ncol // 128
                    if nfull:
                        P.copy("act", pts[:, 0:nfull, 0:nq], pT[:, 0:nfull, 0:nq])
                    if ncol % 128:
                        kb = ncol % 128
                        P.copy("act", pts[0:kb, nfull, 0:nq], pT[0:kb, nfull, 0:nq])

                def stE(i):
                    u, h = items[i]
                    nq, ncol = u["nq"], u["ncol"]
                    ft, bp = h // 2, 64 * (h % 2)
                    pts, nb = stt_[i]["pts"], stt_[i]["nb"]
                    po = psA[bp:bp + 64, (i % 4) * 128:(i % 4) * 128 + nq]
                    vb0 = u["vblk0"]
                    for bi in range(nb):
                        kb = min(128, ncol - bi * 128)
                        P.mm(po, VT[0:kb, vb0 + bi, h * 64:(h + 1) * 64], pts[0:kb, bi, 0:nq],
                             start=(bi == 0), stop=(bi == nb - 1))
                    tq = u["tq0"]
                    P.tt("dve", mixT[bp:bp + 64, 4 + ft, tq:tq + nq], po, ga[bp:bp + 64, ft, tq:tq + nq], ALU.mult)
                    del stt_[i]

                n = len(items)
                for i in range(min(2, n)):
                    stA(i)
                stB1(0)
                for i in range(n):
                    if i + 1 < n:
                        stB1(i + 1)
                    stB2(i)
                    if i + 2 < n:
                        stA(i + 2)
                    stC(i)
                    if i >= 1:
                        stE(i - 1)
                stE(n - 1)
                A32.reset(base32)
                A16.reset(base16)

            if units is not None:
                run_units(units)
            return run_units

        def rwkv_phase(l, T, C, segs, shift_out=None):
            Z = 2 * C
            NCk = T // C
            gc = A16.alloc(4, T)
            xr = A16.alloc(4, T)
            xk = A16.alloc(4, T)
            xv = A16.alloc(4, T)
            markW = A16.top
            xw = A16.alloc(4, T)
            xa = A16.alloc(4, T)
            markR = A16.top
            raw = A16.alloc(4, 4, T)
            def ev_g(ci):
                def f(ps, c2):
                    P.act(gc[:, ci * 2 + c2, :], ps, AF.Silu)
                return f

            top32_0 = A32.top
            newsh = {}
            for kind in range(4):
                for ci in range(2):
                    wb = load_w(l, 20 + kind * 2 + ci)

                    def f(ps, c2, kind=kind, ci=ci):
                        P.copy(ev_eng(), raw[:, kind, ci * 2 + c2, :], ps)
                    fm_proj(wb, T, f)
                    if shift_out is not None:
                        for si, (c0, ncol, st, sc) in enumerate(segs):
                            psr = psI[pin[0]]
                            pin[0] ^= 1
                            lc = c0 + ncol - 1
                            for kt in range(16):
                                P.mm(psr[0:1, 0:CW], xT[:, kt, lc:lc + 1], wb[:, kt, :], start=(kt == 0), stop=(kt == 15))
                            top = A32.top
                            stg = A32.alloc(CW)
                            P.copy("act", stg[0:1, :], psr[0:1, 0:CW])
                            o0 = kind * 512 + ci * CW
                            P.dma("sp", shift_out[si][:, o0:o0 + CW], stg[0:1, :])
                            A32.reset(top)
            for ci in range(2):
                fm_proj(load_w(l, 28 + ci), T, ev_g(ci))
            chk(601)
            dl = A32.alloc(4, 4, T)
            for (c0, ncol, st, sc) in segs:
                P.tt("dve", dl[:, :, :, c0:c0 + 1], sc[:].rearrange("p (a b) -> p a b", a=4).unsqueeze(3),
                     raw[:, :, :, c0:c0 + 1], ALU.subtract)
                if ncol > 1:
                    P.tt("dve", dl[:, :, :, c0 + 1:c0 + ncol], raw[:, :, :, c0:c0 + ncol - 1],
                         raw[:, :, :, c0 + 1:c0 + ncol], ALU.subtract)
            chk(602)
            for si, (c0, ncol, st, sc) in enumerate(segs):
                lc = c0 + ncol - 1
                P.copy("pool", sc[:].rearrange("p (a b) -> p a b", a=4).unsqueeze(3), raw[:, :, :, lc:lc + 1])
            chk(603)
            mu = lambda i: pv[:, i * 4:(i + 1) * 4].unsqueeze(2).to_broadcast([128, 4, T])
            tmp = A32.alloc(4, T)
            for dst, kind, mi in ((xr, 0, 0), (xk, 1, 1), (xv, 2, 2), (xw, 3, 3), (xa, 3, 4)):
                P.tt("dve", tmp, dl[:, kind, :, :], mu(mi), ALU.mult)
                P.tt("dve", dst, tmp, raw[:, kind, :, :], ALU.add)
            A32.reset(top32_0)
            A16.reset(markR)
            chk(61)
            sig = A32.alloc(4, T)
            aa = A32.alloc(4, T)
            for (src, w1, w2, dst, b0, use_tanh) in ((xw, lw1, lw2, sig, 20, True), (xa, la1, la2, aa, 24, False)):
                ps = psI[pin[0]]
                pin[0] ^= 1
                for kt in range(4):
                    P.mm(ps[0:64, 0:T], w1[:, kt, :], src[:, kt, :], start=(kt == 0), stop=(kt == 3))
                top16 = A16.top
                hh = A16.alloc(T)
                if use_tanh:
                    P.act(hh[0:64, :], ps[0:64, 0:T], AF.Tanh)
                else:
                    P.copy("dve", hh[0:64, :], ps[0:64, 0:T])
                for ft in range(4):
                    ps2 = psI[pin[0]]
                    pin[0] ^= 1
                    P.mm(ps2[:, 0:T], w2[:, ft * 128:(ft + 1) * 128], hh[0:64, :])
                    P.act(dst[:, ft, :], ps2[:, 0:T], AF.Sigmoid, bias=pv[:, b0 + ft:b0 + ft + 1])
                A16.reset(top16)
            A16.reset(markW)
            chk(62)
            kkn = A16.alloc(4, T)
            kp = A16.alloc(4, T)
            bb = A16.alloc(4, T)
            top32 = A32.top
            kk32 = A32.alloc(4, T)
            sq16 = A16.alloc(4, T)
            for ft in range(4):
                P.ts("dve", kk32[:, ft, :], xk[:, ft, :], pv[:, 28 + ft:29 + ft], ALU.mult)
            P.tt("pool", sq16, kk32, kk32, ALU.mult)
            rn = A32.alloc(4, T)
            for half in range(0, 4, 2):
                psq = psS[0]
                for ft in range(half, half + 2):
                    P.mm(psq[:, (ft - half) * 512:(ft - half) * 512 + T], obb[:], sq16[:, ft, :])
                    P.ts("dve", rn[:, ft, :], psq[:, (ft - half) * 512:(ft - half) * 512 + T], 1e-24, ALU.max)
            P.act(rn, rn, AF.Sqrt)
            P.recip(rn, rn)
            P.tt("dve", kkn, kk32, rn, ALU.mult)
            t1 = A32.alloc(4, T)
            for ft in range(4):
                P.ts("dve", t1[:, ft, :], aa[:, ft, :], pv[:, 32 + ft:33 + ft], ALU.mult, pvx[:, ft:ft + 1], ALU.add)
            P.tt("dve", kp, xk, t1, ALU.mult)
            P.tt("pool", bb, kkn, aa, ALU.mult)
            A32.reset(top32)
            chk(63)
            ld = A32.alloc(4, T)
            cum = A32.alloc(4, T)
            P.ts("dve", ld, sig, -KAPPA, ALU.mult)
            cmk = cview(cst, "cmask64" if C == 64 else "cmask16", w=T)
            for ft in range(4):
                P.scan(cum[:, ft, :], cmk, ld[:, ft, :], 0.0)
            cumc = cum.rearrange("p f (k c) -> p f k c", c=C)
            cend = cumc[:, :, :, C - 1:C]
            E = A32.alloc(4, T)
            rt = A16.alloc(4, T)
            at = A16.alloc(4, T)
            bt = A16.alloc(4, T)
            kt_ = A16.alloc(4, T)
            bh = A16.alloc(4, T)
            kh = A16.alloc(4, T)
            P.act(E, cum, AF.Exp)
            P.tt("dve", rt, xr, E, ALU.mult)
            P.tt("dve", E, cum, ld, ALU.subtract)
            P.act(E, E, AF.Exp)
            P.stt(at, kkn, -1.0, E, ALU.mult, ALU.mult)
            P.act(E, cum, AF.Exp, scale=-1.0)
            P.tt("dve", bt, bb, E, ALU.mult)
            P.tt("pool", kt_, kp, E, ALU.mult)
            Ec = E.rearrange("p f (k c) -> p f k c", c=C)
            P.tt("dve", Ec, cend.to_broadcast([128, 4, NCk, C]), cumc, ALU.subtract)
            P.act(E, E, AF.Exp)
            P.tt("dve", bh, bb, E, ALU.mult)
            P.tt("pool", kh, kp, E, ALU.mult)
            WC = A32.alloc(4, NCk)
            P.act(WC.unsqueeze(3), cend, AF.Exp)
            rkr = A16.alloc(4, T)
            for ft in range(4):
                P.stt(rkr[:, ft, :], xr[:, ft, :], pv[:, 36 + ft:37 + ft], kp[:, ft, :], ALU.mult, ALU.mult)
            Yb = A32.alloc(4, T)
            chk(64)
            mus, mls, mui = musb[C], mlsb[C], muib[C]
            nsteps = {64: 5, 16: 3}[C]
            zmb = zm.unsqueeze(1).unsqueeze(3)
            musB = mus[:].unsqueeze(1).to_broadcast([Z, 4, Z])
            mlsB = mls[:].unsqueeze(1).to_broadcast([Z, 4, Z])
            loop32, loop16 = A32.top, A16.top
            slots = []
            for _i in range(2):
                slots.append(dict(az=A16.alloc(4, 2, C), Tz=A16.alloc(4, Z)[0:Z], Uka=A16.alloc(4, Z)[0:Z],
                                  Ubk=A16.alloc(2, 4, C)[0:Z], vkT=A16.alloc(2, 4, 128), bT=A16.alloc(4, 128)))
            tmpz = [A16.alloc(4, 2, C) for _i in range(5)]
            Az = [A16.alloc(4, Z)[0:Z], A16.alloc(4, Z)[0:Z]]
            ATz = [A16.alloc(4, Z)[0:Z], A16.alloc(4, Z)[0:Z]]
            STb = A16.alloc(4, 128)
            XT = A16.alloc(4, 128)[0:Z]
            SAT = A16.alloc(4, 128)[0:Z]
            pA = psS[0][:, 0:512].rearrange("p (f z) -> p f z", f=4)[0:Z, :, 0:Z]
            pAT = psS[0][:, 512:1024].rearrange("p (f z) -> p f z", f=4)[0:Z, :, 0:Z]
            pU = psS[1][:, 0:512].rearrange("p (f z) -> p f z", f=4)[0:Z, :, 0:Z]
            pU2 = psS[1][:, 512:1024].rearrange("p (a f c) -> p a f c", a=2, f=4)[0:Z, :, :, 0:C]
            pTt = pU
            pB = psB[:].rearrange("p (a f i) -> p a f i", a=2, f=4)

            def zexp_into(dst, src, t0):
                P.tt("dve", dst, src[:, :, t0:t0 + C].unsqueeze(2).to_broadcast([128, 4, 2, C]),
                     zmb.to_broadcast([128, 4, 2, C]), ALU.mult)
                return dst.rearrange("p f a c -> p f (a c)")

            def pre_gen(ck, sl):
                t0 = ck * C
                az = zexp_into(sl["az"], at, t0)
                bz, kz, bhz, khz, vz = [zexp_into(d_, s_, t0) for d_, s_ in zip(tmpz, (bt, kt_, bh, kh, xv))]
                vkT, bT = sl["vkT"], sl["bT"]
                for ft in range(4):
                    P.tr(pB[0:Z, 0, ft, :], vz[:, ft, :], identb[:])
                    P.tr(pB[0:Z, 1, ft, :], khz[:, ft, :], identb[:])
                P.copy("act", vkT[0:Z], pB[0:Z])
                for ft in range(4):
                    P.tr(pB[0:Z, 0, ft, :], bhz[:, ft, :], identb[:])
                P.copy("act", bT[0:Z], pB[0:Z, 0])
                yield
                for ft in range(4):
                    P.mm(pA[:, ft, :], bz[:, ft, :], az[:, ft, :])
                    P.mm(pAT[:, ft, :], az[:, ft, :], bz[:, ft, :])
                    P.mm(pU[:, ft, :], kz[:, ft, :], az[:, ft, :])
                    P.mm(pU2[:, 0, ft, :], bz[:, ft, :], rt[:, ft, t0:t0 + C])
                    P.mm(pU2[:, 1, ft, :], kz[:, ft, :], rt[:, ft, t0:t0 + C])
                Tz, Uka, Ubk = sl["Tz"], sl["Uka"], sl["Ubk"]
                P.tt("dve", Az[0], pA, musB, ALU.mult)
                P.tt("dve", ATz[0], pAT, mlsB, ALU.mult)
                P.tt("dve", Uka, pU, musB, ALU.mult)
                P.tt("dve", Ubk, pU2, mui[:].unsqueeze(1).unsqueeze(1).to_broadcast([Z, 2, 4, C]), ALU.mult)
                P.tt("pool", Tz, Az[0], identb[0:Z, 0:Z].unsqueeze(1).to_broadcast([Z, 4, Z]), ALU.add)
                yield
                cur = 0
                for stp in range(nsteps):
                    last = (stp == nsteps - 1)
                    nxt = cur ^ 1
                    for ft in range(4):
                        if not last:
                            P.mm(pA[:, ft, :], ATz[cur][:, ft, :], Az[cur][:, ft, :])
                        P.mm(pAT[:, ft, :], Az[cur][:, ft, :], ATz[cur][:, ft, :])
                    if not last:
                        P.copy("act", Az[nxt], pA)
                    P.copy("dve", ATz[nxt], pAT)
                    for ft in range(4):
                        P.mm(pTt[:, ft, :], ATz[nxt][:, ft, :], Tz[:, ft, :])
                    P.tt("dve", Tz, Tz, pTt, ALU.add)
                    cur = nxt
                    yield

            def chain_gen(ck, sl, Sfull):
                t0 = ck * C
                az = sl["az"].rearrange("p f a c -> p f (a c)")
                Tz, Uka, Ubk = sl["Tz"], sl["Uka"], sl["Ubk"]
                vT = sl["vkT"][0:Z, 0]
                kT = sl["vkT"][0:Z, 1]
                bT = sl["bT"]
                P.copy("act", STb, Sfull)
                pX = psI[1][:, 0:512].rearrange("p (f i) -> p f i", f=4)[0:Z]
                for ft in range(4):
                    P.mm(pX[:, ft, :], az[:, ft, :], STb[:, ft, :], start=True, stop=False)
                    P.mm(pX[:, ft, :], Uka[:, ft, :], vT[:, ft, :], start=False, stop=True)
                P.copy("act", XT, pX)
                yield
                for ft in range(4):
                    P.mm(pX[:, ft, :], Tz[:, ft, :], XT[:, ft, :])
                P.copy("act", SAT, pX)
                yield
                pY = psI[0][:, 0:4 * C].rearrange("p (f c) -> p f c", f=4)
                for ft in range(4):
                    P.mm(pY[:, ft, :], STb[:, ft, :], rt[:, ft, t0:t0 + C], start=True, stop=False)
                    P.mm(pY[:, ft, :], SAT[:, ft, :], Ubk[:, 0, ft, :], start=False, stop=False)
                    P.mm(pY[:, ft, :], vT[:, ft, :], Ubk[:, 1, ft, :], start=False, stop=True)
                P.copy("dve", Yb[:, :, t0:t0 + C], pY)
                yield
                pS = psA[:].rearrange("p (f i) -> p f i", f=4)
                for ft in range(4):
                    P.mm(pS[:, ft, :], bT[0:Z, ft, :], SAT[:, ft, :], start=True, stop=False)
                    P.mm(pS[:, ft, :], kT[:, ft, :], vT[:, ft, :], start=False, stop=True)
                P.tt("dve", Sfull, Sfull, WC[:, :, ck:ck + 1].to_broadcast([128, 4, 128]), ALU.mult)
                P.tt("dve", Sfull, Sfull, pS, ALU.add)
                yield

            chunks = [(ck, Sfull) for (c0, ncol, Sfull, sc) in segs for ck in range(c0 // C, (c0 + ncol) // C)]
            for _ in pre_gen(chunks[0][0], slots[0]):
                pass
            for j, (ck, Sfull) in enumerate(chunks):
                g1 = pre_gen(chunks[j + 1][0], slots[(j + 1) % 2]) if j + 1 < len(chunks) else iter(())
                g2 = chain_gen(ck, slots[j % 2], Sfull)
                d1 = d2 = False
                while not (d1 and d2):
                    if not d1:
                        d1 = next(g1, "end") == "end"
                    if not d2:
                        d2 = next(g2, "end") == "end"
            A32.reset(loop32)
            A16.reset(loop16)
            chk(66)
            ybf = A16.alloc(4, T)
            P.copy("act", ybf, Yb)
            pM = psS[0][:].rearrange("p (f t) -> p f t", f=4)[:, :, 0:T] if T == 256 else \
                psS[0][:, 0:4 * T].rearrange("p (f t) -> p f t", f=4)
            pV = psS[1][:].rearrange("p (f t) -> p f t", f=4)[:, :, 0:T] if T == 256 else \
                psS[1][:, 0:4 * T].rearrange("p (f t) -> p f t", f=4)
            for ft in range(4):
                P.mm(pM[:, ft, :], obb[:], ybf[:, ft, :])
            ym = A32.alloc(4, T)
            P.stt(ym, pM, -1.0 / 64, Yb, ALU.mult, ALU.add)
            P.act(ybf, ym, AF.Square)
            for ft in range(4):
                P.mm(pV[:, ft, :], obb[:], ybf[:, ft, :])
            sd = A32.alloc(4, T)
            P.ts("dve", sd, pV, 1.0 / 64, ALU.mult, GN_EPS, ALU.add)
            P.act(sd, sd, AF.Sqrt)
            P.recip(sd, sd)
            P.tt("dve", ym, ym, sd, ALU.mult)
            for ft in range(4):
                P.ts("dve", ym[:, ft, :], ym[:, ft, :], pv[:, 40 + ft:41 + ft], ALU.mult, pv[:, 44 + ft:45 + ft], ALU.add)
            for ft in range(4):
                P.mm(pM[:, ft, :], obb[:], rkr[:, ft, :])
            P.tt("dve", sd, pM, xv, ALU.mult)
            P.tt("dve", ym, ym, sd, ALU.add)
            P.tt("dve", mixT[:, 12:16, 0:T], ym, gc, ALU.mult)

        def out_phase(l, T, xsrc, ydst):
            top32 = A32.top
            nsub = (T + 127) // 128
            lg = A32.alloc(2, D)
            P.dma("sp", lg, I["lngb"][l])
            zs = []
            for si in range(nsub):
                n = min(128, T - si * 128)
                z = A32.alloc(D)
                P.dma("sp", z[0:n, :], xsrc[si * 128:si * 128 + n, :])
                zs.append((z, n))
            for c in range(NCH_OUT):
                wb = load_w(l, NCH_IN + c)
                for si, (z, n) in enumerate(zs):
                    ps = psI[pin[0]]
                    pin[0] ^= 1
                    for kt in range(16):
                        P.mm(ps[0:n, 0:CW], mixT[:, kt, si * 128:si * 128 + n], wb[:, kt, :],
                             start=(kt == 0), stop=(kt == 15))
                    P.stt(z[0:n, c * CW:(c + 1) * CW], z[0:n, c * CW:(c + 1) * CW], ALPHA, ps[0:n, 0:CW],
                          ALU.mult, ALU.add)
            for si, (z, n) in enumerate(zs):
                st = A32.alloc(4, 6)
                mv = A32.alloc(2)
                zc = z.rearrange("p (a b) -> p a b", a=4)
                for a in range(4):
                    P.op("dve", (lambda e, o=st[0:n, a, :], i=zc[0:n, a, :]: e.bn_stats(out=o, in_=i)),
                         [zc[0:n, a, :]], [st[0:n, a, :]])
                stf = st.rearrange("p a b -> p (a b)")
                P.op("dve", (lambda e, o=mv[0:n, :], i=stf[0:n, :]: e.bn_aggr(out=o, in_=i)), [stf[0:n, :]], [mv[0:n, :]])
                rs = A32.alloc(1)
                P.ts("dve", rs[0:n, :], mv[0:n, 1:2], LN_EPS, ALU.add)
                P.act(rs[0:n, :], rs[0:n, :], AF.Sqrt)
                P.recip(rs[0:n, :], rs[0:n, :])
                P.ts("dve", z[0:n, :], z[0:n, :], mv[0:n, 0:1], ALU.subtract, rs[0:n, :], ALU.mult)
                P.tt("pool", z[0:n, :], z[0:n, :], lg[0:n, 0, :], ALU.mult)
                P.tt("dve", z[0:n, :], z[0:n, :], lg[0:n, 1, :], ALU.add)
                P.dma("sp", ydst[si * 128:si * 128 + n, :], z[0:n, :])
            A32.reset(top32)

        try:
            chk(1)
            for l in range(nlayers):
                layer_setup(l)
                chk(2)
                last = (l == nlayers - 1)
                if do_prompt:
                    xsrc = I["xp"] if l == 0 else y0p
                    ydst = O["y_p"] if last else y0p
                    P.memset("pool", KT[:], 0.0)
                    P.memset("pool", VT[:], 0.0)
                    P.memset("dve", hcar[:], 0.0)
                    P.memset("dve", Sst[:], 0.0)
                    P.memset("dve", shc[:], 0.0)
                    for ti in range(NT):
                        A32.reset()
                        A16.reset()
                        ts_ = ti * TT
                        load_xT(xsrc[ts_:ts_ + TT, :], TT)
                        chk(101)
                        ssm_phase(l, TT, [(0, TT, hcar)], None)
                        chk(102)
                        A32.reset()
                        A16.reset()
                        units = []
                        for m in range(TT // 128):
                            a0 = 128 * m
                            units.append(dict(nq=128, ncol=640, maskB=True, vblk0=m, tq0=128 * m,
                                              inv=max(0, min(640, 512 - ts_ - a0)),
                                              halves=[(0, 64, 128 * m, a0, 0, 576),
                                                      (64, 64, 128 * m + 64, a0 + 64, 64, 640)]))
                        kv_out = None
                        if ts_ >= seq - 512:
                            r0 = ts_ - (seq - 512)
                            kv_out = (lambda t0, n, r0=r0: O["p_k"][l, r0 + t0:r0 + t0 + n, :],
                                      lambda t0, n, r0=r0: O["p_v"][l, r0 + t0:r0 + t0 + n, :], True)
                            import os as _os
                            if _os.environ.get("DBGKV") == "k":
                                kv_out = (kv_out[0], None, True)
                            if _os.environ.get("DBGKV") == "v":
                                kv_out = (None, kv_out[1], True)
                        else:
                            kv_out = (None, None, True)
                        att_phase(l, TT, units, 512, 512, kv_out)
                        chk(103)
                        P.copy("pool", KT[:, :, 0:256], KT[:, :, 256:512])
                        P.copy("pool", KT[:, :, 256:512], KT[:, :, 512:768])
                        P.copy("pool", VT[:, 0:2, :], VT[:, 2:4, :])
                        P.copy("pool", VT[:, 2:4, :], VT[:, 4:6, :])
                        A32.reset()
                        A16.reset()
                        chk(104)
                        rwkv_phase(l, TT, 64, [(0, TT, Sst[:], shc)], [O["p_shift"][l]] if ti == NT - 1 else None)
                        chk(105)
                        A32.reset()
                        A16.reset()
                        out_phase(l, TT, xsrc[ts_:ts_ + TT, :], ydst[ts_:ts_ + TT, :])
                    P.dma("sp", O["p_ssm"][l], hcar[:].rearrange("p a b -> p (a b)"))
                    P.dma("sp", O["p_rwkv"][l], Sst[:].rearrange("p a b -> p (a b)"))
                if do_sample:
                    T = NS * SQ
                    xsrc = I["xs"] if l == 0 else y0s
                    ydst = O["y_s"] if last else y0s
                    A32.reset()
                    A16.reset()
                    hc = [hcar, hcar1]
                    Ss = [Sst, Sst1]
                    sh = [shc, shc1]
                    for s in range(NS):
                        P.dma("sp", hc[s][:].rearrange("p a b -> p (a b)"), I["hss"][l, s])
                        P.dma("sp", Ss[s][:].rearrange("p a b -> p (a b)"), I["srw"][l, s])
                        P.dma("sp", sh[s][:], I["ssh"][l, s])
                    load_xT(xsrc, T)
                    chk(3)
                    ssm_phase(l, T, [(s * SQ, SQ, hc[s]) for s in range(NS)], None)
                    chk(4)
                    A32.reset()
                    A16.reset()
                    kv_out = (lambda t0, n: O["s_k"][l, t0:t0 + n, :], lambda t0, n: O["s_v"][l, t0:t0 + n, :], False)
                    att_in = att_phase(l, T, None, 640, 0, kv_out)
                    chk(5)
                    for s in range(NS):
                        for blk in range(4):
                            top = A32.top
                            cs_ = A32.alloc(1024)
                            P.dma("sp", cs_, I["ck"][l, s, blk * 128:(blk + 1) * 128, :])
                            for g in range(2):
                                for j in range(4):
                                    kt = g * 4 + j
                                    P.tr(psA[:, j * 128:(j + 1) * 128], cs_[:, kt * 128:(kt + 1) * 128], identf)
                                P.copy(ev_eng(), KT[:, g * 4:(g + 1) * 4, blk * 128:(blk + 1) * 128],
                                       psA[:].rearrange("p (a b) -> p a b", a=4))
                            A32.reset(top)
                        P.dma("pool", VT[:, 0:4, :], I["cv"][l, s].rearrange("(b p) c -> p b c", p=128))
                        P.copy("pool", KT[:, :, 512:512 + SQ], KT[:, :, 640 + s * SQ:640 + (s + 1) * SQ])
                        P.dma("pool", VT[0:SQ, 4, :], O["s_v"][l, s * SQ:(s + 1) * SQ, :])
                        att_in([dict(nq=SQ, ncol=512 + SQ, maskB=False, vblk0=0, tq0=s * SQ, inv=0,
                                     halves=[(0, SQ, s * SQ, 0, 0, 512 + SQ)])])
                    A32.reset()
                    A16.reset()
                    chk(6)
                    rwkv_phase(l, T, 16, [(s * SQ, SQ, Ss[s][:], sh[s]) for s in range(NS)], [O["s_shift"][l, s] for s in range(NS)])
                    chk(7)
                    A32.reset()
                    A16.reset()
                    out_phase(l, T, xsrc, ydst)
                    for s in range(NS):
                        P.dma("sp", O["s_ssm"][l, s], hc[s][:].rearrange("p a b -> p (a b)"))
                        P.dma("sp", O["s_rwkv"][l, s], Ss[s][:].rearrange("p a b -> p (a b)"))
        except _Stop:
            if do_prompt and stage >= 100:
                P.dma("sp", O["p_ssm"][l], hcar[:].rearrange("p a b -> p (a b)"))
                P.dma("sp", O["p_rwkv"][l], Sst[:].rearrange("p a b -> p (a b)"))
            if do_sample and 3 <= stage < 100:
                for s in range(NS):
                    P.dma("sp", O["s_ssm"][l, s], [hcar, hcar1][s][:].rearrange("p a b -> p (a b)"))
                    P.dma("sp", O["s_rwkv"][l, s], [Sst, Sst1][s][:].rearrange("p a b -> p (a b)"))
        P.fence("sp", list(O.values()))
        P.emit()
        stats = dict(P.stats)
        stats["A32"] = A32.hi
        stats["A16"] = A16.hi
    return nc, stats


def _assemble(results):
    f32 = np.float32
    y_p = np.stack([results[c]["y_p"] for c in range(2)]).astype(f32)
    y_s = np.concatenate([results[c]["y_s"].reshape(NS, SQ, D) for c in range(NCORE)]).astype(f32)
    p_k = np.stack([results[c]["p_k"] for c in range(2)], 1).reshape(NL, 2, 512, 16, 64)
    p_v = np.stack([results[c]["p_v"] for c in range(2)], 1).reshape(NL, 2, 512, 16, 64)

    def ssm_unpack(a):
        a = a.reshape(a.shape[:-2] + (2, 64, 16, 2))
        a = np.moveaxis(a, -2, -4)
        a = a.reshape(a.shape[:-4] + (32, 64, 2))
        return np.ascontiguousarray(a[..., 0]), np.ascontiguousarray(a[..., 1])

    def rwkv_unpack(a):
        a = a.reshape(a.shape[:-2] + (2, 64, 4, 2, 64))
        outs = []
        for ft in range(4):
            for h2 in range(2):
                blk = a[..., h2, :, ft, h2, :]
                outs.append(np.swapaxes(blk, -1, -2))
        return np.ascontiguousarray(np.stack(outs, -3))

    def shift_unpack(a):
        return np.ascontiguousarray(a.reshape(a.shape[:-2] + (2048,)))

    pss = np.stack([results[c]["p_ssm"] for c in range(2)], 1)
    p_re, p_im = ssm_unpack(pss)
    p_rw = rwkv_unpack(np.stack([results[c]["p_rwkv"] for c in range(2)], 1))
    p_sh = shift_unpack(np.stack([results[c]["p_shift"] for c in range(2)], 1))
    s_k = np.concatenate([results[c]["s_k"].reshape(NL, NS, SQ, 16, 64) for c in range(NCORE)], 1)
    s_v = np.concatenate([results[c]["s_v"].reshape(NL, NS, SQ, 16, 64) for c in range(NCORE)], 1)
    sss = np.concatenate([results[c]["s_ssm"] for c in range(NCORE)], 1)
    s_re, s_im = ssm_unpack(sss)
    s_rw = rwkv_unpack(np.concatenate([results[c]["s_rwkv"] for c in range(NCORE)], 1))
    s_sh = shift_unpack(np.concatenate([results[c]["s_shift"] for c in range(NCORE)], 1))
    outs = (y_p, y_s, p_k, p_v, p_re, p_im, p_rw, p_sh, s_k, s_v, s_re, s_im, s_rw, s_sh)
    return tuple(np.ascontiguousarray(o, dtype=f32) for o in outs)


def kernel(**inputs):
    shared = _shared_layouts(inputs)
    in_maps = [_core_inputs(inputs, c, shared) for c in range(NCORE)]
    nc, _ = build_program()
    res = run_bass_kernel_spmd(nc, in_maps, core_ids=list(range(NCORE)))
    return _assemble(res.results)
```

```python
import numpy as np
from concourse.bass_utils import run_bass_kernel_spmd
import concourse.bass as bass
import concourse.mybir as mybir

F32 = mybir.dt.float32
BF16 = mybir.dt.bfloat16
ALU = mybir.AluOpType
AF = mybir.ActivationFunctionType
AX = mybir.AxisListType

ENGS = ("pe", "act", "dve", "pool", "sp")


def _region(ap):
    t = ap.tensor
    pat = ap.ap
    off = int(ap.offset)
    if isinstance(t, bass.DRamTensorHandle):
        ext = 1
        for st, cn in pat:
            ext += (cn - 1) * abs(st)
        return (t.name, 0, 1, off, off + ext)
    row = pat[0][0]
    if row == 0:
        row = 1 << 40
    p0 = off // row
    f0 = off % row
    ext = 1
    for st, cn in pat[1:]:
        ext += (cn - 1) * abs(st)
    return (t.name, p0, p0 + pat[0][1], f0, f0 + ext)


class Op:
    __slots__ = ("eng", "fn", "dma", "deps", "pos", "signal", "sigval", "vc", "dk",
                 "waits", "sem", "semval", "idx", "raw")


class Prog:
    def __init__(self, nc, n_dma_sems=16, same_engine_sync=True):
        self.nc = nc
        self.ops = []
        self.acc = {}
        self.n_dma_sems = n_dma_sems
        self.same_engine_sync = same_engine_sync

    def op(self, eng, fn, reads=(), writes=(), dma=False):
        o = Op()
        o.eng = eng
        o.fn = fn
        o.dma = dma
        o.idx = len(self.ops)
        o.signal = False
        o.raw = set()
        deps = set()
        for ap in reads:
            self._access(o, ap, False, deps)
        for ap in writes:
            self._access(o, ap, True, deps)
        deps.discard(o.idx)
        o.deps = deps
        self.ops.append(o)
        return o

    def _access(self, o, ap, is_w, deps):
        name, p0, p1, f0, f1 = _region(ap)
        lst = self.acc.setdefault(name, [])
        keep = []
        psum = name.startswith("pp_")
        if psum:
            esz = mybir.dt.size(ap.dtype)
            b0, b1 = (f0 * esz) // 2048, ((f1 * esz) - 1) // 2048
            f0, f1 = (b0 * 2048) // esz, ((b1 + 1) * 2048) // esz
            p0, p1 = 0, 128
        for e in lst:
            ov = not (e[3] <= p0 or p1 <= e[2] or e[5] <= f0 or f1 <= e[4])
            if ov and (is_w or e[1]):
                deps.add(e[0])
                if (not is_w) and e[1]:
                    o.raw.add(e[0])
            if psum and (not is_w) and (not e[1]) and e[6] != o.eng and e[0] != o.idx:
                if not (e[8] < b0 or b1 < e[7]):
                    deps.add(e[0])
            if e[0] == o.idx:
                keep.append(e)
                continue
            contained = (p0 <= e[2] and e[3] <= p1 and f0 <= e[4] and e[5] <= f1)
            if contained and is_w:
                continue
            if contained and (not is_w) and (not e[1]) and e[6] == o.eng and not o.dma \
                    and not self.ops[e[0]].dma:
                continue
            keep.append(e)
        keep.append([o.idx, is_w, p0, p1, f0, f1, o.eng] + ([b0, b1] if psum else [0, 0]))
        self.acc[name] = keep

    def emit(self):
        nc = self.nc
        ops = self.ops
        known_vc = {e: {x: 0 for x in ENGS} for e in ENGS}
        known_dk = {e: {} for e in ENGS}
        count = {e: 0 for e in ENGS}
        dma_n = {e: 0 for e in ENGS}
        dma_last = {}
        by_pos = {e: [] for e in ENGS}
        for o in ops:
            E = o.eng
            kv = known_vc[E]
            kd = known_dk[E]
            waits = {}
            for di in sorted(o.deps):
                d = ops[di]
                if d.dma:
                    key = d.sem
                    if kd.get(key, 0) >= d.semval:
                        continue
                    waits[("d",) + key] = max(waits.get(("d",) + key, 0), d.semval)
                    d.signal = True
                    kd[key] = d.semval
                else:
                    Ed = d.eng
                    if kv[Ed] >= d.pos:
                        continue
                    if Ed == E and (E == "pe" or not self.same_engine_sync):
                        continue
                    waits[("c", Ed)] = max(waits.get(("c", Ed), 0), d.pos)
                    kv[Ed] = d.pos
                for x in ENGS:
                    if d.vc[x] > kv[x]:
                        kv[x] = d.vc[x]
                for k2, v2 in d.dk.items():
                    if kd.get(k2, 0) < v2:
                        kd[k2] = v2
            if o.dma:
                n = dma_n[E]
                dma_n[E] += 1
                key = (E, n % self.n_dma_sems)
                prev = dma_last.get(key)
                if prev is not None and kd.get(key, 0) < prev.semval:
                    waits[("d",) + key] = max(waits.get(("d",) + key, 0), prev.semval)
                    kd[key] = prev.semval
                    prev.signal = True
                o.sem = key
                o.semval = 16 * (n // self.n_dma_sems + 1)
                dma_last[key] = o
                o.pos = 0
                o.vc = dict(kv)
                o.dk = dict(kd)
            else:
                count[E] += 1
                o.pos = count[E]
                o.vc = dict(kv)
                o.vc[E] = o.pos
                o.dk = dict(kd)
                by_pos[E].append(o)
            o.waits = waits
        for o in ops:
            for k, v in o.waits.items():
                if k[0] == "c":
                    by_pos[k[1]][v - 1].signal = True
        for E in ENGS:
            s = 0
            for o in by_pos[E]:
                if o.signal:
                    s += 1
                o.sigval = s
        self.stats = {e: count[e] for e in ENGS}
        self.stats["dma"] = dict(dma_n)
        self.stats["waits"] = sum(len(o.waits) for o in ops)

        import contextlib
        with contextlib.ExitStack() as st:
            csem = {e: st.enter_context(nc.semaphore("c_" + e)) for e in ENGS}
            dsem = {}
            for e in ENGS:
                if dma_n[e]:
                    for i in range(min(self.n_dma_sems, dma_n[e])):
                        dsem[(e, i)] = st.enter_context(nc.semaphore("d_%s_%d" % (e, i)))
            block = st.enter_context(nc.Block())

            def run(E, eng):
                for o in ops:
                    if o.eng != E:
                        continue
                    for k, v in o.waits.items():
                        if k[0] == "c":
                            eng.wait_ge(csem[k[1]], by_pos[k[1]][v - 1].sigval)
                        else:
                            eng.wait_ge(dsem[(k[1], k[2])], v)
                    if o.fn is None:
                        continue
                    ins = o.fn(eng)
                    if o.dma:
                        ins.then_inc(dsem[o.sem], 16)
                    elif o.signal:
                        ins.then_inc(csem[E], 1)

            @block.tensor
            def _(eng):
                run("pe", eng)

            @block.scalar
            def _(eng):
                run("act", eng)

            @block.vector
            def _(eng):
                run("dve", eng)

            @block.gpsimd
            def _(eng):
                run("pool", eng)

            @block.sync
            def _(eng):
                run("sp", eng)

    def dma(self, q, out, in_, **kw):
        return self.op(q, lambda e: e.dma_start(out=out, in_=in_, **kw), [in_], [out], dma=True)

    def mm(self, out, lhsT, rhs, start=True, stop=True, **kw):
        return self.op("pe", lambda e: e.matmul(out, lhsT=lhsT, rhs=rhs, start=start, stop=stop, **kw),
                       [lhsT, rhs], [out])

    def tr(self, out, in_, ident):
        return self.op("pe", lambda e: e.transpose(out, in_, ident), [in_, ident], [out])

    def act(self, out, in_, func, bias=None, scale=1.0, accum_out=None, eng="act"):
        rd = [in_]
        wr = [out]
        kw = {}
        if bias is not None:
            kw["bias"] = bias
            if not isinstance(bias, (int, float)):
                rd.append(bias)
        if not isinstance(scale, (int, float)):
            rd.append(scale)
        if accum_out is not None:
            kw["accum_out"] = accum_out
            wr.append(accum_out)
        return self.op(eng, lambda e: e.activation(out=out, in_=in_, func=func, scale=scale, **kw), rd, wr)

    def tt(self, eng, out, in0, in1, op):
        return self.op(eng, lambda e: e.tensor_tensor(out=out, in0=in0, in1=in1, op=op), [in0, in1], [out])

    def ts(self, eng, out, in0, s1, op0, s2=None, op1=None, accum_out=None):
        rd = [in0]
        wr = [out]
        if not isinstance(s1, (int, float)):
            rd.append(s1)
        if s2 is not None and not isinstance(s2, (int, float)):
            rd.append(s2)
        kw = {}
        if op1 is not None:
            kw["op1"] = op1
        if accum_out is not None:
            kw["accum_out"] = accum_out
            wr.append(accum_out)
        return self.op(eng, lambda e: e.tensor_scalar(out=out, in0=in0, scalar1=s1, scalar2=s2, op0=op0, **kw),
                       rd, wr)

    def stt(self, out, in0, scalar, in1, op0, op1, eng="dve"):
        rd = [in0, in1]
        if not isinstance(scalar, (int, float)):
            rd.append(scalar)
        return self.op(eng, lambda e: e.scalar_tensor_tensor(out=out, in0=in0, scalar=scalar, in1=in1,
                                                             op0=op0, op1=op1), rd, [out])

    def scan(self, out, data0, data1, initial, op0=None, op1=None):
        rd = [data0, data1]
        if not isinstance(initial, (int, float)):
            rd.append(initial)
        op0 = op0 or ALU.mult
        op1 = op1 or ALU.add
        return self.op("dve", lambda e: e.tensor_tensor_scan(out=out, data0=data0, data1=data1,
                                                             initial=initial, op0=op0, op1=op1), rd, [out])

    def copy(self, eng, out, in_):
        if eng == "act":
            return self.act(out, in_, AF.Copy)
        return self.op(eng, lambda e: e.tensor_copy(out=out, in_=in_), [in_], [out])

    def memset(self, eng, out, val):
        return self.op(eng, lambda e: e.memset(out, val), [], [out])

    def recip(self, out, in_):
        return self.op("dve", lambda e: e.reciprocal(out=out, in_=in_), [in_], [out])

    def reduce(self, out, in_, op, axis=None, eng="dve"):
        axis = axis or AX.X
        return self.op(eng, lambda e: e.tensor_reduce(out=out, in_=in_, axis=axis, op=op), [in_], [out])

    def fence(self, q, reads):
        return self.op(q, None, [], list(reads))

import math

D = 2048
DIN = 7680
NL = 2
SEQ = 4096
TT = 256
NS = 2
SQ = 16
NCORE = 8
ALPHA = (2.0 * NL) ** 0.25
KAPPA = math.exp(-0.5)
NEG = -30000.0
CW = 256
NCH_IN = DIN // CW
NCH_OUT = D // CW
NCH = NCH_IN + NCH_OUT
GN_EPS = 64e-5
LN_EPS = 1e-5
NPV = 56
DBG_NOSH = False

_CST = {}
_off = 0
for _n, _w in [("identf", 128), ("mus128", 128), ("mls128", 128), ("mui128", 64),
               ("mus32", 32), ("mls32", 32), ("mui32", 16), ("cmask64", TT), ("cmask16", NS * SQ),
               ("zm", 2), ("ob", 128), ("bneg", 256), ("maskB", 64), ("bmask", 8)]:
    _CST[_n] = (_off, _w)
    _off += _w
NCST = _off


def _vec4(v):
    return np.ascontiguousarray(np.asarray(v, np.float32).reshape(4, 128).T)


def _consts():
    c = np.zeros((128, NCST), np.float32)

    def put(name, arr):
        o, w = _CST[name]
        c[:arr.shape[0], o:o + w] = arr

    put("identf", np.eye(128, dtype=np.float32))
    for C, sfx in ((64, "128"), (16, "32")):
        Z = 2 * C
        p = np.arange(Z)[:, None]
        q = np.arange(Z)[None, :]
        same = (p // C) == (q // C)
        put("mus" + sfx, (same & ((p % C) < (q % C))).astype(np.float32))
        put("mls" + sfx, (same & ((p % C) > (q % C))).astype(np.float32))
        t = np.arange(C)[None, :]
        put("mui" + sfx, ((p % C) <= t).astype(np.float32))
    cm = np.ones((128, TT), np.float32)
    cm[:, ::64] = 0.0
    put("cmask64", cm)
    cm = np.ones((128, NS * SQ), np.float32)
    cm[:, ::16] = 0.0
    put("cmask16", cm)
    pp = np.arange(128)
    put("zm", np.stack([(pp < 64), (pp >= 64)], 1).astype(np.float32))
    put("ob", ((pp[:, None] // 64) == (pp[None, :] // 64)).astype(np.float32))
    bn = np.zeros((128, 256), np.float32)
    bn[:64, 192:] = NEG
    put("bneg", bn)
    mb = np.zeros((128, 64), np.float32)
    mb[64:, :] = NEG
    put("maskB", mb)
    gl = pp // 16
    bm = np.zeros((128, 4, 2), np.float32)
    for pm in range(4):
        for g2 in range(2):
            bm[:, pm, g2] = (gl == 2 * pm + g2)
    put("bmask", bm.reshape(128, 8))
    return c


def _shared_layouts(inp):
    f = lambda k: np.asarray(inp[k], np.float32)
    out = {}
    pv = np.zeros((NL, 128, NPV), np.float32)
    for l in range(NL):
        for i in range(5):
            pv[l, :, i * 4:(i + 1) * 4] = _vec4(f("rwkv_mu")[l, i])
        for j, k in enumerate(["rwkv_w0", "rwkv_a0", "rwkv_k_k", "rwkv_k_a", "rwkv_r_k", "rwkv_lnx_g",
                               "rwkv_lnx_b", "ssm_d", "ssm_b_glu"]):
            pv[l, :, 20 + 4 * j:24 + 4 * j] = _vec4(f(k)[l].reshape(-1))
    out["pv"] = pv
    def pair(a):
        return np.ascontiguousarray(a.reshape(16, 2, 64).transpose(1, 2, 0).reshape(128, 16))
    sp = np.zeros((NL, 128, 3, 16), np.float32)
    sc = np.zeros((NL, 128, 5, 4, 64), np.float32)
    cx = np.zeros((NL, 2, 128, 16, 128), np.float32)
    for l in range(NL):
        lr, li, ld = f("ssm_lam_re")[l], f("ssm_lam_im")[l], f("ssm_log_dt")[l]
        sp[l, :, 0] = pair(lr)
        sp[l, :, 1] = pair(li)
        sp[l, :, 2] = pair(np.repeat(ld[:, None], 64, 1))
        def ch(a):
            return np.repeat(a.reshape(4, 8, 1, 64), 16, 2).transpose(1, 2, 0, 3).reshape(128, 4, 64)
        sc[l, :, 0] = ch(lr)
        sc[l, :, 1] = ch(li)
        sc[l, :, 2] = ch(np.repeat(ld[:, None], 64, 1))
        for j, k in ((3, "ssm_b_re"), (4, "ssm_b_im")):
            b = f(k)[l]
            sc[l, :, j] = b.reshape(4, 8, 64, 16).transpose(1, 3, 0, 2).reshape(128, 4, 64)
        for j, k in ((0, "ssm_c_re"), (1, "ssm_c_im")):
            cc = f(k)[l]
            for pi in range(16):
                for g2 in range(2):
                    g = 2 * pi + g2
                    glo = g % 8
                    cx[l, j, g2 * 64:(g2 + 1) * 64, pi, glo * 16:(glo + 1) * 16] = cc[g].T
    out["ssm_pair"] = sp
    out["ssm_ch"] = sc.reshape(NL, 128, 5, 256)
    out["cext"] = cx.reshape(NL, 2, 128, 2048)
    tab = f("att_rel_bias")
    i = np.arange(128)[:, None]
    jj = np.arange(256)[None, :]
    idxA = np.minimum(256 + i - jj, 256)
    idxB = np.minimum(320 + (i - 64) - jj, 256)
    idx = np.where(i < 64, idxA, idxB)
    idx = np.clip(idx, 0, 256)
    braw = tab[:, :, idx]
    out["braw"] = np.ascontiguousarray(braw.transpose(0, 2, 1, 3))
    out["bconst"] = np.ascontiguousarray(np.repeat(tab[:, None, :, 256], 128, 1))
    lg = np.zeros((NL, 128, 2, D), np.float32)
    lg[:, :, 0, :] = f("ln_g")[:, None, :]
    lg[:, :, 1, :] = f("ln_b")[:, None, :]
    out["lngb"] = lg
    out["cst"] = _consts()
    for k in ("w_in", "w_out", "ssm_w_glu", "rwkv_w1", "rwkv_w2", "rwkv_a1", "rwkv_a2"):
        out[k] = np.ascontiguousarray(f(k))
    return out


def _core_inputs(inp, core, shared):
    f = lambda k: np.asarray(inp[k], np.float32)
    m = dict(shared)
    if core < 2:
        m["xp"] = np.ascontiguousarray(f("x_prompt")[core])
    else:
        m["xp"] = np.zeros((SEQ, D), np.float32)
    s0 = NS * core
    m["xs"] = np.ascontiguousarray(f("x_sample")[s0:s0 + NS].reshape(NS * SQ, D))
    m["ck"] = np.ascontiguousarray(f("cache_att_k")[:, s0:s0 + NS].reshape(NL, NS, 512, 1024))
    m["cv"] = np.ascontiguousarray(f("cache_att_v")[:, s0:s0 + NS].reshape(NL, NS, 512, 1024))
    hs = np.zeros((NL, NS, 128, 16, 2), np.float32)
    for j, k in ((0, "state_ssm_re"), (1, "state_ssm_im")):
        a = f(k)[:, s0:s0 + NS]
        hs[..., j] = a.reshape(NL, NS, 16, 2, 64).transpose(0, 1, 3, 4, 2).reshape(NL, NS, 128, 16)
    m["hss"] = hs.reshape(NL, NS, 128, 32)
    sr = f("state_rwkv")[:, s0:s0 + NS]
    z = np.zeros((NL, NS, 2, 64, 4, 2, 64), np.float32)
    for ft in range(4):
        for h2 in range(2):
            z[:, :, h2, :, ft, h2, :] = sr[:, :, 2 * ft + h2].transpose(0, 1, 3, 2)
    m["srw"] = z.reshape(NL, NS, 128, 512)
    sh = f("state_rwkv_shift")[:, s0:s0 + NS]
    m["ssh"] = np.ascontiguousarray(sh.reshape(NL, NS, 16, 128).transpose(0, 1, 3, 2))
    return m

class Arena:
    def __init__(self, t, size):
        self.t = t
        self.size = size
        self.top = 0
        self.hi = 0

    def alloc(self, *shape, parts=128):
        n = 1
        for s in shape:
            n *= s
        off = self.top
        self.top += n
        self.hi = max(self.hi, self.top)
        assert self.top <= self.size, ("arena overflow", self.top, self.size)
        ap = self.t[0:parts, off:off + n]
        if len(shape) == 2:
            ap = ap.rearrange("p (a b) -> p a b", a=shape[0])
        elif len(shape) == 3:
            ap = ap.rearrange("p (a b c) -> p a b c", a=shape[0], b=shape[1])
        elif len(shape) == 4:
            ap = ap.rearrange("p (a b c d) -> p a b c d", a=shape[0], b=shape[1], c=shape[2])
        return ap

    def reset(self, top=0):
        self.top = top


def cview(cst, name, parts=128, w=None):
    o, ww = _CST[name]
    return cst[0:parts, o:o + (w or ww)]


class _Stop(Exception):
    pass


def build_program(seq=SEQ, do_prompt=True, do_sample=True, nlayers=NL, stage=99, ntiles=None):
    import contextlib
    nc = bass.Bass("TRN2", target_bir_lowering=False)
    NT = seq // TT if ntiles is None else ntiles

    def din(name, shape, dt=F32):
        return nc.dram_tensor(name, list(shape), dt, kind="ExternalInput").ap()

    def dout(name, shape, dt=F32):
        return nc.dram_tensor(name, list(shape), dt, kind="ExternalOutput").ap()

    I = {}
    I["xp"] = din("xp", [SEQ, D])
    I["xs"] = din("xs", [NS * SQ, D])
    I["ck"] = din("ck", [NL, NS, 512, 1024])
    I["cv"] = din("cv", [NL, NS, 512, 1024])
    I["hss"] = din("hss", [NL, NS, 128, 32])
    I["srw"] = din("srw", [NL, NS, 128, 512])
    I["ssh"] = din("ssh", [NL, NS, 128, 16])
    I["w_in"] = din("w_in", [NL, D, DIN])
    I["w_out"] = din("w_out", [NL, D, D])
    I["pv"] = din("pv", [NL, 128, NPV])
    I["ssm_pair"] = din("ssm_pair", [NL, 128, 3, 16])
    I["ssm_ch"] = din("ssm_ch", [NL, 128, 5, 256])
    I["cext"] = din("cext", [NL, 2, 128, 2048])
    I["ssm_w_glu"] = din("ssm_w_glu", [NL, 512, 512])
    I["braw"] = din("braw", [NL, 128, 16, 256])
    I["bconst"] = din("bconst", [NL, 128, 16])
    I["rwkv_w1"] = din("rwkv_w1", [NL, 512, 64])
    I["rwkv_w2"] = din("rwkv_w2", [NL, 64, 512])
    I["rwkv_a1"] = din("rwkv_a1", [NL, 512, 64])
    I["rwkv_a2"] = din("rwkv_a2", [NL, 64, 512])
    I["lngb"] = din("lngb", [NL, 128, 2, D])
    I["cst"] = din("cst", [128, NCST])

    O = {}
    O["y_p"] = dout("y_p", [SEQ, D])
    O["y_s"] = dout("y_s", [NS * SQ, D])
    O["p_k"] = dout("p_k", [NL, 512, 1024])
    O["p_v"] = dout("p_v", [NL, 512, 1024])
    O["p_ssm"] = dout("p_ssm", [NL, 128, 32])
    O["p_rwkv"] = dout("p_rwkv", [NL, 128, 512])
    O["p_shift"] = dout("p_shift", [NL, 1, 2048])
    O["s_k"] = dout("s_k", [NL, NS * SQ, 1024])
    O["s_v"] = dout("s_v", [NL, NS * SQ, 1024])
    O["s_ssm"] = dout("s_ssm", [NL, NS, 128, 32])
    O["s_rwkv"] = dout("s_rwkv", [NL, NS, 128, 512])
    O["s_shift"] = dout("s_shift", [NL, NS, 1, 2048])
    wsc = nc.dram_tensor("wsc", [NL, NCH, 128, 16 * CW], BF16, kind="Internal").ap()
    y0p = nc.dram_tensor("y0p", [SEQ, D], F32, kind="Internal").ap()
    y0s = nc.dram_tensor("y0s", [NS * SQ, D], F32, kind="Internal").ap()

    ST = contextlib.ExitStack()

    def sb(name, shape, dt=F32):
        return ST.enter_context(nc.sbuf_tensor("sb_" + name, list(shape), dt))

    def pst(name, shape, dt=F32):
        return ST.enter_context(nc.psum_tensor("pp_" + name, list(shape), dt))

    with ST:
        P = Prog(nc)
        cst = sb("cst", [128, NCST])
        identb = sb("identb", [128, 128], BF16)
        obb = sb("obb", [128, 128], BF16)
        musb = {64: sb("mus64", [128, 128], BF16), 16: sb("mus16", [32, 32], BF16)}
        mlsb = {64: sb("mls64", [128, 128], BF16), 16: sb("mls16", [32, 32], BF16)}
        muib = {64: sb("mui64", [128, 64], BF16), 16: sb("mui16", [32, 16], BF16)}
        maskBb = sb("maskBb", [128, 64], BF16)
        xT = sb("xT", [128, 16, TT], BF16)
        wbuf = [sb("wbuf0", [128, 16, CW], BF16), sb("wbuf1", [128, 16, CW], BF16)]
        mixT = sb("mixT", [128, 16, TT], BF16)
        pv = sb("pv", [128, NPV])
        pvx = sb("pvx", [128, 8])
        tabc = sb("tabc", [128, 16, 128])
        tabs = sb("tabs", [128, 16, 128])
        rho = sb("rho", [128, 16])
        bext = [sb("bext_re", [128, 16, 128], BF16), sb("bext_im", [128, 16, 128], BF16)]
        cext = [sb("cext_re", [128, 16, 128], BF16), sb("cext_nim", [128, 16, 128], BF16)]
        wglu = sb("wglu", [128, 4, 512], BF16)
        hcar = sb("hcar", [128, 16, 2])
        KT = sb("KT", [128, 8, 768], BF16)
        VT = sb("VT", [128, 6, 1024], BF16)
        btab = sb("btab", [128, 16, 256], BF16)
        lw1 = sb("lw1", [128, 4, 64], BF16)
        la1 = sb("la1", [128, 4, 64], BF16)
        lw2 = sb("lw2", [64, 512], BF16)
        la2 = sb("la2", [64, 512], BF16)
        Sst = sb("Sst", [128, 4, 128])
        shc = sb("shc", [128, 16])
        hcar1 = sb("hcar1", [128, 16, 2])
        Sst1 = sb("Sst1", [128, 4, 128])
        shc1 = sb("shc1", [128, 16])
        A32N = 8448
        A16N = 30400
        ar32 = sb("ar32", [128, A32N])
        ar16 = sb("ar16", [128, A16N], BF16)
        A32 = Arena(ar32, A32N)
        A16 = Arena(ar16, A16N)
        psA = pst("psA", [128, 512])
        psI = [pst("psI0", [128, 512]), pst("psI1", [128, 512])]
        psS = [pst("psS0", [128, 1024]), pst("psS1", [128, 1024])]
        psB = pst("psB", [128, 1024], BF16)

        identf = cview(cst, "identf")

        def chk(n):
            if stage == n:
                raise _Stop()

        P.dma("sp", cst[:], I["cst"][:, :])
        P.copy("dve", identb[:], identf)
        P.copy("dve", obb[:], cview(cst, "ob"))
        P.copy("dve", musb[64][:], cview(cst, "mus128"))
        P.copy("dve", mlsb[64][:], cview(cst, "mls128"))
        P.copy("dve", muib[64][:], cview(cst, "mui128"))
        P.copy("dve", musb[16][:], cview(cst, "mus32", 32))
        P.copy("dve", mlsb[16][:], cview(cst, "mls32", 32))
        P.copy("dve", muib[16][:], cview(cst, "mui32", 32))
        P.copy("dve", maskBb[:], cview(cst, "maskB"))
        zm = cview(cst, "zm")

        cvt_stage = [A16.alloc(16, CW), A16.alloc(16, CW)]
        for l in range(nlayers):
            for c in range(NCH):
                if c < NCH_IN:
                    src = I["w_in"][l, :, c * CW:(c + 1) * CW]
                else:
                    src = I["w_out"][l, :, (c - NCH_IN) * CW:(c - NCH_IN + 1) * CW]
                stg = cvt_stage[(l * NCH + c) % 2]
                P.dma("pool", stg, src.rearrange("(kt p) c -> p kt c", p=128))
                P.dma("sp", wsc[l, c].rearrange("p (kt c) -> p kt c", kt=16), stg)

        wslot = [0]

        def load_w(l, c):
            wb = wbuf[wslot[0]]
            wslot[0] ^= 1
            P.dma("sp", wb[:], wsc[l, c].rearrange("p (kt c) -> p kt c", kt=16))
            return wb

        evq = [0]

        def ev_eng():
            evq[0] ^= 1
            return "act" if evq[0] else "dve"

        pin = [0]

        def fm_proj(wb, T, evac):
            for c2 in range(2):
                ps = psI[pin[0]]
                pin[0] ^= 1
                for kt in range(16):
                    P.mm(ps[:, 0:T], wb[:, kt, c2 * 128:(c2 + 1) * 128], xT[:, kt, 0:T],
                         start=(kt == 0), stop=(kt == 15))
                evac(ps[:, 0:T], c2)

        def tm_proj(wb, T, evac):
            for t0 in range(0, T, 128):
                n = min(128, T - t0)
                ps = psI[pin[0]]
                pin[0] ^= 1
                for kt in range(16):
                    P.mm(ps[0:n, 0:CW], xT[:, kt, t0:t0 + n], wb[:, kt, :], start=(kt == 0), stop=(kt == 15))
                evac(ps[0:n, 0:CW], t0, n)

        def layer_setup(l):
            A32.reset()
            A16.reset()
            P.dma("sp", pv[:], I["pv"][l])
            P.ts("dve", pvx[:, 0:4], pv[:, 32:36], -1.0, ALU.mult, 1.0, ALU.add)
            P.dma("pool", wglu[:], I["ssm_w_glu"][l].rearrange("(kt p) c -> p kt c", p=128))
            P.dma("pool", lw1[:], I["rwkv_w1"][l].rearrange("(kt p) c -> p kt c", p=128))
            P.dma("pool", la1[:], I["rwkv_a1"][l].rearrange("(kt p) c -> p kt c", p=128))
            P.dma("pool", lw2[:], I["rwkv_w2"][l])
            P.dma("pool", la2[:], I["rwkv_a2"][l])
            P.dma("pool", cext[0][:], I["cext"][l, 0].rearrange("p (a b) -> p a b", a=16))
            P.dma("pool", cext[1][:], I["cext"][l, 1].rearrange("p (a b) -> p a b", a=16))
            P.ts("pool", cext[1][:], cext[1][:], -1.0, ALU.mult)
            braw = A32.alloc(16, 256)
            bco = A32.alloc(16)
            P.dma("sp", braw, I["braw"][l])
            P.dma("sp", bco, I["bconst"][l])
            P.tt("dve", braw, braw, bco.unsqueeze(2).to_broadcast([128, 16, 256]), ALU.subtract)
            P.tt("dve", btab[:], braw, cview(cst, "bneg").unsqueeze(1).to_broadcast([128, 16, 256]), ALU.add)
            A32.reset()

            def abq(lr, li, ldt, shape, want_q):
                al = lambda: A32.alloc(*shape)
                dt = al()
                P.act(dt, ldt, AF.Exp)
                e = al()
                P.tt("dve", e, lr, dt, ALU.mult)
                P.act(e, e, AF.Exp)
                th = al()
                P.tt("dve", th, li, dt, ALU.mult)
                kf = al()
                P.ts("dve", kf, th, 1.0 / (2 * math.pi), ALU.mult)
                P.ts("dve", kf, kf, 12582912.0, ALU.add)
                P.ts("dve", kf, kf, 12582912.0, ALU.subtract)
                r = al()
                P.stt(r, kf, -6.28125, th, ALU.mult, ALU.add)
                P.stt(r, kf, -(2 * math.pi - 6.28125), r, ALU.mult, ALU.add)
                P.ts("dve", r, r, math.pi, ALU.min, -math.pi, ALU.max)
                sn = al()
                P.act(sn, r, AF.Sin)
                ab = al()
                P.act(ab, r, AF.Abs)
                P.ts("dve", ab, ab, -1.0, ALU.mult, math.pi / 2, ALU.add)
                cs = al()
                P.act(cs, ab, AF.Sin)
                res = {"rho": e, "cos": cs, "sin": sn}
                if want_q:
                    abr = al()
                    abi = al()
                    P.tt("dve", abr, e, cs, ALU.mult)
                    P.tt("dve", abi, e, sn, ALU.mult)
                    P.ts("dve", abr, abr, -1.0, ALU.add)
                    den = al()
                    t2 = al()
                    P.tt("dve", den, lr, lr, ALU.mult)
                    P.tt("dve", t2, li, li, ALU.mult)
                    P.tt("dve", den, den, t2, ALU.add)
                    P.recip(den, den)
                    qr = al()
                    qi = al()
                    P.tt("dve", qr, abr, lr, ALU.mult)
                    P.tt("dve", t2, abi, li, ALU.mult)
                    P.tt("dve", qr, qr, t2, ALU.add)
                    P.tt("dve", qr, qr, den, ALU.mult)
                    P.tt("dve", qi, abi, lr, ALU.mult)
                    P.tt("dve", t2, abr, li, ALU.mult)
                    P.tt("dve", qi, qi, t2, ALU.subtract)
                    P.tt("dve", qi, qi, den, ALU.mult)
                    res["qr"] = qr
                    res["qi"] = qi
                return res

            spr = A32.alloc(3, 16)
            P.dma("sp", spr, I["ssm_pair"][l])
            r1 = abq(spr[:, 0, :], spr[:, 1, :], spr[:, 2, :], (16,), False)
            P.copy("dve", rho[:], r1["rho"])
            P.copy("dve", tabc[:, :, 0], r1["cos"])
            P.copy("dve", tabs[:, :, 0], r1["sin"])
            n = 1
            while n < 128:
                tc0 = tabc[:, :, 0:n]
                ts0 = tabs[:, :, 0:n]
                Cn = tabc[:, :, n - 1:n].to_broadcast([128, 16, n])
                Sn = tabs[:, :, n - 1:n].to_broadcast([128, 16, n])
                t1 = A32.alloc(16, n)
                t2 = A32.alloc(16, n)
                P.tt("dve", t1, tc0, Cn, ALU.mult)
                P.tt("dve", t2, ts0, Sn, ALU.mult)
                P.tt("dve", tabc[:, :, n:2 * n], t1, t2, ALU.subtract)
                P.tt("dve", t1, ts0, Cn, ALU.mult)
                P.tt("dve", t2, tc0, Sn, ALU.mult)
                P.tt("dve", tabs[:, :, n:2 * n], t1, t2, ALU.add)
                n *= 2
            A32.reset()
            sch = A32.alloc(5, 256)
            P.dma("sp", sch, I["ssm_ch"][l])
            r2 = abq(sch[:, 0, :], sch[:, 1, :], sch[:, 2, :], (256,), True)
            bre = A32.alloc(4, 64)
            bim = A32.alloc(4, 64)
            t3 = A32.alloc(256)
            brf = bre.rearrange("p a b -> p (a b)")
            bif = bim.rearrange("p a b -> p (a b)")
            P.tt("dve", brf, r2["qr"], sch[:, 3, :], ALU.mult)
            P.tt("dve", t3, r2["qi"], sch[:, 4, :], ALU.mult)
            P.tt("dve", brf, brf, t3, ALU.subtract)
            P.tt("dve", bif, r2["qr"], sch[:, 4, :], ALU.mult)
            P.tt("dve", t3, r2["qi"], sch[:, 3, :], ALU.mult)
            P.tt("dve", bif, bif, t3, ALU.add)
            bmo = _CST["bmask"][0]
            for pi in range(16):
                for g2 in range(2):
                    msk = cst[:, bmo + (pi % 4) * 2 + g2: bmo + (pi % 4) * 2 + g2 + 1]
                    P.ts("dve", bext[0][:, pi, g2 * 64:(g2 + 1) * 64], bre[:, pi // 4, :], msk, ALU.mult)
                    P.ts("pool", bext[1][:, pi, g2 * 64:(g2 + 1) * 64], bim[:, pi // 4, :], msk, ALU.mult)
            A32.reset()

        def load_xT(src, T):
            for t0 in range(0, T, 128):
                n = min(128, T - t0)
                top = A32.top
                xs = A32.alloc(D)
                P.dma("sp", xs[0:n, :], src[t0:t0 + n, :])
                for g in range(4):
                    for j in range(4):
                        kt = g * 4 + j
                        P.tr(psA[:, j * 128:j * 128 + n], xs[0:n, kt * 128:(kt + 1) * 128], identf[0:n, 0:n])
                    P.copy(ev_eng(), xT[:, g * 4:(g + 1) * 4, t0:t0 + n],
                           psA[:].rearrange("p (a b) -> p a b", a=4)[:, :, 0:n])
                A32.reset(top)

        def ssm_phase(l, T, segs, gs_out):
            ussm = A16.alloc(4, T)
            gs = A16.alloc(4, T)

            def ev_u(ci):
                def f(ps, c2):
                    P.copy(ev_eng(), ussm[:, ci * 2 + c2, :], ps)
                return f

            def ev_g(ci):
                def f(ps, c2):
                    P.act(gs[:, ci * 2 + c2, :], ps, AF.Silu)
                return f

            for ci in range(2):
                fm_proj(load_w(l, ci), T, ev_u(ci))
            for ci in range(2):
                fm_proj(load_w(l, 2 + ci), T, ev_g(ci))
            z32 = A32.alloc(4, T)
            zbf = A16.alloc(4, T)
            for ct in range(4):
                pY = psS[0][:, 512:512 + T]
                items = []
                for (c0, ncol, car) in segs:
                    for s0 in range(c0, c0 + ncol, 128):
                        ns = min(128, c0 + ncol - s0)
                        for pm in range(4):
                            items.append(dict(s0=s0, ns=ns, pm=pm, pi=ct * 4 + pm, car=car))
                base32, base16 = A32.top, A16.top

                def stA(n, it):
                    pZt = (psS[1][:, 0:256], psS[1][:, 512:768], psA[:, 0:256])[n % 3]
                    pZ = pZt.rearrange("p (a b) -> p a b", a=2)[:, :, 0:it["ns"]]
                    it["pZ"] = pZ
                    u_ = ussm[:, ct, it["s0"]:it["s0"] + it["ns"]]
                    P.mm(pZ[:, 0, :], bext[0][:, it["pi"], :], u_)
                    P.mm(pZ[:, 1, :], bext[1][:, it["pi"], :], u_)

                def stB1(n, it):
                    ns, pi, car, pZ = it["ns"], it["pi"], it["car"], it["pZ"]
                    A32.reset(base32 + (n % 3) * 1024)
                    A16.reset(base16 + (n % 3) * 256)
                    tcb = tabc[:, pi, 0:ns].unsqueeze(1).to_broadcast([128, 2, ns])
                    tsb = tabs[:, pi, 0:ns].unsqueeze(1).to_broadcast([128, 2, ns])
                    M1 = A32.alloc(2, ns)
                    M2 = A32.alloc(2, ns)
                    P.tt("dve", M1, pZ, tcb, ALU.mult)
                    P.tt("dve", M2, pZ, tsb, ALU.mult)
                    Zp = A32.alloc(2, ns)
                    P.tt("pool", Zp[:, 0, :], M1[:, 0, :], M2[:, 1, :], ALU.add)
                    P.tt("pool", Zp[:, 1, :], M1[:, 1, :], M2[:, 0, :], ALU.subtract)
                    Hp = A32.alloc(2, ns)
                    Dm = A16.alloc(2, ns)
                    it.update(M1=M1, M2=M2, Zp=Zp, Hp=Hp, Dm=Dm, tcb=tcb, tsb=tsb)

                def stB2(n, it):
                    ns, pi, car = it["ns"], it["pi"], it["car"]
                    M1, M2, Zp, Hp, Dm, tcb, tsb = (it[k_] for k_ in ("M1", "M2", "Zp", "Hp", "Dm", "tcb", "tsb"))
                    rb = rho[:, pi:pi + 1].to_broadcast([128, ns])
                    P.scan(Hp[:, 0, :], rb, Zp[:, 0, :], car[:, pi, 0:1])
                    P.scan(Hp[:, 1, :], rb, Zp[:, 1, :], car[:, pi, 1:2])
                    P.tt("dve", M1, Hp, tcb, ALU.mult)
                    P.tt("dve", M2, Hp, tsb, ALU.mult)
                    P.tt("pool", Dm[:, 0, :], M1[:, 0, :], M2[:, 1, :], ALU.subtract)
                    P.tt("pool", Dm[:, 1, :], M1[:, 1, :], M2[:, 0, :], ALU.add)
                    P.tt("pool", car[:, pi, 0:1], M1[:, 0, ns - 1:ns], M2[:, 1, ns - 1:ns], ALU.subtract)
                    P.tt("pool", car[:, pi, 1:2], M1[:, 1, ns - 1:ns], M2[:, 0, ns - 1:ns], ALU.add)
                    it["Dm"] = Dm

                def stC(n, it):
                    s0, ns, pm, pi, Dm = it["s0"], it["ns"], it["pm"], it["pi"], it["Dm"]
                    P.mm(pY[:, s0:s0 + ns], cext[0][:, pi, :], Dm[:, 0, :], start=(pm == 0), stop=False)
                    P.mm(pY[:, s0:s0 + ns], cext[1][:, pi, :], Dm[:, 1, :], start=False, stop=(pm == 3))

                nit = len(items)
                for n in range(min(2, nit)):
                    stA(n, items[n])
                stB1(0, items[0])
                for n in range(nit):
                    if n + 1 < nit:
                        stB1(n + 1, items[n + 1])
                    stB2(n, items[n])
                    if n + 2 < nit:
                        stA(n + 2, items[n + 2])
                    stC(n, items[n])
                A32.reset(base32)
                A16.reset(base16)
                top32 = A32.top
                ysb = A32.alloc(T)
                P.stt(ysb, ussm[:, ct, :], pv[:, 48 + ct:49 + ct], pY, ALU.mult, ALU.add)
                t1 = A32.alloc(T)
                P.tt("pool", t1, ysb, ysb, ALU.mult)
                P.ts("pool", t1, t1, 0.044715, ALU.mult, 1.0, ALU.add)
                P.tt("pool", t1, t1, ysb, ALU.mult)
                P.act(t1, t1, AF.Sigmoid, scale=1.5957691216057308)
                P.tt("dve", z32[:, ct, :], ysb, t1, ALU.mult)
                P.copy("pool", zbf[:, ct, :], z32[:, ct, :])
                A32.reset(top32)
            for co in range(4):
                ps = psI[pin[0]]
                pin[0] ^= 1
                for ci in range(4):
                    P.mm(ps[:, 0:T], wglu[:, ci, co * 128:(co + 1) * 128], zbf[:, ci, :], start=(ci == 0), stop=(ci == 3))
                top32 = A32.top
                sg = A32.alloc(T)
                P.act(sg, ps[:, 0:T], AF.Sigmoid, bias=pv[:, 52 + co:53 + co])
                P.tt("dve", sg, sg, z32[:, co, :], ALU.mult)
                P.tt("dve", mixT[:, co, 0:T], sg, gs[:, co, :], ALU.mult)
                A32.reset(top32)

        def att_phase(l, T, units, kcol0, vrow0, kv_out):
            qT = A16.alloc(8, T)
            ga = A16.alloc(8, T)
            kvstage = {}

            def ev_q(ci):
                def f(ps, c2):
                    P.ts("dve", qT[:, ci * 2 + c2, :], ps, 0.125, ALU.mult)
                return f

            def ev_k(ci):
                def f(ps, c2):
                    P.copy(ev_eng(), KT[:, ci * 2 + c2, kcol0:kcol0 + T], ps)
                return f

            def ev_g(ci):
                def f(ps, c2):
                    P.act(ga[:, ci * 2 + c2, :], ps, AF.Silu)
                return f

            def ev_tok(ci, dst_bf, dram_fn):
                def f(ps, t0, n):
                    if dst_bf is not None:
                        r0 = vrow0 + t0
                        P.copy("act", VT[(r0 % 128):(r0 % 128) + n, r0 // 128, ci * CW:(ci + 1) * CW], ps)
                    if dram_fn is not None:
                        key = (id(dram_fn), t0)
                        if key not in kvstage:
                            kvstage[key] = A32.alloc(1024)
                        stg = kvstage[key]
                        P.copy("dve", stg[0:n, ci * CW:(ci + 1) * CW], ps)
                        if ci == 3:
                            P.dma("sp", dram_fn(t0, n), stg[0:n, :])
                return f

            for ci in range(4):
                fm_proj(load_w(l, 4 + ci), T, ev_q(ci))
            for ci in range(4):
                wb = load_w(l, 8 + ci)
                fm_proj(wb, T, ev_k(ci))
                if kv_out[0] is not None:
                    tm_proj(wb, T, ev_tok(ci, None, kv_out[0]))
            for ci in range(4):
                tm_proj(load_w(l, 12 + ci), T, ev_tok(ci, True if kv_out[2] else None, kv_out[1]))
            for ci in range(4):
                fm_proj(load_w(l, 16 + ci), T, ev_g(ci))

            sbuf_i = [0]
            att_rot = [0]

            def run_units(units):
                items = [(u, h) for u in units for h in range(16)]
                base32, base16 = A32.top, A16.top
                stt_ = {}

                def stA(i):
                    u, h = items[i]
                    nq, ncol = u["nq"], u["ncol"]
                    ft, bp = h // 2, 64 * (h % 2)
                    ps = psS[i % 2]
                    for (r0, nr, tq0, kc0, c0, c1) in u["halves"]:
                        for (a_, b_) in ((c0, min(c1, 512)), (max(c0, 512), c1)):
                            if b_ <= a_:
                                continue
                            P.mm(ps[r0:r0 + nr, a_:b_], qT[bp:bp + 64, ft, tq0:tq0 + nr],
                                 KT[bp:bp + 64, ft, kc0 + a_ - c0:kc0 + b_ - c0], start=True, stop=False,
                                 skip_group_check=True)
                    for (a_, b_) in ((384, 512), (512, min(640, ncol))):
                        if b_ <= a_:
                            continue
                        P.mm(ps[0:nq, a_:b_], identb[0:nq, 0:nq], btab[0:nq, h, a_ - 384:b_ - 384], start=False, stop=True,
                             skip_group_check=True)
                    if u["maskB"]:
                        P.mm(ps[0:nq, 0:64], identb[0:nq, 0:nq], maskBb[0:nq, :], start=False, stop=True,
                             skip_group_check=True)
                    if u["inv"] > 0:
                        P.ts("dve", ps[0:nq, 0:u["inv"]], ps[0:nq, 0:u["inv"]], NEG, ALU.add)
                    stt_[i] = dict(ps=ps)

                def stB1(i):
                    u, h = items[i]
                    nq, ncol = u["nq"], u["ncol"]
                    nb = (ncol + 127) // 128
                    ps = stt_[i]["ps"]
                    A32.reset(base32 + (i % 3) * 4)
                    A16.reset(base16 + (i % 3) * 1280)
                    mx = A32.alloc(1)
                    P.op("dve", (lambda e, o_=mx[0:nq, :], i_=ps[0:nq, 0:ncol]: e.tensor_reduce(
                        out=o_, in_=i_, axis=AX.X, op=ALU.max, negate=True)), [ps[0:nq, 0:ncol]], [mx[0:nq, :]])
                    pb = A16.alloc(nb * 128)
                    rs = A32.alloc(1)
                    P.act(pb[0:nq, 0:ncol], ps[0:nq, 0:ncol], AF.Exp, bias=mx[0:nq, :], accum_out=rs[0:nq, :])
                    pts = A16.alloc(nb, 128)
                    stt_[i].update(pb=pb, pts=pts, nb=nb, rs=rs)

                def stB2(i):
                    u, h = items[i]
                    nq, ncol = u["nq"], u["ncol"]
                    pb, rs = stt_[i]["pb"], stt_[i]["rs"]
                    P.recip(rs[0:nq, :], rs[0:nq, :])
                    P.ts("dve", pb[0:nq, 0:ncol], pb[0:nq, 0:ncol], rs[0:nq, :], ALU.mult)

                def stC(i):
                    u, h = items[i]
                    nq, ncol = u["nq"], u["ncol"]
                    pb, nb = stt_[i]["pb"], stt_[i]["nb"]
                    pT = psB[:, 0:nb * 128].rearrange("p (a b) -> p a b", a=nb)
                    for bi in range(nb):
                        kb = min(128, ncol - bi * 128)
                        P.tr(pT[0:kb, bi, 0:nq], pb[0:nq, bi * 128:bi * 128 + kb], identb[0:nq, 0:nq])
                    pts = stt_[i]["pts"]
                    nfull = ncol // 128
                    if nfull:
                        P.copy("act", pts[:, 0:nfull, 0:nq], pT[:, 0:nfull, 0:nq])
                    if ncol % 128:
                        kb = ncol % 128
                        P.copy("act", pts[0:kb, nfull, 0:nq], pT[0:kb, nfull, 0:nq])

                def stE(i):
                    u, h = items[i]
                    nq, ncol = u["nq"], u["ncol"]
                    ft, bp = h // 2, 64 * (h % 2)
                    pts, nb = stt_[i]["pts"], stt_[i]["nb"]
                    po = psA[bp:bp + 64, (i % 4) * 128:(i % 4) * 128 + nq]
                    vb0 = u["vblk0"]
                    for bi in range(nb):
                        kb = min(128, ncol - bi * 128)
                        P.mm(po, VT[0:kb, vb0 + bi, h * 64:(h + 1) * 64], pts[0:kb, bi, 0:nq],
                             start=(bi == 0), stop=(bi == nb - 1))
                    tq = u["tq0"]
                    P.tt("dve", mixT[bp:bp + 64, 4 + ft, tq:tq + nq], po, ga[bp:bp + 64, ft, tq:tq + nq], ALU.mult)
                    del stt_[i]

                n = len(items)
                for i in range(min(2, n)):
                    stA(i)
                stB1(0)
                for i in range(n):
                    if i + 1 < n:
                        stB1(i + 1)
                    stB2(i)
                    if i + 2 < n:
                        stA(i + 2)
                    stC(i)
                    if i >= 1:
                        stE(i - 1)
                stE(n - 1)
                A32.reset(base32)
                A16.reset(base16)

            if units is not None:
                run_units(units)
            return run_units

        def rwkv_phase(l, T, C, segs, shift_out=None):
            Z = 2 * C
            NCk = T // C
            gc = A16.alloc(4, T)
            xr = A16.alloc(4, T)
            xk = A16.alloc(4, T)
            xv = A16.alloc(4, T)
            markW = A16.top
            xw = A16.alloc(4, T)
            xa = A16.alloc(4, T)
            markR = A16.top
            raw = A16.alloc(4, 4, T)
            def ev_g(ci):
                def f(ps, c2):
                    P.act(gc[:, ci * 2 + c2, :], ps, AF.Silu)
                return f

            top32_0 = A32.top
            newsh = {}
            for kind in range(4):
                for ci in range(2):
                    wb = load_w(l, 20 + kind * 2 + ci)

                    def f(ps, c2, kind=kind, ci=ci):
                        P.copy(ev_eng(), raw[:, kind, ci * 2 + c2, :], ps)
                    fm_proj(wb, T, f)
                    if shift_out is not None:
                        for si, (c0, ncol, st, sc) in enumerate(segs):
                            psr = psI[pin[0]]
                            pin[0] ^= 1
                            lc = c0 + ncol - 1
                            for kt in range(16):
                                P.mm(psr[0:1, 0:CW], xT[:, kt, lc:lc + 1], wb[:, kt, :], start=(kt == 0), stop=(kt == 15))
                            top = A32.top
                            stg = A32.alloc(CW)
                            P.copy("act", stg[0:1, :], psr[0:1, 0:CW])
                            o0 = kind * 512 + ci * CW
                            P.dma("sp", shift_out[si][:, o0:o0 + CW], stg[0:1, :])
                            A32.reset(top)
            for ci in range(2):
                fm_proj(load_w(l, 28 + ci), T, ev_g(ci))
            chk(601)
            dl = A32.alloc(4, 4, T)
            for (c0, ncol, st, sc) in segs:
                P.tt("dve", dl[:, :, :, c0:c0 + 1], sc[:].rearrange("p (a b) -> p a b", a=4).unsqueeze(3),
                     raw[:, :, :, c0:c0 + 1], ALU.subtract)
                if ncol > 1:
                    P.tt("dve", dl[:, :, :, c0 + 1:c0 + ncol], raw[:, :, :, c0:c0 + ncol - 1],
                         raw[:, :, :, c0 + 1:c0 + ncol], ALU.subtract)
            chk(602)
            for si, (c0, ncol, st, sc) in enumerate(segs):
                lc = c0 + ncol - 1
                P.copy("pool", sc[:].rearrange("p (a b) -> p a b", a=4).unsqueeze(3), raw[:, :, :, lc:lc + 1])
            chk(603)
            mu = lambda i: pv[:, i * 4:(i + 1) * 4].unsqueeze(2).to_broadcast([128, 4, T])
            tmp = A32.alloc(4, T)
            for dst, kind, mi in ((xr, 0, 0), (xk, 1, 1), (xv, 2, 2), (xw, 3, 3), (xa, 3, 4)):
                P.tt("dve", tmp, dl[:, kind, :, :], mu(mi), ALU.mult)
                P.tt("dve", dst, tmp, raw[:, kind, :, :], ALU.add)
            A32.reset(top32_0)
            A16.reset(markR)
            chk(61)
            sig = A32.alloc(4, T)
            aa = A32.alloc(4, T)
            for (src, w1, w2, dst, b0, use_tanh) in ((xw, lw1, lw2, sig, 20, True), (xa, la1, la2, aa, 24, False)):
                ps = psI[pin[0]]
                pin[0] ^= 1
                for kt in range(4):
                    P.mm(ps[0:64, 0:T], w1[:, kt, :], src[:, kt, :], start=(kt == 0), stop=(kt == 3))
                top16 = A16.top
                hh = A16.alloc(T)
                if use_tanh:
                    P.act(hh[0:64, :], ps[0:64, 0:T], AF.Tanh)
                else:
                    P.copy("dve", hh[0:64, :], ps[0:64, 0:T])
                for ft in range(4):
                    ps2 = psI[pin[0]]
                    pin[0] ^= 1
                    P.mm(ps2[:, 0:T], w2[:, ft * 128:(ft + 1) * 128], hh[0:64, :])
                    P.act(dst[:, ft, :], ps2[:, 0:T], AF.Sigmoid, bias=pv[:, b0 + ft:b0 + ft + 1])
                A16.reset(top16)
            A16.reset(markW)
            chk(62)
            kkn = A16.alloc(4, T)
            kp = A16.alloc(4, T)
            bb = A16.alloc(4, T)
            top32 = A32.top
            kk32 = A32.alloc(4, T)
            sq16 = A16.alloc(4, T)
            for ft in range(4):
                P.ts("dve", kk32[:, ft, :], xk[:, ft, :], pv[:, 28 + ft:29 + ft], ALU.mult)
            P.tt("pool", sq16, kk32, kk32, ALU.mult)
            rn = A32.alloc(4, T)
            for half in range(0, 4, 2):
                psq = psS[0]
                for ft in range(half, half + 2):
                    P.mm(psq[:, (ft - half) * 512:(ft - half) * 512 + T], obb[:], sq16[:, ft, :])
                    P.ts("dve", rn[:, ft, :], psq[:, (ft - half) * 512:(ft - half) * 512 + T], 1e-24, ALU.max)
            P.act(rn, rn, AF.Sqrt)
            P.recip(rn, rn)
            P.tt("dve", kkn, kk32, rn, ALU.mult)
            t1 = A32.alloc(4, T)
            for ft in range(4):
                P.ts("dve", t1[:, ft, :], aa[:, ft, :], pv[:, 32 + ft:33 + ft], ALU.mult, pvx[:, ft:ft + 1], ALU.add)
            P.tt("dve", kp, xk, t1, ALU.mult)
            P.tt("pool", bb, kkn, aa, ALU.mult)
            A32.reset(top32)
            chk(63)
            ld = A32.alloc(4, T)
            cum = A32.alloc(4, T)
            P.ts("dve", ld, sig, -KAPPA, ALU.mult)
            cmk = cview(cst, "cmask64" if C == 64 else "cmask16", w=T)
            for ft in range(4):
                P.scan(cum[:, ft, :], cmk, ld[:, ft, :], 0.0)
            cumc = cum.rearrange("p f (k c) -> p f k c", c=C)
            cend = cumc[:, :, :, C - 1:C]
            E = A32.alloc(4, T)
            rt = A16.alloc(4, T)
            at = A16.alloc(4, T)
            bt = A16.alloc(4, T)
            kt_ = A16.alloc(4, T)
            bh = A16.alloc(4, T)
            kh = A16.alloc(4, T)
            P.act(E, cum, AF.Exp)
            P.tt("dve", rt, xr, E, ALU.mult)
            P.tt("dve", E, cum, ld, ALU.subtract)
            P.act(E, E, AF.Exp)
            P.stt(at, kkn, -1.0, E, ALU.mult, ALU.mult)
            P.act(E, cum, AF.Exp, scale=-1.0)
            P.tt("dve", bt, bb, E, ALU.mult)
            P.tt("pool", kt_, kp, E, ALU.mult)
            Ec = E.rearrange("p f (k c) -> p f k c", c=C)
            P.tt("dve", Ec, cend.to_broadcast([128, 4, NCk, C]), cumc, ALU.subtract)
            P.act(E, E, AF.Exp)
            P.tt("dve", bh, bb, E, ALU.mult)
            P.tt("pool", kh, kp, E, ALU.mult)
            WC = A32.alloc(4, NCk)
            P.act(WC.unsqueeze(3), cend, AF.Exp)
            rkr = A16.alloc(4, T)
            for ft in range(4):
                P.stt(rkr[:, ft, :], xr[:, ft, :], pv[:, 36 + ft:37 + ft], kp[:, ft, :], ALU.mult, ALU.mult)
            Yb = A32.alloc(4, T)
            chk(64)
            mus, mls, mui = musb[C], mlsb[C], muib[C]
            nsteps = {64: 5, 16: 3}[C]
            zmb = zm.unsqueeze(1).unsqueeze(3)
            musB = mus[:].unsqueeze(1).to_broadcast([Z, 4, Z])
            mlsB = mls[:].unsqueeze(1).to_broadcast([Z, 4, Z])
            loop32, loop16 = A32.top, A16.top
            slots = []
            for _i in range(2):
                slots.append(dict(az=A16.alloc(4, 2, C), Tz=A16.alloc(4, Z)[0:Z], Uka=A16.alloc(4, Z)[0:Z],
                                  Ubk=A16.alloc(2, 4, C)[0:Z], vkT=A16.alloc(2, 4, 128), bT=A16.alloc(4, 128)))
            tmpz = [A16.alloc(4, 2, C) for _i in range(5)]
            Az = [A16.alloc(4, Z)[0:Z], A16.alloc(4, Z)[0:Z]]
            ATz = [A16.alloc(4, Z)[0:Z], A16.alloc(4, Z)[0:Z]]
            STb = A16.alloc(4, 128)
            XT = A16.alloc(4, 128)[0:Z]
            SAT = A16.alloc(4, 128)[0:Z]
            pA = psS[0][:, 0:512].rearrange("p (f z) -> p f z", f=4)[0:Z, :, 0:Z]
            pAT = psS[0][:, 512:1024].rearrange("p (f z) -> p f z", f=4)[0:Z, :, 0:Z]
            pU = psS[1][:, 0:512].rearrange("p (f z) -> p f z", f=4)[0:Z, :, 0:Z]
            pU2 = psS[1][:, 512:1024].rearrange("p (a f c) -> p a f c", a=2, f=4)[0:Z, :, :, 0:C]
            pTt = pU
            pB = psB[:].rearrange("p (a f i) -> p a f i", a=2, f=4)

            def zexp_into(dst, src, t0):
                P.tt("dve", dst, src[:, :, t0:t0 + C].unsqueeze(2).to_broadcast([128, 4, 2, C]),
                     zmb.to_broadcast([128, 4, 2, C]), ALU.mult)
                return dst.rearrange("p f a c -> p f (a c)")

            def pre_gen(ck, sl):
                t0 = ck * C
                az = zexp_into(sl["az"], at, t0)
                bz, kz, bhz, khz, vz = [zexp_into(d_, s_, t0) for d_, s_ in zip(tmpz, (bt, kt_, bh, kh, xv))]
                vkT, bT = sl["vkT"], sl["bT"]
                for ft in range(4):
                    P.tr(pB[0:Z, 0, ft, :], vz[:, ft, :], identb[:])
                    P.tr(pB[0:Z, 1, ft, :], khz[:, ft, :], identb[:])
                P.copy("act", vkT[0:Z], pB[0:Z])
                for ft in range(4):
                    P.tr(pB[0:Z, 0, ft, :], bhz[:, ft, :], identb[:])
                P.copy("act", bT[0:Z], pB[0:Z, 0])
                yield
                for ft in range(4):
                    P.mm(pA[:, ft, :], bz[:, ft, :], az[:, ft, :])
                    P.mm(pAT[:, ft, :], az[:, ft, :], bz[:, ft, :])
                    P.mm(pU[:, ft, :], kz[:, ft, :], az[:, ft, :])
                    P.mm(pU2[:, 0, ft, :], bz[:, ft, :], rt[:, ft, t0:t0 + C])
                    P.mm(pU2[:, 1, ft, :], kz[:, ft, :], rt[:, ft, t0:t0 + C])
                Tz, Uka, Ubk = sl["Tz"], sl["Uka"], sl["Ubk"]
                P.tt("dve", Az[0], pA, musB, ALU.mult)
                P.tt("dve", ATz[0], pAT, mlsB, ALU.mult)
                P.tt("dve", Uka, pU, musB, ALU.mult)
                P.tt("dve", Ubk, pU2, mui[:].unsqueeze(1).unsqueeze(1).to_broadcast([Z, 2, 4, C]), ALU.mult)
                P.tt("pool", Tz, Az[0], identb[0:Z, 0:Z].unsqueeze(1).to_broadcast([Z, 4, Z]), ALU.add)
                yield
                cur = 0
                for stp in range(nsteps):
                    last = (stp == nsteps - 1)
                    nxt = cur ^ 1
                    for ft in range(4):
                        if not last:
                            P.mm(pA[:, ft, :], ATz[cur][:, ft, :], Az[cur][:, ft, :])
                        P.mm(pAT[:, ft, :], Az[cur][:, ft, :], ATz[cur][:, ft, :])
                    if not last:
                        P.copy("act", Az[nxt], pA)
                    P.copy("dve", ATz[nxt], pAT)
                    for ft in range(4):
                        P.mm(pTt[:, ft, :], ATz[nxt][:, ft, :], Tz[:, ft, :])
                    P.tt("dve", Tz, Tz, pTt, ALU.add)
                    cur = nxt
                    yield

            def chain_gen(ck, sl, Sfull):
                t0 = ck * C
                az = sl["az"].rearrange("p f a c -> p f (a c)")
                Tz, Uka, Ubk = sl["Tz"], sl["Uka"], sl["Ubk"]
                vT = sl["vkT"][0:Z, 0]
                kT = sl["vkT"][0:Z, 1]
                bT = sl["bT"]
                P.copy("act", STb, Sfull)
                pX = psI[1][:, 0:512].rearrange("p (f i) -> p f i", f=4)[0:Z]
                for ft in range(4):
                    P.mm(pX[:, ft, :], az[:, ft, :], STb[:, ft, :], start=True, stop=False)
                    P.mm(pX[:, ft, :], Uka[:, ft, :], vT[:, ft, :], start=False, stop=True)
                P.copy("act", XT, pX)
                yield
                for ft in range(4):
                    P.mm(pX[:, ft, :], Tz[:, ft, :], XT[:, ft, :])
                P.copy("act", SAT, pX)
                yield
                pY = psI[0][:, 0:4 * C].rearrange("p (f c) -> p f c", f=4)
                for ft in range(4):
                    P.mm(pY[:, ft, :], STb[:, ft, :], rt[:, ft, t0:t0 + C], start=True, stop=False)
                    P.mm(pY[:, ft, :], SAT[:, ft, :], Ubk[:, 0, ft, :], start=False, stop=False)
                    P.mm(pY[:, ft, :], vT[:, ft, :], Ubk[:, 1, ft, :], start=False, stop=True)
                P.copy("dve", Yb[:, :, t0:t0 + C], pY)
                yield
                pS = psA[:].rearrange("p (f i) -> p f i", f=4)
                for ft in range(4):
                    P.mm(pS[:, ft, :], bT[0:Z, ft, :], SAT[:, ft, :], start=True, stop=False)
                    P.mm(pS[:, ft, :], kT[:, ft, :], vT[:, ft, :], start=False, stop=True)
                P.tt("dve", Sfull, Sfull, WC[:, :, ck:ck + 1].to_broadcast([128, 4, 128]), ALU.mult)
                P.tt("dve", Sfull, Sfull, pS, ALU.add)
                yield

            chunks = [(ck, Sfull) for (c0, ncol, Sfull, sc) in segs for ck in range(c0 // C, (c0 + ncol) // C)]
            for _ in pre_gen(chunks[0][0], slots[0]):
                pass
            for j, (ck, Sfull) in enumerate(chunks):
                g1 = pre_gen(chunks[j + 1][0], slots[(j + 1) % 2]) if j + 1 < len(chunks) else iter(())
                g2 = chain_gen(ck, slots[j % 2], Sfull)
                d1 = d2 = False
                while not (d1 and d2):
                    if not d1:
                        d1 = next(g1, "end") == "end"
                    if not d2:
                        d2 = next(g2, "end") == "end"
            A32.reset(loop32)
            A16.reset(loop16)
            chk(66)
            ybf = A16.alloc(4, T)
            P.copy("act", ybf, Yb)
            pM = psS[0][:].rearrange("p (f t) -> p f t", f=4)[:, :, 0:T] if T == 256 else \
                psS[0][:, 0:4 * T].rearrange("p (f t) -> p f t", f=4)
            pV = psS[1][:].rearrange("p (f t) -> p f t", f=4)[:, :, 0:T] if T == 256 else \
                psS[1][:, 0:4 * T].rearrange("p (f t) -> p f t", f=4)
            for ft in range(4):
                P.mm(pM[:, ft, :], obb[:], ybf[:, ft, :])
            ym = A32.alloc(4, T)
            P.stt(ym, pM, -1.0 / 64, Yb, ALU.mult, ALU.add)
            P.act(ybf, ym, AF.Square)
            for ft in range(4):
                P.mm(pV[:, ft, :], obb[:], ybf[:, ft, :])
            sd = A32.alloc(4, T)
            P.ts("dve", sd, pV, 1.0 / 64, ALU.mult, GN_EPS, ALU.add)
            P.act(sd, sd, AF.Sqrt)
            P.recip(sd, sd)
            P.tt("dve", ym, ym, sd, ALU.mult)
            for ft in range(4):
                P.ts("dve", ym[:, ft, :], ym[:, ft, :], pv[:, 40 + ft:41 + ft], ALU.mult, pv[:, 44 + ft:45 + ft], ALU.add)
            for ft in range(4):
                P.mm(pM[:, ft, :], obb[:], rkr[:, ft, :])
            P.tt("dve", sd, pM, xv, ALU.mult)
            P.tt("dve", ym, ym, sd, ALU.add)
            P.tt("dve", mixT[:, 12:16, 0:T], ym, gc, ALU.mult)

        def out_phase(l, T, xsrc, ydst):
            top32 = A32.top
            nsub = (T + 127) // 128
            lg = A32.alloc(2, D)
            P.dma("sp", lg, I["lngb"][l])
            zs = []
            for si in range(nsub):
                n = min(128, T - si * 128)
                z = A32.alloc(D)
                P.dma("sp", z[0:n, :], xsrc[si * 128:si * 128 + n, :])
                zs.append((z, n))
            for c in range(NCH_OUT):
                wb = load_w(l, NCH_IN + c)
                for si, (z, n) in enumerate(zs):
                    ps = psI[pin[0]]
                    pin[0] ^= 1
                    for kt in range(16):
                        P.mm(ps[0:n, 0:CW], mixT[:, kt, si * 128:si * 128 + n], wb[:, kt, :],
                             start=(kt == 0), stop=(kt == 15))
                    P.stt(z[0:n, c * CW:(c + 1) * CW], z[0:n, c * CW:(c + 1) * CW], ALPHA, ps[0:n, 0:CW],
                          ALU.mult, ALU.add)
            for si, (z, n) in enumerate(zs):
                st = A32.alloc(4, 6)
                mv = A32.alloc(2)
                zc = z.rearrange("p (a b) -> p a b", a=4)
                for a in range(4):
                    P.op("dve", (lambda e, o=st[0:n, a, :], i=zc[0:n, a, :]: e.bn_stats(out=o, in_=i)),
                         [zc[0:n, a, :]], [st[0:n, a, :]])
                stf = st.rearrange("p a b -> p (a b)")
                P.op("dve", (lambda e, o=mv[0:n, :], i=stf[0:n, :]: e.bn_aggr(out=o, in_=i)), [stf[0:n, :]], [mv[0:n, :]])
                rs = A32.alloc(1)
                P.ts("dve", rs[0:n, :], mv[0:n, 1:2], LN_EPS, ALU.add)
                P.act(rs[0:n, :], rs[0:n, :], AF.Sqrt)
                P.recip(rs[0:n, :], rs[0:n, :])
                P.ts("dve", z[0:n, :], z[0:n, :], mv[0:n, 0:1], ALU.subtract, rs[0:n, :], ALU.mult)
                P.tt("pool", z[0:n, :], z[0:n, :], lg[0:n, 0, :], ALU.mult)
                P.tt("dve", z[0:n, :], z[0:n, :], lg[0:n, 1, :], ALU.add)
                P.dma("pool", ydst[si * 128:si * 128 + n, :], z[0:n, :])
            A32.reset(top32)

        try:
            chk(1)
            for l in range(nlayers):
                layer_setup(l)
                chk(2)
                last = (l == nlayers - 1)
                if do_prompt:
                    xsrc = I["xp"] if l == 0 else y0p
                    ydst = O["y_p"] if last else y0p
                    P.memset("pool", KT[:], 0.0)
                    P.memset("pool", VT[:], 0.0)
                    P.memset("dve", hcar[:], 0.0)
                    P.memset("dve", Sst[:], 0.0)
                    P.memset("dve", shc[:], 0.0)
                    for ti in range(NT):
                        A32.reset()
                        A16.reset()
                        ts_ = ti * TT
                        load_xT(xsrc[ts_:ts_ + TT, :], TT)
                        chk(101)
                        ssm_phase(l, TT, [(0, TT, hcar)], None)
                        chk(102)
                        A32.reset()
                        A16.reset()
                        units = []
                        for m in range(TT // 128):
                            a0 = 128 * m
                            units.append(dict(nq=128, ncol=640, maskB=True, vblk0=m, tq0=128 * m,
                                              inv=max(0, min(640, 512 - ts_ - a0)),
                                              halves=[(0, 64, 128 * m, a0, 0, 576),
                                                      (64, 64, 128 * m + 64, a0 + 64, 64, 640)]))
                        kv_out = None
                        if ts_ >= seq - 512:
                            r0 = ts_ - (seq - 512)
                            kv_out = (lambda t0, n, r0=r0: O["p_k"][l, r0 + t0:r0 + t0 + n, :],
                                      lambda t0, n, r0=r0: O["p_v"][l, r0 + t0:r0 + t0 + n, :], True)
                            import os as _os
                            if _os.environ.get("DBGKV") == "k":
                                kv_out = (kv_out[0], None, True)
                            if _os.environ.get("DBGKV") == "v":
                                kv_out = (None, kv_out[1], True)
                        else:
                            kv_out = (None, None, True)
                        att_phase(l, TT, units, 512, 512, kv_out)
                        chk(103)
                        P.copy("pool", KT[:, :, 0:256], KT[:, :, 256:512])
                        P.copy("pool", KT[:, :, 256:512], KT[:, :, 512:768])
                        P.copy("pool", VT[:, 0:2, :], VT[:, 2:4, :])
                        P.copy("pool", VT[:, 2:4, :], VT[:, 4:6, :])
                        A32.reset()
                        A16.reset()
                        chk(104)
                        rwkv_phase(l, TT, 64, [(0, TT, Sst[:], shc)], [O["p_shift"][l]] if ti == NT - 1 else None)
                        chk(105)
                        A32.reset()
                        A16.reset()
                        out_phase(l, TT, xsrc[ts_:ts_ + TT, :], ydst[ts_:ts_ + TT, :])
                    P.dma("sp", O["p_ssm"][l], hcar[:].rearrange("p a b -> p (a b)"))
                    P.dma("sp", O["p_rwkv"][l], Sst[:].rearrange("p a b -> p (a b)"))
                if do_sample:
                    T = NS * SQ
                    xsrc = I["xs"] if l == 0 else y0s
                    ydst = O["y_s"] if last else y0s
                    A32.reset()
                    A16.reset()
                    hc = [hcar, hcar1]
                    Ss = [Sst, Sst1]
                    sh = [shc, shc1]
                    for s in range(NS):
                        P.dma("sp", hc[s][:].rearrange("p a b -> p (a b)"), I["hss"][l, s])
                        P.dma("sp", Ss[s][:].rearrange("p a b -> p (a b)"), I["srw"][l, s])
                        P.dma("sp", sh[s][:], I["ssh"][l, s])
                    load_xT(xsrc, T)
                    chk(3)
                    ssm_phase(l, T, [(s * SQ, SQ, hc[s]) for s in range(NS)], None)
                    chk(4)
                    A32.reset()
                    A16.reset()
                    kv_out = (lambda t0, n: O["s_k"][l, t0:t0 + n, :], lambda t0, n: O["s_v"][l, t0:t0 + n, :], False)
                    att_in = att_phase(l, T, None, 640, 0, kv_out)
                    chk(5)
                    for s in range(NS):
                        for blk in range(4):
                            top = A32.top
                            cs_ = A32.alloc(1024)
                            P.dma("sp", cs_, I["ck"][l, s, blk * 128:(blk + 1) * 128, :])
                            for g in range(2):
                                for j in range(4):
                                    kt = g * 4 + j
                                    P.tr(psA[:, j * 128:(j + 1) * 128], cs_[:, kt * 128:(kt + 1) * 128], identf)
                                P.copy(ev_eng(), KT[:, g * 4:(g + 1) * 4, blk * 128:(blk + 1) * 128],
                                       psA[:].rearrange("p (a b) -> p a b", a=4))
                            A32.reset(top)
                        P.dma("pool", VT[:, 0:4, :], I["cv"][l, s].rearrange("(b p) c -> p b c", p=128))
                        P.copy("pool", KT[:, :, 512:512 + SQ], KT[:, :, 640 + s * SQ:640 + (s + 1) * SQ])
                        P.dma("pool", VT[0:SQ, 4, :], O["s_v"][l, s * SQ:(s + 1) * SQ, :])
                        att_in([dict(nq=SQ, ncol=512 + SQ, maskB=False, vblk0=0, tq0=s * SQ, inv=0,
                                     halves=[(0, SQ, s * SQ, 0, 0, 512 + SQ)])])
                    A32.reset()
                    A16.reset()
                    chk(6)
                    rwkv_phase(l, T, 16, [(s * SQ, SQ, Ss[s][:], sh[s]) for s in range(NS)], [O["s_shift"][l, s] for s in range(NS)])
                    chk(7)
                    A32.reset()
                    A16.reset()
                    out_phase(l, T, xsrc, ydst)
                    for s in range(NS):
                        P.dma("sp", O["s_ssm"][l, s], hc[s][:].rearrange("p a b -> p (a b)"))
                        P.dma("sp", O["s_rwkv"][l, s], Ss[s][:].rearrange("p a b -> p (a b)"))
        except _Stop:
            if do_prompt and stage >= 100:
                P.dma("sp", O["p_ssm"][l], hcar[:].rearrange("p a b -> p (a b)"))
                P.dma("sp", O["p_rwkv"][l], Sst[:].rearrange("p a b -> p (a b)"))
            if do_sample and 3 <= stage < 100:
                for s in range(NS):
                    P.dma("sp", O["s_ssm"][l, s], [hcar, hcar1][s][:].rearrange("p a b -> p (a b)"))
                    P.dma("sp", O["s_rwkv"][l, s], [Sst, Sst1][s][:].rearrange("p a b -> p (a b)"))
        P.fence("sp", list(O.values()))
        P.emit()
        stats = dict(P.stats)
        stats["A32"] = A32.hi
        stats["A16"] = A16.hi
    return nc, stats


def _assemble(results):
    f32 = np.float32
    y_p = np.stack([results[c]["y_p"] for c in range(2)]).astype(f32)
    y_s = np.concatenate([results[c]["y_s"].reshape(NS, SQ, D) for c in range(NCORE)]).astype(f32)
    p_k = np.stack([results[c]["p_k"] for c in range(2)], 1).reshape(NL, 2, 512, 16, 64)
    p_v = np.stack([results[c]["p_v"] for c in range(2)], 1).reshape(NL, 2, 512, 16, 64)

    def ssm_unpack(a):
        a = a.reshape(a.shape[:-2] + (2, 64, 16, 2))
        a = np.moveaxis(a, -2, -4)
        a = a.reshape(a.shape[:-4] + (32, 64, 2))
        return np.ascontiguousarray(a[..., 0]), np.ascontiguousarray(a[..., 1])

    def rwkv_unpack(a):
        a = a.reshape(a.shape[:-2] + (2, 64, 4, 2, 64))
        outs = []
        for ft in range(4):
            for h2 in range(2):
                blk = a[..., h2, :, ft, h2, :]
                outs.append(np.swapaxes(blk, -1, -2))
        return np.ascontiguousarray(np.stack(outs, -3))

    def shift_unpack(a):
        return np.ascontiguousarray(a.reshape(a.shape[:-2] + (2048,)))

    pss = np.stack([results[c]["p_ssm"] for c in range(2)], 1)
    p_re, p_im = ssm_unpack(pss)
    p_rw = rwkv_unpack(np.stack([results[c]["p_rwkv"] for c in range(2)], 1))
    p_sh = shift_unpack(np.stack([results[c]["p_shift"] for c in range(2)], 1))
    s_k = np.concatenate([results[c]["s_k"].reshape(NL, NS, SQ, 16, 64) for c in range(NCORE)], 1)
    s_v = np.concatenate([results[c]["s_v"].reshape(NL, NS, SQ, 16, 64) for c in range(NCORE)], 1)
    sss = np.concatenate([results[c]["s_ssm"] for c in range(NCORE)], 1)
    s_re, s_im = ssm_unpack(sss)
    s_rw = rwkv_unpack(np.concatenate([results[c]["s_rwkv"] for c in range(NCORE)], 1))
    s_sh = shift_unpack(np.concatenate([results[c]["s_shift"] for c in range(NCORE)], 1))
    outs = (y_p, y_s, p_k, p_v, p_re, p_im, p_rw, p_sh, s_k, s_v, s_re, s_im, s_rw, s_sh)
    return tuple(np.ascontiguousarray(o, dtype=f32) for o in outs)


def kernel(**inputs):
    shared = _shared_layouts(inputs)
    in_maps = [_core_inputs(inputs, c, shared) for c in range(NCORE)]
    nc, _ = build_program()
    res = run_bass_kernel_spmd(nc, in_maps, core_ids=list(range(NCORE)))
    return _assemble(res.results)
```
